# Optimizing a Trainium2 kernel written in Bass

```python
import math
import jax, jax.numpy as jnp
from jax import lax
import numpy as np

D_MODEL = 1024
BATCH = 8
SEQ = 4096
DEPTH = 4

D_MIX = D_MODEL
D_FF = 2816
NORM_EPS = 1e-6
CONV_W = 4
A_WIDTH = 384
A_HEADS = 6
A_HEAD_DIM = A_WIDTH // A_HEADS
LRU_C = 8.0
B_WIDTH = 384
B_HEADS = 6
B_HEAD_DIM = B_WIDTH // B_HEADS
B_GROUPS = 2
B_STATE = 128
B_CHUNK = 128
B_CONV_DIM = B_WIDTH + 2 * B_GROUPS * B_STATE
B_IN = B_WIDTH + B_CONV_DIM + B_HEADS
C_WIDTH = 256
C_GROUPS = 4
C_GROUP_DIM = C_WIDTH // C_GROUPS
C_CHUNK = 128
IN_COLS = 2 * A_WIDTH + B_IN + 2 * C_WIDTH

kernel_name = "hybrid_lru_ssd_gmlp_macaron"


def rms_norm(x, g):
    xf = x.astype(jnp.float32)
    y = xf * lax.rsqrt(jnp.mean(xf * xf, axis=-1, keepdims=True) + NORM_EPS)
    return (y * g.astype(jnp.float32)).astype(x.dtype)


def layer_norm(x, g, b):
    xf = x.astype(jnp.float32)
    mu = jnp.mean(xf, axis=-1, keepdims=True)
    xc = xf - mu
    y = xc * lax.rsqrt(jnp.mean(xc * xc, axis=-1, keepdims=True) + NORM_EPS)
    return (y * g.astype(jnp.float32) + b.astype(jnp.float32)).astype(x.dtype)


def swiglu(x, w_gu, w_down):
    g, u = jnp.split(x @ w_gu, 2, axis=-1)
    return (jax.nn.silu(g) * u) @ w_down


def causal_dwconv(x, w, b):
    c = x.shape[-1]
    y = lax.conv_general_dilated(
        x, w[:, None, :].astype(x.dtype), window_strides=(1,),
        padding=((CONV_W - 1, 0),), dimension_numbers=("NWC", "WIO", "NWC"),
        feature_group_count=c)
    return y + b


def rg_lru(x, w_r, b_r, w_i, b_i, lam):
    bsz, s, _ = x.shape
    xh = x.reshape(bsz, s, A_HEADS, A_HEAD_DIM)
    r = jax.nn.sigmoid(jnp.einsum("bshi,hij->bshj", xh, w_r).reshape(bsz, s, A_WIDTH) + b_r)
    i = jax.nn.sigmoid(jnp.einsum("bshi,hij->bshj", xh, w_i).reshape(bsz, s, A_WIDTH) + b_i)
    log_a = -LRU_C * r.astype(jnp.float32) * jax.nn.softplus(-lam.astype(jnp.float32))
    a = jnp.exp(log_a)
    mult = jnp.sqrt(-jnp.expm1(2.0 * log_a))
    u = mult * (i * x).astype(jnp.float32)

    def combine(left, right):
        a1, b1 = left
        a2, b2 = right
        return a1 * a2, a2 * b1 + b2

    _, h = lax.associative_scan(combine, (a, u), axis=1)
    return h.astype(x.dtype)


def segsum(a):
    t = a.shape[-1]
    cs = jnp.cumsum(a, axis=-1)
    diff = cs[..., :, None] - cs[..., None, :]
    mask = jnp.tril(jnp.ones((t, t), dtype=bool))
    return jnp.where(mask, diff, -jnp.inf)


def ssd_mixer(zxbcdt, conv_w, conv_b, dt_bias, a_log, d_skip, norm_g):
    bsz, s, _ = zxbcdt.shape
    f32 = jnp.float32
    nc = s // B_CHUNK
    hpg = B_HEADS // B_GROUPS
    z, xbc, dt = jnp.split(zxbcdt, [B_WIDTH, B_WIDTH + B_CONV_DIM], axis=-1)
    xbc = jax.nn.silu(causal_dwconv(xbc, conv_w, conv_b))
    xs, bm, cm = jnp.split(xbc, [B_WIDTH, B_WIDTH + B_GROUPS * B_STATE], axis=-1)
    dt = jax.nn.softplus(dt.astype(f32) + dt_bias.astype(f32))
    a = -jnp.exp(a_log.astype(f32))
    x_c = xs.astype(f32).reshape(bsz, nc, B_CHUNK, B_GROUPS, hpg, B_HEAD_DIM)
    dt_c = dt.reshape(bsz, nc, B_CHUNK, B_GROUPS, hpg)
    xdt = x_c * dt_c[..., None]
    ad = jnp.moveaxis((dt * a).reshape(bsz, nc, B_CHUNK, B_GROUPS, hpg), 2, -1)
    b_c = bm.astype(f32).reshape(bsz, nc, B_CHUNK, B_GROUPS, B_STATE)
    c_c = cm.astype(f32).reshape(bsz, nc, B_CHUNK, B_GROUPS, B_STATE)
    a_cs = jnp.cumsum(ad, axis=-1)
    lmat = jnp.exp(segsum(ad))
    cb = jnp.einsum("bclgn,bcsgn->bcgls", c_c, b_c)
    y_diag = jnp.einsum("bcgjls,bcsgjp->bclgjp", cb[:, :, :, None] * lmat, xdt)
    decay_states = jnp.exp(a_cs[..., -1:] - a_cs)
    chunk_states = jnp.einsum("bclgn,bcgjl,bclgjp->bcgjpn", b_c, decay_states, xdt)
    chunk_decay = jnp.exp(a_cs[..., -1])

    def step(h, inp):
        st, dc = inp
        return dc[..., None, None] * h + st, h

    h0 = jnp.zeros((bsz, B_GROUPS, hpg, B_HEAD_DIM, B_STATE), f32)
    _, prev = lax.scan(step, h0, (jnp.moveaxis(chunk_states, 1, 0), jnp.moveaxis(chunk_decay, 1, 0)))
    prev = jnp.moveaxis(prev, 0, 1)
    y_off = jnp.einsum("bclgn,bcgjpn,bcgjl->bclgjp", c_c, prev, jnp.exp(a_cs))
    y = y_diag + y_off + x_c * d_skip.astype(f32).reshape(B_GROUPS, hpg)[:, :, None]
    y = y.reshape(bsz, s, B_WIDTH)
    y = rms_norm(y * jax.nn.silu(z.astype(f32)), norm_g)
    return y.astype(zxbcdt.dtype)


def chunk_sgu(uv, ln_g, ln_b, w_s, b_s):
    bsz, s, _ = uv.shape
    nc = s // C_CHUNK
    u, v = jnp.split(jax.nn.gelu(uv), 2, axis=-1)
    v = layer_norm(v, ln_g, ln_b)
    vc = v.reshape(bsz, nc, C_CHUNK, C_GROUPS, C_GROUP_DIM)
    mask = jnp.tril(jnp.ones((C_CHUNK, C_CHUNK), dtype=bool))
    w = jnp.where(mask, w_s, jnp.zeros_like(w_s))
    mixed = jnp.einsum("gts,bcsgd->bctgd", w, vc) + jnp.swapaxes(b_s, 0, 1)[:, :, None]
    return u * mixed.reshape(bsz, s, C_WIDTH)


def hybrid_mixer(h, w_in, w_out, lru_conv_w, lru_conv_b, lru_w_r, lru_b_r, lru_w_i, lru_b_i, lru_lambda,
                 ssd_conv_w, ssd_conv_b, ssd_dt_bias, ssd_a_log, ssd_d, ssd_norm_g,
                 sgu_ln_g, sgu_ln_b, sgu_w_s, sgu_b_s):
    proj = h @ w_in
    pa, pb, pc = jnp.split(proj, [2 * A_WIDTH, 2 * A_WIDTH + B_IN], axis=-1)
    gate_a, rec_a = jnp.split(pa, 2, axis=-1)
    rec = causal_dwconv(rec_a, lru_conv_w, lru_conv_b)
    ya = rg_lru(rec, lru_w_r, lru_b_r, lru_w_i, lru_b_i, lru_lambda) * jax.nn.gelu(gate_a)
    yb = ssd_mixer(pb, ssd_conv_w, ssd_conv_b, ssd_dt_bias, ssd_a_log, ssd_d, ssd_norm_g)
    yc = chunk_sgu(pc, sgu_ln_g, sgu_ln_b, sgu_w_s, sgu_b_s)
    return jnp.concatenate([ya, yb, yc], axis=-1) @ w_out


def setup_inputs(seed: int = 0) -> dict:
    key = jax.random.key(seed)
    ks = iter(jax.random.split(key, 40))
    f32 = jnp.float32
    L = DEPTH

    def nrm(shape, scale):
        return jax.random.normal(next(ks), shape, f32) * scale

    def gain(shape):
        return 1.0 + 0.05 * jax.random.normal(next(ks), shape, f32)

    x = jax.random.normal(next(ks), (BATCH, SEQ, D_MODEL), f32)
    ffn1_pre_g = gain((L, D_MODEL))
    ffn1_post_g = gain((L, D_MODEL))
    ffn1_w_gu = nrm((L, D_MODEL, 2 * D_FF), D_MODEL ** -0.5)
    ffn1_w_down = nrm((L, D_FF, D_MODEL), D_FF ** -0.5)
    mix_pre_g = gain((L, D_MODEL))
    mix_post_g = gain((L, D_MODEL))
    mix_w_in = nrm((L, D_MODEL, IN_COLS), D_MODEL ** -0.5)
    mix_w_out = nrm((L, D_MIX, D_MODEL), D_MIX ** -0.5)
    lru_conv_w = nrm((L, CONV_W, A_WIDTH), CONV_W ** -0.5)
    lru_conv_b = nrm((L, A_WIDTH), 0.02)
    lru_w_r = nrm((L, A_HEADS, A_HEAD_DIM, A_HEAD_DIM), A_HEAD_DIM ** -0.5)
    lru_b_r = nrm((L, A_WIDTH), 0.02)
    lru_w_i = nrm((L, A_HEADS, A_HEAD_DIM, A_HEAD_DIM), A_HEAD_DIM ** -0.5)
    lru_b_i = nrm((L, A_WIDTH), 0.02)
    a_c = jax.random.uniform(next(ks), (L, A_WIDTH), f32, 0.9, 0.999)
    a0 = a_c ** (1.0 / LRU_C)
    lru_lambda = jnp.log(a0) - jnp.log1p(-a0)
    ssd_conv_w = nrm((L, CONV_W, B_CONV_DIM), CONV_W ** -0.5)
    ssd_conv_b = nrm((L, B_CONV_DIM), 0.02)
    dt0 = jnp.exp(jax.random.uniform(next(ks), (L, B_HEADS), f32, math.log(1e-3), math.log(1e-1)))
    ssd_dt_bias = dt0 + jnp.log(-jnp.expm1(-dt0))
    ssd_a_log = jnp.log(jax.random.uniform(next(ks), (L, B_HEADS), f32, 1.0, 16.0))
    ssd_d = 1.0 + 0.1 * jax.random.normal(next(ks), (L, B_HEADS), f32)
    ssd_norm_g = gain((L, B_WIDTH))
    sgu_ln_g = gain((L, C_WIDTH))
    sgu_ln_b = nrm((L, C_WIDTH), 0.02)
    sgu_w_s = nrm((L, C_GROUPS, C_CHUNK, C_CHUNK), C_CHUNK ** -0.5)
    sgu_b_s = 1.0 + 0.1 * jax.random.normal(next(ks), (L, C_GROUPS, C_CHUNK), f32)
    ffn2_pre_g = gain((L, D_MODEL))
    ffn2_post_g = gain((L, D_MODEL))
    ffn2_w_gu = nrm((L, D_MODEL, 2 * D_FF), D_MODEL ** -0.5)
    ffn2_w_down = nrm((L, D_FF, D_MODEL), D_FF ** -0.5)
    return {
        "x": x,
        "ffn1_pre_g": ffn1_pre_g, "ffn1_post_g": ffn1_post_g,
        "ffn1_w_gu": ffn1_w_gu, "ffn1_w_down": ffn1_w_down,
        "mix_pre_g": mix_pre_g, "mix_post_g": mix_post_g,
        "mix_w_in": mix_w_in, "mix_w_out": mix_w_out,
        "lru_conv_w": lru_conv_w, "lru_conv_b": lru_conv_b,
        "lru_w_r": lru_w_r, "lru_b_r": lru_b_r,
        "lru_w_i": lru_w_i, "lru_b_i": lru_b_i, "lru_lambda": lru_lambda,
        "ssd_conv_w": ssd_conv_w, "ssd_conv_b": ssd_conv_b,
        "ssd_dt_bias": ssd_dt_bias, "ssd_a_log": ssd_a_log,
        "ssd_d": ssd_d, "ssd_norm_g": ssd_norm_g,
        "sgu_ln_g": sgu_ln_g, "sgu_ln_b": sgu_ln_b,
        "sgu_w_s": sgu_w_s, "sgu_b_s": sgu_b_s,
        "ffn2_pre_g": ffn2_pre_g, "ffn2_post_g": ffn2_post_g,
        "ffn2_w_gu": ffn2_w_gu, "ffn2_w_down": ffn2_w_down,
    }


def reference(x, ffn1_pre_g, ffn1_post_g, ffn1_w_gu, ffn1_w_down,
              mix_pre_g, mix_post_g, mix_w_in, mix_w_out,
              lru_conv_w, lru_conv_b, lru_w_r, lru_b_r, lru_w_i, lru_b_i, lru_lambda,
              ssd_conv_w, ssd_conv_b, ssd_dt_bias, ssd_a_log, ssd_d, ssd_norm_g,
              sgu_ln_g, sgu_ln_b, sgu_w_s, sgu_b_s,
              ffn2_pre_g, ffn2_post_g, ffn2_w_gu, ffn2_w_down):
    for l in range(DEPTH):
        f = swiglu(rms_norm(x, ffn1_pre_g[l]), ffn1_w_gu[l], ffn1_w_down[l])
        x = x + 0.5 * rms_norm(f, ffn1_post_g[l])
        m = hybrid_mixer(rms_norm(x, mix_pre_g[l]), mix_w_in[l], mix_w_out[l],
                         lru_conv_w[l], lru_conv_b[l], lru_w_r[l], lru_b_r[l],
                         lru_w_i[l], lru_b_i[l], lru_lambda[l],
                         ssd_conv_w[l], ssd_conv_b[l], ssd_dt_bias[l], ssd_a_log[l],
                         ssd_d[l], ssd_norm_g[l],
                         sgu_ln_g[l], sgu_ln_b[l], sgu_w_s[l], sgu_b_s[l])
        x = x + rms_norm(m, mix_post_g[l])
        f = swiglu(rms_norm(x, ffn2_pre_g[l]), ffn2_w_gu[l], ffn2_w_down[l])
        x = x + 0.5 * rms_norm(f, ffn2_post_g[l])
    return x
```

```python
import numpy as np
from contextlib import ExitStack
import concourse.bass as bass
import concourse.mybir as mybir
from concourse.bass_utils import run_bass_kernel_spmd

F32 = mybir.dt.float32
BF16 = mybir.dt.bfloat16
AF = mybir.ActivationFunctionType
ALU = mybir.AluOpType

D = 1024
DFF = 2816
NFC = 22
INC = 2566
T = 512
EPS = 1e-6
SLOT = 4160
NSLOT = 4
import os
TOG_OUT = os.environ.get('TOG_OUT', '1') == '1'
TOG_SGU = os.environ.get('TOG_SGU', '1') == '1'


class Op:
    __slots__ = ("eng", "fn", "is_dma", "key", "waits_dma", "deps", "sig", "idx", "tag")


class Prog:
    ENGS = ("pe", "act", "dve", "pool", "sp")

    def __init__(self, nc):
        self.nc = nc
        self.ops = []
        self.last_w = {}
        self.readers = {}
        self.dma_count = {}
        self.dma_keys = []
        self.tag = ""

    def add(self, eng, fn, reads=(), writes=(), dma_key=None):
        op = Op()
        op.eng = eng
        op.fn = fn
        op.is_dma = dma_key is not None
        op.key = dma_key
        op.idx = len(self.ops)
        op.sig = 0
        op.tag = self.tag
        deps = {}
        wd = {}

        def dep(d, same_ok):
            if d is None or d is op:
                return
            if d.is_dma:
                wd[d.key] = self.dma_count[d.key]
            else:
                if d.eng == eng and same_ok and not op.is_dma and eng == "pe":
                    return
                deps[d.idx] = d

        for t in reads:
            dep(self.last_w.get(t), False)
        for t in writes:
            dep(self.last_w.get(t), True)
            for r in self.readers.get(t, ()):
                dep(r, True)
        op.deps = list(deps.values())
        op.waits_dma = wd
        for d in op.deps:
            d.sig = 1
        for t in reads:
            self.readers.setdefault(t, []).append(op)
        for t in writes:
            self.last_w[t] = op
            self.readers[t] = []
        if op.is_dma:
            if dma_key not in self.dma_count:
                self.dma_count[dma_key] = 0
                self.dma_keys.append(dma_key)
            self.dma_count[dma_key] += 16
        self.ops.append(op)
        return op

    def emit(self, final_dma_keys=()):
        nc = self.nc
        with ExitStack() as es:
            esem = {e: es.enter_context(nc.semaphore("s_" + e)) for e in self.ENGS}
            dsem = {k: es.enter_context(nc.semaphore("d_%d" % i)) for i, k in enumerate(self.dma_keys)}
            cnt = {e: 0 for e in self.ENGS}
            for op in self.ops:
                if (not op.is_dma) and op.sig:
                    cnt[op.eng] += 1
                    op.sig = cnt[op.eng]
            per = {e: [o for o in self.ops if o.eng == e] for e in self.ENGS}
            stats = {e: [len(per[e]), 0] for e in self.ENGS}
            block = es.enter_context(nc.Block())

            def run(engname, engobj):
                waited = {}
                for op in per[engname]:
                    for d in op.deps:
                        k = ("e", d.eng)
                        if waited.get(k, 0) < d.sig:
                            engobj.wait_ge(esem[d.eng], d.sig)
                            waited[k] = d.sig
                            stats[engname][1] += 1
                    for key, val in op.waits_dma.items():
                        k = ("d", key)
                        if waited.get(k, 0) < val:
                            engobj.wait_ge(dsem[key], val)
                            waited[k] = val
                            stats[engname][1] += 1
                    ins = op.fn(engobj)
                    if op.is_dma:
                        ins.then_inc(dsem[op.key], 16)
                    elif op.sig:
                        ins.then_inc(esem[engname], 1)
                if engname == "sp":
                    for key in final_dma_keys:
                        engobj.wait_ge(dsem[key], self.dma_count[key])

            @block.tensor
            def _(e):
                run("pe", e)

            @block.scalar
            def _(e):
                run("act", e)

            @block.vector
            def _(e):
                run("dve", e)

            @block.gpsimd
            def _(e):
                run("pool", e)

            @block.sync
            def _(e):
                run("sp", e)
        return stats


PV_SPEC = [
    ("ffn1_pre_g", 8), ("ffn1_post_g", 8), ("mix_pre_g", 8), ("mix_post_g", 8),
    ("ffn2_pre_g", 8), ("ffn2_post_g", 8),
    ("lru_cw0", 3), ("lru_cw1", 3), ("lru_cw2", 3), ("lru_cw3", 3), ("lru_conv_b", 3),
    ("lru_b_r", 3), ("lru_b_i", 3), ("lru_lambda", 3),
    ("ssd_cw0", 7), ("ssd_cw1", 7), ("ssd_cw2", 7), ("ssd_cw3", 7), ("ssd_conv_b", 7),
    ("ssd_norm_g", 3), ("ssd_dvec", 3), ("sgu_ln_g", 2), ("sgu_ln_b", 2),
]
PV_PER_LAYER = sum(n for _, n in PV_SPEC)


def pv_off(l, name):
    o = l * PV_PER_LAYER
    for nm, n in PV_SPEC:
        if nm == name:
            return o
        o += n
    raise KeyError(name)


def _colmajor(v):
    v = np.asarray(v, np.float32)
    return np.ascontiguousarray(v.reshape(-1, 128).T)


def prep_params(inp, L):
    pv = np.zeros((128, L * PV_PER_LAYER), np.float32)
    for l in range(L):
        vecs = {
            "ffn1_pre_g": inp["ffn1_pre_g"][l], "ffn1_post_g": inp["ffn1_post_g"][l],
            "mix_pre_g": inp["mix_pre_g"][l], "mix_post_g": inp["mix_post_g"][l],
            "ffn2_pre_g": inp["ffn2_pre_g"][l], "ffn2_post_g": inp["ffn2_post_g"][l],
            "lru_conv_b": inp["lru_conv_b"][l], "lru_b_r": inp["lru_b_r"][l],
            "lru_b_i": inp["lru_b_i"][l], "lru_lambda": inp["lru_lambda"][l],
            "ssd_conv_b": inp["ssd_conv_b"][l], "ssd_norm_g": inp["ssd_norm_g"][l],
            "ssd_dvec": np.repeat(np.asarray(inp["ssd_d"][l]), 64),
            "sgu_ln_g": inp["sgu_ln_g"][l], "sgu_ln_b": inp["sgu_ln_b"][l],
        }
        for k in range(4):
            vecs["lru_cw%d" % k] = inp["lru_conv_w"][l][k]
            vecs["ssd_cw%d" % k] = inp["ssd_conv_w"][l][k]
        for nm, n in PV_SPEC:
            o = pv_off(l, nm)
            pv[:, o:o + n] = _colmajor(vecs[nm])
    hb = np.zeros((128, L * 48), np.float32)
    for l in range(L):
        hb[:, l * 48:l * 48 + 24] = np.tile(np.asarray(inp["ssd_dt_bias"][l], np.float32), 4)[None, :]
        hb[:, l * 48 + 24:l * 48 + 48] = np.tile(np.asarray(inp["ssd_a_log"][l], np.float32), 4)[None, :]
    wbd = np.zeros((128, L * 6 * 128), np.float32)
    for l in range(L):
        for gi, nm in enumerate(("lru_w_r", "lru_w_i")):
            w = np.asarray(inp[nm][l], np.float32)
            for c in range(3):
                o = ((l * 2 + gi) * 3 + c) * 128
                for hh in range(2):
                    wbd[hh * 64:(hh + 1) * 64, o + hh * 64:o + (hh + 1) * 64] = w[2 * c + hh]
    ws = np.asarray(inp["sgu_w_s"], np.float32)[:L]
    wst = np.ascontiguousarray(ws.transpose(3, 0, 1, 2).reshape(128, L * 4 * 128))
    bs = np.ascontiguousarray(np.asarray(inp["sgu_b_s"], np.float32)[:L].reshape(1, L * 4 * 128))
    k = np.arange(128)
    ident = np.eye(128, dtype=np.float32)
    U = (k[:, None] <= k[None, :]).astype(np.float32)
    Ls = (k[:, None] > k[None, :]).astype(np.float32)
    cst = np.ascontiguousarray(np.concatenate([ident, U, Ls], axis=1))
    return {"pv": pv, "hb": hb, "wbd": wbd, "wst": wst, "bs": bs, "cst": cst}


IN_CHUNKS = ([("gate", c, c * 128, 128) for c in range(3)] +
             [("rec", c, 384 + c * 128, 128) for c in range(3)] +
             [("z", c, 768 + c * 128, 128) for c in range(3)] +
             [("xbc", c, 1152 + c * 128, 128) for c in range(7)] +
             [("dt", 0, 2048, 6)] +
             [("u", c, 2054 + c * 128, 128) for c in range(2)] +
             [("v", c, 2310 + c * 128, 128) for c in range(2)])
IN_PIECES = [(0, 512), (512, 1024), (1024, 1536), (1536, 2048), (2048, 2566)]


def build(n_tiles, L):
    NTOK = n_tiles * T
    nc = bass.Bass("TRN2", target_bir_lowering=False)

    def din(name, shape, dt=F32):
        return nc.dram_tensor(name, shape, dt, kind="ExternalInput").ap()

    x_d = din("x", [NTOK, D])
    wgu_d = [din("ffn1_w_gu", [L, D, 2 * DFF]), din("ffn2_w_gu", [L, D, 2 * DFF])]
    wdn_d = [din("ffn1_w_down", [L, DFF, D]), din("ffn2_w_down", [L, DFF, D])]
    win_d = din("mix_w_in", [L, D, INC])
    wout_d = din("mix_w_out", [L, D, D])
    pv_d = din("pv", [128, L * PV_PER_LAYER])
    hb_d = din("hb", [128, L * 48])
    wbd_d = din("wbd", [128, L * 6 * 128])
    wst_d = din("wst", [128, L * 4 * 128])
    bs_d = din("bs", [1, L * 4 * 128])
    cst_d = din("cst", [128, 3 * 128])
    out_d = nc.dram_tensor("out", [NTOK, D], F32, kind="ExternalOutput").ap()
    dbg_d = nc.dram_tensor("dbg", [128, 64], F32, kind="ExternalOutput").ap() if os.environ.get("KDBG") else None

    PPL = 45
    wsc = nc.dram_tensor("wsc", [L * PPL, 128, SLOT], BF16).ap()

    def pid(l, sub, i):
        base = {"gu0": 0, "dn0": 11, "in": 19, "out": 24, "gu1": 26, "dn1": 37}[sub]
        return l * PPL + base + i

    P = Prog(nc)
    es = ExitStack()
    with es:
        def sb(name, n, dt=F32, parts=128):
            return es.enter_context(nc.sbuf_tensor(name, [parts, n], dt))

        def psum(name, n, dt=F32):
            return es.enter_context(nc.psum_tensor(name, [128, n], dt))

        xT = sb("xT", 8 * T)
        xn = sb("xn", 8 * T, BF16)
        fT = sb("fT", 8 * T)
        hT = sb("hT", NFC * T, BF16)
        wsl = [sb("wsl%d" % i, SLOT, BF16) for i in range(NSLOT)]
        sq = [sb("sq%d" % i, T, BF16) for i in range(2)]
        rstd = sb("rstd", T)
        sg = [sb("sg%d" % i, T) for i in range(2)]
        pv = sb("pv_s", L * PV_PER_LAYER)
        hb = sb("hb_s", L * 48)
        wbd = sb("wbd_s", L * 6 * 128, BF16)
        wst32 = sb("wst32", 4 * 128)
        wst = sb("wst_s", L * 4 * 128, BF16)
        bs = sb("bs_s", L * 4 * 128, BF16, parts=1)
        cst = sb("cst_s", 3 * 128)
        identb = sb("identb", 128, BF16)
        onesb = sb("onesb", 128, BF16)
        onesf = sb("onesf", 128)
        halo = sb("halo", L * 10 * 3)
        hstate = sb("hstate", L * 3)
        H32 = sb("H32", L * 384)
        gate_g = sb("gate_g", 3 * T)
        rec32 = sb("rec32", 3 * T)
        zs = sb("zs", 3 * T)
        raw = [sb("raw%d" % i, T + 3) for i in range(3)]
        cacc = [sb("cacc%d" % i, T) for i in range(3)]
        u_g = sb("u_g", 2 * T)
        v_g = sb("v_g", 2 * T)
        tmp = [sb("tmp%d" % i, T) for i in range(5)]
        rhsU = [sb("rhsU%d" % i, 384) for i in range(2)]
        Eb = [sb("E%d" % i, 384) for i in range(2)]
        E2b = [sb("E2%d" % i, 384) for i in range(2)]
        dt_a = sb("dt_a", 24)
        ad_t = sb("ad_t", 24)
        dtw = sb("dtw", 6)
        r2tok = sb("r2tok", 4)
        MTall = sb("MTall", 8 * 384, BF16)
        CsTall = sb("CsTall", 8 * 384, BF16)
        wcol = sb("wcol", 24)
        neg3 = sb("neg3", 384)
        dtw2 = sb("dtw2", 6)
        dcol = sb("dcol", 24)
        dummy = sb("dummy_act", 2)
        yT = sb("yT", 3 * T)
        yT2 = yT
        ident = cst[:, 0:128]
        Umask = cst[:, 128:256]
        Lstr = cst[:, 256:384]

        def hch(j, n=T, o=0):
            return hT[:, j * T + o:j * T + o + n]
        MT = [hT[:, 12 * T + g * 384:12 * T + (g + 1) * 384] for g in range(2)]
        CsT = [hT[:, 14 * T + g * 384:14 * T + (g + 1) * 384] for g in range(2)]
        xdt = hT[:, 16 * T:16 * T + 384]
        xdtw = hT[:, 17 * T:17 * T + 384]
        Btok = hT[:, 18 * T:18 * T + 256]
        Hbf = [hT[:, 19 * T:19 * T + 384], hT[:, 20 * T:20 * T + 384]]
        vtok = [hT[:, 21 * T:21 * T + 256], hT[:, 18 * T + 256:18 * T + 512]]
        tvtok = ["h21", "h18"]
        tMT = [["h12"], ["h12", "h13"]]
        tCsT = [["h14"], ["h14", "h15"]]

        NPB = 5
        pbank = [psum("pb%d" % i, 512) for i in range(NPB)]
        ps_st = psum("ps_st", 512)
        pbf = [psum("pbf%d" % i, 1024, BF16) for i in range(2)]
        st = {"pb": 0, "pbf": 0, "ws": 0, "sq": 0, "sg": 0, "raw": 0, "cacc": 0, "pool": "pool", "first": True}

        def nb():
            i = st["pb"] % NPB
            st["pb"] += 1
            return pbank[i], "pb%d" % i

        def nbf():
            i = st["pbf"] % 2
            st["pbf"] += 1
            return pbf[i], "pbf%d" % i

        def mm(out, lhsT, rhs, start, stop, reads, writes):
            P.add("pe", lambda e: e.matmul(out, lhsT=lhsT, rhs=rhs, start=start, stop=stop), reads, writes)

        def tr(out, in_, idn, reads, writes):
            P.add("pe", lambda e: e.transpose(out, in_, idn), reads, writes)

        def act(out, in_, func, reads, writes, bias=None, scale=None):
            kw = {}
            if bias is not None:
                kw["bias"] = bias
            if scale is not None:
                kw["scale"] = scale
            P.add("act", lambda e: e.activation(out=out, in_=in_, func=func, **kw), reads, writes)

        def tt(eng, out, in0, in1, op, reads, writes):
            P.add(eng, lambda e: e.tensor_tensor(out=out, in0=in0, in1=in1, op=op), reads, writes)

        def stt(out, in0, scalar, in1, op0, op1, reads, writes):
            P.add("dve", lambda e: e.scalar_tensor_tensor(out=out, in0=in0, scalar=scalar, in1=in1, op0=op0, op1=op1), reads, writes)

        def ts(out, in0, s1, s2, op0, op1, reads, writes):
            if s2 is None:
                P.add("dve", lambda e: e.tensor_scalar(out=out, in0=in0, scalar1=s1, scalar2=None, op0=op0), reads, writes)
            else:
                P.add("dve", lambda e: e.tensor_scalar(out=out, in0=in0, scalar1=s1, scalar2=s2, op0=op0, op1=op1), reads, writes)

        def cp(eng, out, in_, reads, writes):
            if eng == "act":
                act(out, in_, AF.Copy, reads, writes)
            else:
                P.add(eng, lambda e: e.tensor_copy(out=out, in_=in_), reads, writes)

        def xc(buf, c, n=T):
            return buf[:, c * n:(c + 1) * n]

        def pvc(l, name, c):
            o = pv_off(l, name) + c
            return pv[:, o:o + 1]

        P.add("sp", lambda e: e.dma_start(out=cst[:], in_=cst_d), writes=["cst"], dma_key="c_cst")
        P.add("sp", lambda e: e.dma_start(out=pv[:], in_=pv_d), writes=["pv"], dma_key="c_pv")
        P.add("sp", lambda e: e.dma_start(out=hb[:], in_=hb_d), writes=["hb"], dma_key="c_hb")
        P.add("pool", lambda e: e.dma_start(out=wbd[:], in_=wbd_d), writes=["wbd"], dma_key="c_wbd")
        P.add("pool", lambda e: e.dma_start(out=bs[:], in_=bs_d), writes=["bs"], dma_key="c_bs")
        P.add("pool", lambda e: e.dma_start(out=identb[:], in_=cst_d[:, 0:128]), writes=["identb"], dma_key="c_idb")
        P.add("dve", lambda e: e.memset(onesb[:], 1.0), writes=["onesb"])
        P.add("dve", lambda e: e.memset(onesf[:], 1.0), writes=["onesf"])
        P.add("dve", lambda e: e.memset(dummy[:], 1.0), writes=["dummy0", "dummy1"])
        P.add("dve", lambda e: e.memset(halo[:], 0.0), writes=["halo%d" % ci for ci in range(10)])
        P.add("dve", lambda e: e.memset(hstate[:], 0.0), writes=["hst0", "hst1", "hst2"])
        P.add("dve", lambda e: e.memset(H32[:], 0.0), writes=["H32"])
        for j in range(3):
            ts(neg3[:, j * 128:(j + 1) * 128], Umask, -1.0, 30000.0, ALU.add, ALU.mult, ["cst"], ["neg3"])
        for l in range(L):
            for nm in ("ffn1_post_g", "ffn2_post_g"):
                o = pv_off(l, nm)
                ts(pv[:, o:o + 8], pv[:, o:o + 8], 0.5, None, ALU.mult, None, ["pv"], ["pv"])
            o = pv_off(l, "lru_lambda")
            act(pv[:, o:o + 3], pv[:, o:o + 3], AF.Exp, ["pv"], ["pv"], scale=-1.0)
            act(pv[:, o:o + 3], pv[:, o:o + 3], AF.Ln, ["pv"], ["pv"], bias=1.0)
            ts(pv[:, o:o + 3], pv[:, o:o + 3], -8.0, None, ALU.mult, None, ["pv"], ["pv"])
            o = l * 48 + 24
            act(hb[:, o:o + 24], hb[:, o:o + 24], AF.Exp, ["hb"], ["hb"])
            ts(hb[:, o:o + 24], hb[:, o:o + 24], -1.0, None, ALU.mult, None, ["hb"], ["hb"])
            for g in range(4):
                o = (l * 4 + g) * 128
                P.add("sp", lambda e, o=o, g=g: e.dma_start(out=wst32[:, g * 128:(g + 1) * 128], in_=wst_d[:, o:o + 128]),
                      writes=["wst32_%d" % g], dma_key="c_wst%d" % g)
                tt("dve", wst[:, o:o + 128], wst32[:, g * 128:(g + 1) * 128], Umask, ALU.mult,
                   ["wst32_%d" % g, "cst"], ["wst"])

        def piece_parts(p_):
            l, r_ = divmod(p_, PPL)
            if r_ < 11 or 26 <= r_ < 37:
                f, jp = (0, r_) if r_ < 11 else (1, r_ - 26)
                wg = wgu_d[f][l].rearrange("(k p) c -> p k c", p=128)
                return [(s * 2048, 8, 256, wg[:, :, s * DFF + jp * 256:s * DFF + (jp + 1) * 256]) for s in range(2)]
            if 11 <= r_ < 19 or r_ >= 37:
                f, dp = (0, r_ - 11) if r_ < 19 else (1, r_ - 37)
                wd = wdn_d[f][l].rearrange("(k p) c -> p k c", p=128)
                return [(0, NFC, 128, wd[:, :, dp * 128:(dp + 1) * 128])]
            if 19 <= r_ < 24:
                c0, c1 = IN_PIECES[r_ - 19]
                wi = win_d[l].rearrange("(k p) c -> p k c", p=128)
                return [(0, 8, c1 - c0, wi[:, :, c0:c1])]
            op_ = r_ - 24
            wo = wout_d[l].rearrange("(k p) c -> p k c", p=128)
            return [(0, 8, 512, wo[:, :, op_ * 512:(op_ + 1) * 512])]

        def wload(p_, n):
            i = st["ws"] % NSLOT
            st["ws"] += 1
            if st["first"]:
                for (off, k, c, src) in piece_parts(p_):
                    dstv = wsl[i][:, off:off + k * c].rearrange("p (k c) -> p k c", k=k)
                    P.add("pool", lambda e, dstv=dstv, src=src: e.dma_start(out=dstv, in_=src),
                          writes=["wsl%d" % i], dma_key="wp%d" % i)
                P.add("sp", lambda e: e.dma_start(out=wsc[p_][:, 0:n], in_=wsl[i][:, 0:n]),
                      reads=["wsl%d" % i], writes=["wsc%d" % p_], dma_key="wb%d" % i)
            else:
                P.add("sp", lambda e: e.dma_start(out=wsl[i][:, 0:n], in_=wsc[p_][:, 0:n]),
                      reads=["wsc%d" % p_], writes=["wsl%d" % i], dma_key="w%d" % i)
            return wsl[i], "wsl%d" % i

        def stats_of(src_ap, src_tok, c, n_c):
            i = st["sq"] % 2
            st["sq"] += 1
            act(sq[i][:], src_ap, AF.Square, [src_tok], ["sq%d" % i])
            return lambda: mm(ps_st[:], onesb[:], sq[i][:], c == 0, c == n_c - 1, ["onesb", "sq%d" % i], ["ps_st"])

        def preload(func):
            kw = {"bias": 1.0} if func == AF.Ln else {}
            P.add("act", lambda e: e.activation(out=dummy[:, 1:2], in_=dummy[:, 0:1], func=func, **kw),
                  ["dummy0"], ["dummy1"])

        def finish_rstd(n):
            act(rstd[:], ps_st[:], AF.Ln, ["ps_st"], ["rstd"], bias=EPS, scale=1.0 / n)
            act(rstd[:], rstd[:], AF.Exp, ["rstd"], ["rstd"], scale=-0.5)

        def prenorm(l, gname):
            for c in range(8):
                stats_of(xc(xT, c), "xT%d" % c, c, 8)()
            finish_rstd(D)
            for c in range(8):
                stt(xc(xn, c), xc(xT, c), pvc(l, gname, c), rstd[:], ALU.mult, ALU.mult,
                    ["xT%d" % c, "pv", "rstd"], ["xn%d" % c])

        def prenorm_lazy(l, gname):
            for c in range(8):
                stats_of(xc(xT, c), "xT%d" % c, c, 8)()
                act(xc(xn, c), xc(xT, c), AF.Copy, ["xT%d" % c, "pv"], ["xn%d" % c], scale=pvc(l, gname, c))
            finish_rstd(D)

        def postnorm(l, gname):
            finish_rstd(D)
            for c in range(8):
                tt("dve", xc(fT, c), xc(fT, c), rstd[:], ALU.mult, ["fT%d" % c, "rstd"], ["fT%d" % c])
                stt(xc(xT, c), xc(fT, c), pvc(l, gname, c), xc(xT, c), ALU.mult, ALU.add,
                    ["fT%d" % c, "pv", "xT%d" % c], ["xT%d" % c])

        def out_group(ps, pst, dc, pend):
            act(xc(fT, dc), ps[:], AF.Copy, [pst], ["fT%d" % dc])
            pend.append(stats_of(ps[:], pst, dc, 8))

        def ffn(l, f):
            pre, post = ("ffn1_pre_g", "ffn1_post_g") if f == 0 else ("ffn2_pre_g", "ffn2_post_g")
            P.tag = 'ffn_pre'
            prenorm_lazy(l, pre)
            P.tag = 'ffn_up'
            preload(AF.Silu)
            for jp in range(11):
                w, wt = wload(pid(l, "gu%d" % f, jp), 4096)
                banks = [[nb(), nb()] for fi in range(2)]
                if jp == 0:
                    for kc in range(8):
                        for fi in range(2):
                            for s in range(2):
                                o = s * 2048 + kc * 256 + fi * 128
                                ps, pst = banks[fi][s]
                                mm(ps[:], w[:, o:o + 128], xc(xn, kc), kc == 0, kc == 7, [wt, "xn%d" % kc], [pst])
                else:
                    for fi in range(2):
                        for s in range(2):
                            ps, pst = banks[fi][s]
                            for kc in range(8):
                                o = s * 2048 + kc * 256 + fi * 128
                                mm(ps[:], w[:, o:o + 128], xc(xn, kc), kc == 0, kc == 7, [wt, "xn%d" % kc], [pst])
                for fi in range(2):
                    j = jp * 2 + fi
                    (pg, pgt), (pu, put) = banks[fi]
                    i = st["sg"] % 2
                    st["sg"] += 1
                    tt("dve", sg[i][:], pg[:], rstd[:], ALU.mult, [pgt, "rstd"], ["sg%d" % i])
                    act(sg[i][:], sg[i][:], AF.Silu, ["sg%d" % i], ["sg%d" % i])
                    tt("dve", cacc[i][:], pu[:], rstd[:], ALU.mult, [put, "rstd"], ["cacc%d" % i])
                    tt("dve", hch(j), sg[i][:], cacc[i][:], ALU.mult, ["sg%d" % i, "cacc%d" % i], ["h%d" % j])
            preload(AF.Ln)
            P.tag = 'ffn_down'
            pend = []
            for dc in range(8):
                w, wt = wload(pid(l, "dn%d" % f, dc), NFC * 128)
                pf, pft = nb()
                for j in range(NFC):
                    o = j * 128
                    mm(pf[:], w[:, o:o + 128], hch(j), j == 0, j == NFC - 1, [wt, "h%d" % j], [pft])
                for fn in pend:
                    fn()
                pend = []
                out_group(pf, pft, dc, pend)
            for fn in pend:
                fn()
            P.tag = 'ffn_post'
            postnorm(l, post)

        NRAW = 3

        def evac_scaled(ps, pst):
            i = st["raw"] % NRAW
            st["raw"] += 1
            pl = st["pend2"]
            while any(bi == i for bi, _ in pl):
                pl.pop(0)[1]()
            tt("dve", raw[i][:, 3:T + 3], ps[:], rstd[:], ALU.mult, [pst, "rstd"], ["raw%d" % i])
            return i

        def conv_s1(l, i, ci, wname, bname, wc, out_ap, out_tok):
            r, rt, rh = raw[i], "raw%d" % i, "rawh%d" % i
            ho = (l * 10 + ci) * 3
            cp("act", r[:, 0:3], halo[:, ho:ho + 3], ["halo%d" % ci], [rh, rt])
            act(out_ap, r[:, 3:T + 3], AF.Identity, [rt, "pv"], [out_tok], bias=pvc(l, bname, wc), scale=pvc(l, wname % 3, wc))

            def s2():
                for k in (2, 1, 0):
                    stt(out_ap, r[:, k:T + k], pvc(l, wname % k, wc), out_ap, ALU.mult, ALU.add,
                        [rt, rh, "pv", out_tok], [out_tok])
                cp("dve", halo[:, ho:ho + 3], r[:, T:T + 3], [rt], ["halo%d" % ci])
            return s2

        def mixer(l):
            P.tag = 'mix_pre'
            prenorm_lazy(l, "mix_pre_g")
            pr2, pr2t = nb()
            for q in range(4):
                tr(pr2[:, q * 128:(q + 1) * 128], rstd[:, q * 128:(q + 1) * 128], ident, ["rstd", "cst"], [pr2t])
            P.add("dve", lambda e: e.tensor_copy(out=r2tok[:].unsqueeze(2),
                                                 in_=pr2[:].rearrange("p (q c) -> p q c", q=4)[:, :, 0:1]),
                  [pr2t], ["r2tok"])
            P.tag = 'mix_in'
            pend2 = []
            st["pend2"] = pend2

            def flush(keep):
                while len(pend2) > keep:
                    pend2.pop(0)[1]()
            def sgu_part1():
                p1, p1t = nb()
                p2, p2t = nb()
                for c in range(2):
                    mm(p1[:], onesf[:], xc(v_g, c), c == 0, c == 1, ["onesf", "v%d" % c], [p1t])
                for c in range(2):
                    tt("dve", tmp[c][:], xc(v_g, c), xc(v_g, c), ALU.mult, ["v%d" % c], ["tmp%d" % c])
                    mm(p2[:], onesf[:], tmp[c][:], c == 0, c == 1, ["onesf", "tmp%d" % c], [p2t])
                mean, var = tmp[2], tmp[3]
                act(mean[:], p1[:], AF.Copy, [p1t], ["tmp2"], scale=1.0 / 256)
                tt("dve", var[:], mean[:], mean[:], ALU.mult, ["tmp2"], ["tmp3"])
                stt(var[:], p2[:], 1.0 / 256, var[:], ALU.mult, ALU.subtract, [p2t, "tmp3"], ["tmp3"])
                act(var[:], var[:], AF.Ln, ["tmp3"], ["tmp3"], bias=EPS)
                act(var[:], var[:], AF.Exp, ["tmp3"], ["tmp3"], scale=-0.5)
                act(dt_a[:], dt_a[:], AF.Exp, ["dt_a"], ["dt_a"])
                act(dt_a[:], dt_a[:], AF.Ln, ["dt_a"], ["dt_a"], bias=1.0)
                tt("dve", ad_t[:], dt_a[:], hb[:, l * 48 + 24:l * 48 + 48], ALU.mult, ["dt_a", "hb"], ["ad_t"])
                if dbg_d is not None and l == 0 and st["first"]:
                    P.add("sp", lambda e: e.dma_start(out=dbg_d[:, 0:24], in_=dt_a[:]), reads=["dt_a"], dma_key="dbg")
                    P.add("sp", lambda e: e.dma_start(out=dbg_d[:, 24:28], in_=r2tok[:]), reads=["r2tok"], dma_key="dbg")
                    P.add("sp", lambda e: e.dma_start(out=dbg_d[:, 28:60], in_=rstd[:, 0:32]), reads=["rstd"], dma_key="dbg")
                for c in range(2):
                    tt("dve", tmp[c][:], xc(v_g, c), mean[:], ALU.subtract, ["v%d" % c, "tmp2"], ["tmp%d" % c])
                    tt("dve", tmp[c][:], tmp[c][:], var[:], ALU.mult, ["tmp%d" % c, "tmp3"], ["tmp%d" % c])
                    ts(hch(10 + c), tmp[c][:], pvc(l, "sgu_ln_g", c), pvc(l, "sgu_ln_b", c), ALU.mult, ALU.add,
                       ["tmp%d" % c, "pv"], ["h%d" % (10 + c)])

            for ip in (4, 3, 2, 1, 0):
                if ip == 3 and TOG_SGU:
                    flush(0)
                    sgu_part1()
                c0, c1 = IN_PIECES[ip]
                n = c1 - c0
                w, wt = wload(pid(l, "in", ip), 8 * n)
                for (kind, c, col0, width) in IN_CHUNKS:
                    if not (c0 <= col0 < c1):
                        continue
                    lo = col0 - c0
                    if kind == "dt":
                        pdt, pdtt = nb()
                        for q in range(4):
                            for kc in range(8):
                                mm(pdt[:, q * 6:(q + 1) * 6], xn[:, kc * T + q * 128:kc * T + (q + 1) * 128],
                                   w[:, kc * n + lo:kc * n + lo + 6], kc == 0, kc == 7, [wt, "xn%d" % kc], [pdtt])
                        P.add("dve", lambda e, pdt=pdt: e.tensor_tensor(
                            out=dt_a[:].rearrange("p (q h) -> p q h", q=4),
                            in0=pdt[:, 0:24].rearrange("p (q h) -> p q h", q=4),
                            in1=r2tok[:].unsqueeze(2).to_broadcast([128, 4, 6]), op=ALU.mult),
                            [pdtt, "r2tok"], ["dt_a"])
                        tt("dve", dt_a[:], dt_a[:], hb[:, l * 48:l * 48 + 24], ALU.add, ["dt_a", "hb"], ["dt_a"])
                        continue
                    ps, pst = nb()
                    for kc in range(8):
                        mm(ps[:], w[:, kc * n + lo:kc * n + lo + 128], xc(xn, kc), kc == 0, kc == 7,
                           [wt, "xn%d" % kc], [pst])
                    ri = evac_scaled(ps, pst)
                    rsrc, rtk = raw[ri][:, 3:T + 3], "raw%d" % ri
                    if kind == "gate":
                        act(xc(gate_g, c), rsrc, AF.Gelu, [rtk], ["gate%d" % c])
                    elif kind == "z":
                        act(xc(zs, c), rsrc, AF.Silu, [rtk], ["zs%d" % c])
                    elif kind == "u":
                        act(xc(u_g, c), rsrc, AF.Gelu, [rtk], ["u%d" % c])
                    elif kind == "v":
                        act(xc(v_g, c), rsrc, AF.Gelu, [rtk], ["v%d" % c])
                    elif kind == "rec":
                        s2 = conv_s1(l, ri, c, "lru_cw%d", "lru_conv_b", c, xc(rec32, c), "rec%d" % c)

                        def s2rec(s2=s2, c=c):
                            s2()
                            cp("act", hch(c), xc(rec32, c), ["rec%d" % c], ["h%d" % c])
                        flush(1)
                        pend2.append((ri, s2rec))
                        continue
                    elif kind == "xbc":
                        i = st["cacc"] % NRAW
                        st["cacc"] += 1
                        s2 = conv_s1(l, ri, 3 + c, "ssd_cw%d", "ssd_conv_b", c, cacc[i][:], "cacc%d" % i)

                        def s2x(s2=s2, c=c, i=i):
                            s2()
                            act(hch(3 + c), cacc[i][:], AF.Silu, ["cacc%d" % i], ["h%d" % (3 + c)])
                        flush(1)
                        pend2.append((ri, s2x))
                        continue
                    flush(1)
            flush(0)
            if not TOG_SGU:
                sgu_part1()

            P.tag = 'sgu'
            pm = [nb(), nb()]
            pend = []
            for q in range(4):
                pt, ptt = nbf()
                for c in range(2):
                    tr(pt[:, c * 128:(c + 1) * 128], hT[:, (10 + c) * T + q * 128:(10 + c) * T + (q + 1) * 128],
                       identb[:], ["h%d" % (10 + c), "identb"], [ptt])
                vt = vtok[q % 2]
                cp("act", vt, pt[:, 0:256], [ptt], [tvtok[q % 2]])
                for fn in pend:
                    fn()
                pend = []

                def mixq(q=q, vt=vt):
                    for g in range(4):
                        c, po = g // 2, (g % 2) * 64
                        o = (l * 4 + g) * 128
                        dst = pm[c][0][po:po + 64, q * 128:(q + 1) * 128]
                        mm(dst, vt[:, g * 64:(g + 1) * 64], wst[:, o:o + 128], True, False,
                           [tvtok[q % 2], "wst"], [pm[c][1]])
                        mm(dst, onesb[0:1, 0:64], bs[0:1, o:o + 128], False, True, ["onesb", "bs"], [pm[c][1]])
                pend.append(mixq)
            for fn in pend:
                fn()
            for c in range(2):
                tt("dve", xc(xn, 6 + c), xc(u_g, c), pm[c][0][:], ALU.mult, ["u%d" % c, pm[c][1]], ["xn%d" % (6 + c)])

            lru = []
            A_ = [tmp[0], tmp[1], tmp[2]]
            tA = ["tmp0", "tmp1", "tmp2"]
            B_ = [cacc[0][:], cacc[1][:], cacc[2][:]]
            tB = ["cacc0", "cacc1", "cacc2"]
            C_ = [tmp[3][:], tmp[4][:], rstd[:]]
            tC = ["tmp3", "tmp4", "rstd"]

            def L(fn):
                lru.append(fn)
            for c in range(3):
                def gates(c=c):
                    pr, prt = nb()
                    pi_, pit = nb()
                    o = ((l * 2 + 0) * 3 + c) * 128
                    mm(pr[:], wbd[:, o:o + 128], hch(c), True, True, ["wbd", "h%d" % c], [prt])
                    o = ((l * 2 + 1) * 3 + c) * 128
                    mm(pi_[:], wbd[:, o:o + 128], hch(c), True, True, ["wbd", "h%d" % c], [pit])
                    act(A_[c][:], pr[:], AF.Sigmoid, [prt, "pv"], [tA[c]], bias=pvc(l, "lru_b_r", c))
                    act(B_[c], pi_[:], AF.Sigmoid, [pit, "pv"], [tB[c]], bias=pvc(l, "lru_b_i", c))
                L(gates)
            for c in range(3):
                L(lambda c=c: act(A_[c][:], A_[c][:], AF.Exp, [tA[c], "pv"], [tA[c]], scale=pvc(l, "lru_lambda", c)))
                L(lambda c=c: tt("dve", C_[c], A_[c][:], A_[c][:], ALU.mult, [tA[c]], [tC[c]]))
                L(lambda c=c: tt("dve", B_[c], B_[c], xc(rec32, c), ALU.mult, [tB[c], "rec%d" % c], [tB[c]]))
            for c in range(3):
                L(lambda c=c: act(C_[c], C_[c], AF.Ln, [tC[c]], [tC[c]], bias=1.0, scale=-1.0))
            for c in range(3):
                L(lambda c=c: act(C_[c], C_[c], AF.Exp, [tC[c]], [tC[c]], scale=0.5))
                L(lambda c=c: tt("dve", B_[c], B_[c], C_[c], ALU.mult, [tB[c], tC[c]], [tB[c]]))
            for c in range(3):
                def scan(c=c):
                    hs = hstate[:, l * 3 + c:l * 3 + c + 1]
                    P.add("dve", lambda e: e.tensor_tensor_scan(
                        out=C_[c], data0=A_[c][:], data1=B_[c], initial=hs, op0=ALU.mult, op1=ALU.add),
                        [tA[c], tB[c], "hst%d" % c], [tC[c]])
                    cp(st["pool"], hs, C_[c][:, T - 1:T], [tC[c]], ["hst%d" % c])
                    tt("dve", xc(xn, c), C_[c], xc(gate_g, c), ALU.mult, [tC[c], "gate%d" % c], ["xn%d" % c])
                L(scan)

            def lru_some(k):
                for _ in range(k):
                    if lru:
                        lru.pop(0)()

            P.tag = 'ssd'
            lru_some(3)
            cp("act", Hbf[0], H32[:, l * 384:(l + 1) * 384], ["H32"], ["h19"])
            pe2 = st["pool"]

            def mk_rhsU(k):
                kb, a0 = k % 2, (k // 2) * 6 + (k % 2) * 3
                P.add("dve", lambda e: e.tensor_tensor(
                    out=rhsU[kb][:].rearrange("p (j l) -> p j l", j=3),
                    in0=Umask.unsqueeze(1).to_broadcast([128, 3, 128]),
                    in1=ad_t[:, a0:a0 + 3].unsqueeze(2).to_broadcast([128, 3, 128]), op=ALU.mult),
                    ["cst", "ad_t"], ["rhsU%d" % kb])
            mk_rhsU(0)
            for q in range(4):
                for g in range(2):
                    k = q * 2 + g
                    kb = k % 2
                    a0 = q * 6 + g * 3
                    v3 = lambda ap: ap.rearrange("p (j l) -> p j l", j=3)
                    pD, pDt = nb()
                    mm(pD[:, 0:384], Lstr, rhsU[kb][:], True, False, ["cst", "rhsU%d" % kb], [pDt])
                    mm(pD[:, 0:384], ident, neg3[:], False, True, ["cst", "neg3"], [pDt])
                    pC, pCt = nb()
                    mm(pC[:, 0:384], onesf[:], rhsU[kb][:], True, True, ["onesf", "rhsU%d" % kb], [pCt])
                    pcb, pcbt = nb()
                    mm(pcb[:, 0:128], hT[:, (6 + g) * T + q * 128:(6 + g) * T + (q + 1) * 128],
                       hT[:, (8 + g) * T + q * 128:(8 + g) * T + (q + 1) * 128], True, True,
                       ["h%d" % (6 + g), "h%d" % (8 + g)], [pcbt])
                    if k < 7:
                        mk_rhsU(k + 1)
                    act(Eb[kb][:], pD[:, 0:384], AF.Exp, [pDt], ["E%d" % kb])
                    act(E2b[kb][:], pC[:, 0:384], AF.Exp, [pCt], ["E2%d" % kb])
                    P.add("dve", lambda e, kb=kb, k=k: e.tensor_copy(
                        out=wcol[:, k * 3:(k + 1) * 3].unsqueeze(2),
                        in_=Eb[kb][:].rearrange("p (j l) -> p j l", j=3)[:, :, 127:128]),
                        ["E%d" % kb], ["wcol"])
                    P.add("dve", lambda e, kb=kb, k=k: e.tensor_copy(
                        out=dcol[:, k * 3:(k + 1) * 3].unsqueeze(2),
                        in_=E2b[kb][:].rearrange("p (j l) -> p j l", j=3)[:, :, 127:128]),
                        ["E2%d" % kb], ["dcol"])
                    P.add("dve", lambda e, kb=kb, k=k, pcb=pcb: e.tensor_tensor(
                        out=MTall[:, k * 384:(k + 1) * 384].rearrange("p (j l) -> p j l", j=3),
                        in0=Eb[kb][:].rearrange("p (j l) -> p j l", j=3),
                        in1=pcb[:, 0:128].unsqueeze(1).to_broadcast([128, 3, 128]), op=ALU.mult),
                        ["E%d" % kb, pcbt], ["MT%d" % k])
                    P.add(pe2, lambda e, kb=kb, k=k, g=g, q=q: e.tensor_tensor(
                        out=CsTall[:, k * 384:(k + 1) * 384].rearrange("p (j l) -> p j l", j=3),
                        in0=E2b[kb][:].rearrange("p (j l) -> p j l", j=3),
                        in1=hT[:, (8 + g) * T + q * 128:(8 + g) * T + (q + 1) * 128].unsqueeze(1).to_broadcast([128, 3, 128]),
                        op=ALU.mult),
                        ["E2%d" % kb, "h%d" % (8 + g)], ["CsT%d" % k])
                if q % 2 == 1:
                    lru_some(3)
            xdtB = [xdt, hT[:, 12 * T:12 * T + 384]]
            xdtwB = [xdtw, hT[:, 13 * T:13 * T + 384]]
            BtokB = [Btok, hT[:, 14 * T:14 * T + 256]]
            dtwB = [dtw, dtw2]
            txdt, txdtw, tbtok, tdtw = ["h16", "h12"], ["h17", "h13"], ["h18", "h14"], ["dtw", "dtw2"]

            def prep(q):
                pb_ = q % 2
                pt, ptt = nbf()
                for c in range(3):
                    tr(pt[:, c * 128:(c + 1) * 128], hT[:, (3 + c) * T + q * 128:(3 + c) * T + (q + 1) * 128],
                       identb[:], ["h%d" % (3 + c), "identb"], [ptt])
                for g in range(2):
                    tr(pt[:, 384 + g * 128:384 + (g + 1) * 128],
                       hT[:, (6 + g) * T + q * 128:(6 + g) * T + (q + 1) * 128], identb[:],
                       ["h%d" % (6 + g), "identb"], [ptt])
                P.add("dve", lambda e: e.tensor_tensor(
                    out=xdtB[pb_].rearrange("p (h d) -> p h d", h=6),
                    in0=pt[:, 0:384].rearrange("p (h d) -> p h d", h=6),
                    in1=dt_a[:, q * 6:(q + 1) * 6].unsqueeze(2).to_broadcast([128, 6, 64]), op=ALU.mult),
                    [ptt, "dt_a"], [txdt[pb_]])
                tt("dve", dtwB[pb_][:], dt_a[:, q * 6:(q + 1) * 6], wcol[:, q * 6:(q + 1) * 6], ALU.mult,
                   ["dt_a", "wcol"], [tdtw[pb_]])
                P.add("dve", lambda e: e.tensor_tensor(
                    out=xdtwB[pb_].rearrange("p (h d) -> p h d", h=6),
                    in0=pt[:, 0:384].rearrange("p (h d) -> p h d", h=6),
                    in1=dtwB[pb_][:].unsqueeze(2).to_broadcast([128, 6, 64]), op=ALU.mult),
                    [ptt, tdtw[pb_]], [txdtw[pb_]])
                cp("dve", BtokB[pb_], pt[:, 384:640], [ptt], [tbtok[pb_]])
            prep(0)
            for q in range(4):
                par = q % 2
                if q < 3:
                    prep(q + 1)
                xdt_, xdtw_, Btok_ = xdtB[par], xdtwB[par], BtokB[par]
                py, pyt = nb()
                for h in range(6):
                    g, j, cc, po = h // 3, h % 3, h // 2, (h % 2) * 64
                    k = q * 2 + g
                    dst = py[po:po + 64, cc * 128:(cc + 1) * 128]
                    mm(dst, xdt_[:, h * 64:(h + 1) * 64], MTall[:, k * 384 + j * 128:k * 384 + (j + 1) * 128], True, False,
                       [txdt[par], "MT%d" % k], [pyt])
                    mm(dst, Hbf[par][:, h * 64:(h + 1) * 64], CsTall[:, k * 384 + j * 128:k * 384 + (j + 1) * 128],
                       False, True, ["h%d" % (19 + par), "CsT%d" % k], [pyt])
                for cc in range(3):
                    stt(yT2[:, cc * T + q * 128:cc * T + (q + 1) * 128],
                        hT[:, (3 + cc) * T + q * 128:(3 + cc) * T + (q + 1) * 128], pvc(l, "ssd_dvec", cc),
                        py[:, cc * 128:(cc + 1) * 128], ALU.mult, ALU.add,
                        ["h%d" % (3 + cc), "pv", pyt], ["yS%d" % cc])
                pS, pSt = nb()
                for g in range(2):
                    mm(pS[:, g * 192:(g + 1) * 192], Btok_[:, g * 128:(g + 1) * 128], xdtw_[:, g * 192:(g + 1) * 192],
                       True, True, [tbtok[par], txdtw[par]], [pSt])
                P.add(pe2, lambda e, q=q: e.tensor_tensor(
                    out=H32[:, l * 384:(l + 1) * 384].rearrange("p (h d) -> p h d", h=6),
                    in0=H32[:, l * 384:(l + 1) * 384].rearrange("p (h d) -> p h d", h=6),
                    in1=dcol[:, q * 6:(q + 1) * 6].unsqueeze(2).to_broadcast([128, 6, 64]),
                    op=ALU.mult), ["H32", "dcol"], ["H32"])
                tt("dve", H32[:, l * 384:(l + 1) * 384], H32[:, l * 384:(l + 1) * 384], pS[:, 0:384], ALU.add,
                   ["H32", pSt], ["H32"])
                if q < 3:
                    cp("act", Hbf[1 - par], H32[:, l * 384:(l + 1) * 384], ["H32"], ["h%d" % (19 + 1 - par)])
                lru_some(3)
            lru_some(100)
            P.tag = 'mix_out'
            pend = []
            korder = (6, 7, 0, 1, 2, 3, 4, 5)
            w0, wt0 = wload(pid(l, "out", 0), 4096)
            bks = [nb() for di in range(4)]
            for ki, kc in enumerate(korder[:5]):
                for di in range(4):
                    o = kc * 512 + di * 128
                    mm(bks[di][0][:], w0[:, o:o + 128], xc(xn, kc), ki == 0, False, [wt0, "xn%d" % kc], [bks[di][1]])
            P.tag = 'ssd_norm'
            for cc in range(3):
                tt("dve", xc(yT2, cc), xc(yT2, cc), xc(zs, cc), ALU.mult, ["yS%d" % cc, "zs%d" % cc], ["yS%d" % cc])
                stats_of(xc(yT2, cc), "yS%d" % cc, cc, 3)()
            finish_rstd(384)
            for cc in range(3):
                stt(xc(xn, 3 + cc), xc(yT2, cc), pvc(l, "ssd_norm_g", cc), rstd[:], ALU.mult, ALU.mult,
                    ["yS%d" % cc, "pv", "rstd"], ["xn%d" % (3 + cc)])
            P.tag = 'mix_out'
            for ki, kc in enumerate(korder[5:]):
                for di in range(4):
                    o = kc * 512 + di * 128
                    mm(bks[di][0][:], w0[:, o:o + 128], xc(xn, kc), False, ki == 2, [wt0, "xn%d" % kc], [bks[di][1]])
            for di in range(4):
                for fn in pend:
                    fn()
                del pend[:]
                out_group(bks[di][0], bks[di][1], di, pend)
            w, wt = wload(pid(l, "out", 1), 4096)
            for di in range(4):
                dc = 4 + di
                pf, pft = nb()
                for ki, kc in enumerate(korder):
                    o = kc * 512 + di * 128
                    mm(pf[:], w[:, o:o + 128], xc(xn, kc), ki == 0, ki == 7, [wt, "xn%d" % kc], [pft])
                for fn in pend:
                    fn()
                pend = []
                out_group(pf, pft, dc, pend)
            for fn in pend:
                fn()
            P.tag = 'mix_post'
            postnorm(l, "mix_post_g")

        fall = ["fT%d" % c for c in range(8)]
        for ti in range(n_tiles):
            P.tag = 'io_in'
            src = x_d[ti * T:(ti + 1) * T, :].rearrange("(q p) d -> p q d", p=128)
            P.add("sp", lambda e, src=src: e.dma_start(out=fT[:].rearrange("p (q d) -> p q d", q=4), in_=src),
                  writes=fall, dma_key="xin")
            for c in range(8):
                ps, pst = nb()
                for q in range(4):
                    tr(ps[:, q * 128:(q + 1) * 128], fT[:, q * D + c * 128:q * D + (c + 1) * 128], ident,
                       fall + ["cst"], [pst])
                cp("dve" if c % 2 else "act", xc(xT, c), ps[:], [pst], ["xT%d" % c])
            st["pool"] = "dve" if ti == 0 else "pool"
            st["first"] = (ti == 0)
            for l in range(L):
                ffn(l, 0)
                mixer(l)
                ffn(l, 1)
            P.tag = 'io_out'
            for q in range(4):
                for hf in range(2):
                    ps, pst = nb()
                    for c4 in range(4):
                        c = hf * 4 + c4
                        tr(ps[:, c4 * 128:(c4 + 1) * 128], xT[:, c * T + q * 128:c * T + (q + 1) * 128], ident,
                           ["xT%d" % c, "cst"], [pst])
                    cp("dve" if hf else "act", fT[:, q * D + hf * 512:q * D + (hf + 1) * 512], ps[:], [pst], fall)
            dst = out_d[ti * T:(ti + 1) * T, :].rearrange("(q p) d -> p q d", p=128)
            P.add("sp", lambda e, dst=dst: e.dma_start(out=dst, in_=fT[:].rearrange("p (q d) -> p q d", q=4)),
                  reads=fall, dma_key="xout")

        stats = P.emit(final_dma_keys=["xout"])
    build.prog = P
    return nc, stats


N_CORES = 8
SEQ = 4096
DEPTH = 4


def kernel(**inputs):
    L = DEPTH
    x = np.asarray(inputs["x"], np.float32)
    B = x.shape[0]
    n_tiles = x.shape[1] // T
    nc, _ = build(n_tiles, L)
    shared = prep_params(inputs, L)
    for nm in ("ffn1_w_gu", "ffn2_w_gu", "ffn1_w_down", "ffn2_w_down", "mix_w_in", "mix_w_out"):
        shared[nm] = np.ascontiguousarray(np.asarray(inputs[nm], np.float32))
    in_maps = []
    for b in range(B):
        m = dict(shared)
        m["x"] = np.ascontiguousarray(x[b])
        in_maps.append(m)
    res = run_bass_kernel_spmd(nc, in_maps, core_ids=list(range(B)))
    return np.stack([np.asarray(r["out"], np.float32) for r in res.results], axis=0)
```

```python
import numpy as np
from contextlib import ExitStack
import concourse.bass as bass
import concourse.mybir as mybir
from concourse.bass_utils import run_bass_kernel_spmd

F32 = mybir.dt.float32
BF16 = mybir.dt.bfloat16
AF = mybir.ActivationFunctionType
ALU = mybir.AluOpType

D = 1024
DFF = 2816
NFC = 22
INC = 2566
T = 512
EPS = 1e-6
SLOT = 4160
NSLOT = 4
import os
TOG_OUT = os.environ.get('TOG_OUT', '1') == '1'
TOG_SGU = os.environ.get('TOG_SGU', '1') == '1'


class Op:
    __slots__ = ("eng", "fn", "is_dma", "key", "waits_dma", "deps", "sig", "idx", "tag")


class Prog:
    ENGS = ("pe", "act", "dve", "pool", "sp")

    def __init__(self, nc):
        self.nc = nc
        self.ops = []
        self.last_w = {}
        self.readers = {}
        self.dma_count = {}
        self.dma_keys = []
        self.tag = ""

    def add(self, eng, fn, reads=(), writes=(), dma_key=None):
        op = Op()
        op.eng = eng
        op.fn = fn
        op.is_dma = dma_key is not None
        op.key = dma_key
        op.idx = len(self.ops)
        op.sig = 0
        op.tag = self.tag
        deps = {}
        wd = {}

        def dep(d, same_ok):
            if d is None or d is op:
                return
            if d.is_dma:
                wd[d.key] = self.dma_count[d.key]
            else:
                if d.eng == eng and same_ok and not op.is_dma and eng == "pe":
                    return
                deps[d.idx] = d

        for t in reads:
            dep(self.last_w.get(t), False)
        for t in writes:
            dep(self.last_w.get(t), True)
            for r in self.readers.get(t, ()):
                dep(r, True)
        op.deps = list(deps.values())
        op.waits_dma = wd
        for d in op.deps:
            d.sig = 1
        for t in reads:
            self.readers.setdefault(t, []).append(op)
        for t in writes:
            self.last_w[t] = op
            self.readers[t] = []
        if op.is_dma:
            if dma_key not in self.dma_count:
                self.dma_count[dma_key] = 0
                self.dma_keys.append(dma_key)
            self.dma_count[dma_key] += 16
        self.ops.append(op)
        return op

    def emit(self, final_dma_keys=()):
        nc = self.nc
        with ExitStack() as es:
            esem = {e: es.enter_context(nc.semaphore("s_" + e)) for e in self.ENGS}
            dsem = {k: es.enter_context(nc.semaphore("d_%d" % i)) for i, k in enumerate(self.dma_keys)}
            cnt = {e: 0 for e in self.ENGS}
            for op in self.ops:
                if (not op.is_dma) and op.sig:
                    cnt[op.eng] += 1
                    op.sig = cnt[op.eng]
            per = {e: [o for o in self.ops if o.eng == e] for e in self.ENGS}
            stats = {e: [len(per[e]), 0] for e in self.ENGS}
            block = es.enter_context(nc.Block())

            def run(engname, engobj):
                waited = {}
                for op in per[engname]:
                    for d in op.deps:
                        k = ("e", d.eng)
                        if waited.get(k, 0) < d.sig:
                            engobj.wait_ge(esem[d.eng], d.sig)
                            waited[k] = d.sig
                            stats[engname][1] += 1
                    for key, val in op.waits_dma.items():
                        k = ("d", key)
                        if waited.get(k, 0) < val:
                            engobj.wait_ge(dsem[key], val)
                            waited[k] = val
                            stats[engname][1] += 1
                    ins = op.fn(engobj)
                    if op.is_dma:
                        ins.then_inc(dsem[op.key], 16)
                    elif op.sig:
                        ins.then_inc(esem[engname], 1)
                if engname == "sp":
                    for key in final_dma_keys:
                        engobj.wait_ge(dsem[key], self.dma_count[key])

            @block.tensor
            def _(e):
                run("pe", e)

            @block.scalar
            def _(e):
                run("act", e)

            @block.vector
            def _(e):
                run("dve", e)

            @block.gpsimd
            def _(e):
                run("pool", e)

            @block.sync
            def _(e):
                run("sp", e)
        return stats


PV_SPEC = [
    ("ffn1_pre_g", 8), ("ffn1_post_g", 8), ("mix_pre_g", 8), ("mix_post_g", 8),
    ("ffn2_pre_g", 8), ("ffn2_post_g", 8),
    ("lru_cw0", 3), ("lru_cw1", 3), ("lru_cw2", 3), ("lru_cw3", 3), ("lru_conv_b", 3),
    ("lru_b_r", 3), ("lru_b_i", 3), ("lru_lambda", 3),
    ("ssd_cw0", 7), ("ssd_cw1", 7), ("ssd_cw2", 7), ("ssd_cw3", 7), ("ssd_conv_b", 7),
    ("ssd_norm_g", 3), ("ssd_dvec", 3), ("sgu_ln_g", 2), ("sgu_ln_b", 2),
]
PV_PER_LAYER = sum(n for _, n in PV_SPEC)


def pv_off(l, name):
    o = l * PV_PER_LAYER
    for nm, n in PV_SPEC:
        if nm == name:
            return o
        o += n
    raise KeyError(name)


def _colmajor(v):
    v = np.asarray(v, np.float32)
    return np.ascontiguousarray(v.reshape(-1, 128).T)


def prep_params(inp, L):
    pv = np.zeros((128, L * PV_PER_LAYER), np.float32)
    for l in range(L):
        vecs = {
            "ffn1_pre_g": inp["ffn1_pre_g"][l], "ffn1_post_g": inp["ffn1_post_g"][l],
            "mix_pre_g": inp["mix_pre_g"][l], "mix_post_g": inp["mix_post_g"][l],
            "ffn2_pre_g": inp["ffn2_pre_g"][l], "ffn2_post_g": inp["ffn2_post_g"][l],
            "lru_conv_b": inp["lru_conv_b"][l], "lru_b_r": inp["lru_b_r"][l],
            "lru_b_i": inp["lru_b_i"][l], "lru_lambda": inp["lru_lambda"][l],
            "ssd_conv_b": inp["ssd_conv_b"][l], "ssd_norm_g": inp["ssd_norm_g"][l],
            "ssd_dvec": np.repeat(np.asarray(inp["ssd_d"][l]), 64),
            "sgu_ln_g": inp["sgu_ln_g"][l], "sgu_ln_b": inp["sgu_ln_b"][l],
        }
        for k in range(4):
            vecs["lru_cw%d" % k] = inp["lru_conv_w"][l][k]
            vecs["ssd_cw%d" % k] = inp["ssd_conv_w"][l][k]
        for nm, n in PV_SPEC:
            o = pv_off(l, nm)
            pv[:, o:o + n] = _colmajor(vecs[nm])
    hb = np.zeros((128, L * 48), np.float32)
    for l in range(L):
        hb[:, l * 48:l * 48 + 24] = np.tile(np.asarray(inp["ssd_dt_bias"][l], np.float32), 4)[None, :]
        hb[:, l * 48 + 24:l * 48 + 48] = np.tile(np.asarray(inp["ssd_a_log"][l], np.float32), 4)[None, :]
    wbd = np.zeros((128, L * 6 * 128), np.float32)
    for l in range(L):
        for gi, nm in enumerate(("lru_w_r", "lru_w_i")):
            w = np.asarray(inp[nm][l], np.float32)
            for c in range(3):
                o = ((l * 2 + gi) * 3 + c) * 128
                for hh in range(2):
                    wbd[hh * 64:(hh + 1) * 64, o + hh * 64:o + (hh + 1) * 64] = w[2 * c + hh]
    ws = np.asarray(inp["sgu_w_s"], np.float32)[:L]
    wst = np.ascontiguousarray(ws.transpose(3, 0, 1, 2).reshape(128, L * 4 * 128))
    bs = np.ascontiguousarray(np.asarray(inp["sgu_b_s"], np.float32)[:L].reshape(1, L * 4 * 128))
    k = np.arange(128)
    ident = np.eye(128, dtype=np.float32)
    U = (k[:, None] <= k[None, :]).astype(np.float32)
    Ls = (k[:, None] > k[None, :]).astype(np.float32)
    cst = np.ascontiguousarray(np.concatenate([ident, U, Ls], axis=1))
    return {"pv": pv, "hb": hb, "wbd": wbd, "wst": wst, "bs": bs, "cst": cst}


IN_CHUNKS = ([("gate", c, c * 128, 128) for c in range(3)] +
             [("rec", c, 384 + c * 128, 128) for c in range(3)] +
             [("z", c, 768 + c * 128, 128) for c in range(3)] +
             [("xbc", c, 1152 + c * 128, 128) for c in range(7)] +
             [("dt", 0, 2048, 6)] +
             [("u", c, 2054 + c * 128, 128) for c in range(2)] +
             [("v", c, 2310 + c * 128, 128) for c in range(2)])
IN_PIECES = [(0, 512), (512, 1024), (1024, 1536), (1536, 2048), (2048, 2566)]


def build(n_tiles, L):
    NTOK = n_tiles * T
    nc = bass.Bass("TRN2", target_bir_lowering=False)

    def din(name, shape, dt=F32):
        return nc.dram_tensor(name, shape, dt, kind="ExternalInput").ap()

    x_d = din("x", [NTOK, D])
    wgu_d = [din("ffn1_w_gu", [L, D, 2 * DFF]), din("ffn2_w_gu", [L, D, 2 * DFF])]
    wdn_d = [din("ffn1_w_down", [L, DFF, D]), din("ffn2_w_down", [L, DFF, D])]
    win_d = din("mix_w_in", [L, D, INC])
    wout_d = din("mix_w_out", [L, D, D])
    pv_d = din("pv", [128, L * PV_PER_LAYER])
    hb_d = din("hb", [128, L * 48])
    wbd_d = din("wbd", [128, L * 6 * 128])
    wst_d = din("wst", [128, L * 4 * 128])
    bs_d = din("bs", [1, L * 4 * 128])
    cst_d = din("cst", [128, 3 * 128])
    out_d = nc.dram_tensor("out", [NTOK, D], F32, kind="ExternalOutput").ap()
    dbg_d = nc.dram_tensor("dbg", [128, 64], F32, kind="ExternalOutput").ap() if os.environ.get("KDBG") else None

    PPL = 45
    wsc = nc.dram_tensor("wsc", [L * PPL, 128, SLOT], BF16).ap()

    def pid(l, sub, i):
        base = {"gu0": 0, "dn0": 11, "in": 19, "out": 24, "gu1": 26, "dn1": 37}[sub]
        return l * PPL + base + i

    P = Prog(nc)
    es = ExitStack()
    with es:
        def sb(name, n, dt=F32, parts=128):
            return es.enter_context(nc.sbuf_tensor(name, [parts, n], dt))

        def psum(name, n, dt=F32):
            return es.enter_context(nc.psum_tensor(name, [128, n], dt))

        xT = sb("xT", 8 * T)
        xn = sb("xn", 8 * T, BF16)
        fT = sb("fT", 8 * T)
        hT = sb("hT", NFC * T, BF16)
        wsl = [sb("wsl%d" % i, SLOT, BF16) for i in range(NSLOT)]
        sq = [sb("sq%d" % i, T, BF16) for i in range(2)]
        rstd = sb("rstd", T)
        sg = [sb("sg%d" % i, T) for i in range(2)]
        pv = sb("pv_s", L * PV_PER_LAYER)
        hb = sb("hb_s", L * 48)
        wbd = sb("wbd_s", L * 6 * 128, BF16)
        wst32 = sb("wst32", 4 * 128)
        wst = sb("wst_s", L * 4 * 128, BF16)
        bs = sb("bs_s", L * 4 * 128, BF16, parts=1)
        cst = sb("cst_s", 3 * 128)
        identb = sb("identb", 128, BF16)
        onesb = sb("onesb", 128, BF16)
        onesf = sb("onesf", 128)
        halo = sb("halo", L * 10 * 3)
        hstate = sb("hstate", L * 3)
        H32 = sb("H32", L * 384)
        gate_g = sb("gate_g", 3 * T)
        rec32 = sb("rec32", 3 * T)
        zs = sb("zs", 3 * T)
        raw = [sb("raw%d" % i, T + 3) for i in range(3)]
        cacc = [sb("cacc%d" % i, T) for i in range(3)]
        u_g = sb("u_g", 2 * T)
        v_g = sb("v_g", 2 * T)
        tmp = [sb("tmp%d" % i, T) for i in range(5)]
        rhsU = [sb("rhsU%d" % i, 384) for i in range(2)]
        Eb = [sb("E%d" % i, 384) for i in range(2)]
        E2b = [sb("E2%d" % i, 384) for i in range(2)]
        dt_a = sb("dt_a", 24)
        ad_t = sb("ad_t", 24)
        dtw = sb("dtw", 6)
        r2tok = sb("r2tok", 4)
        MTall = sb("MTall", 8 * 384, BF16)
        CsTall = sb("CsTall", 8 * 384, BF16)
        wcol = sb("wcol", 24)
        neg3 = sb("neg3", 384)
        dtw2 = sb("dtw2", 6)
        dcol = sb("dcol", 24)
        dummy = sb("dummy_act", 2)
        yT = sb("yT", 3 * T)
        yT2 = yT
        ident = cst[:, 0:128]
        Umask = cst[:, 128:256]
        Lstr = cst[:, 256:384]

        def hch(j, n=T, o=0):
            return hT[:, j * T + o:j * T + o + n]
        MT = [hT[:, 12 * T + g * 384:12 * T + (g + 1) * 384] for g in range(2)]
        CsT = [hT[:, 14 * T + g * 384:14 * T + (g + 1) * 384] for g in range(2)]
        xdt = hT[:, 16 * T:16 * T + 384]
        xdtw = hT[:, 17 * T:17 * T + 384]
        Btok = hT[:, 18 * T:18 * T + 256]
        Hbf = [hT[:, 19 * T:19 * T + 384], hT[:, 20 * T:20 * T + 384]]
        vtok = [hT[:, 21 * T:21 * T + 256], hT[:, 18 * T + 256:18 * T + 512]]
        tvtok = ["h21", "h18"]
        tMT = [["h12"], ["h12", "h13"]]
        tCsT = [["h14"], ["h14", "h15"]]

        NPB = 5
        pbank = [psum("pb%d" % i, 512) for i in range(NPB)]
        ps_st = psum("ps_st", 512)
        pbf = [psum("pbf%d" % i, 1024, BF16) for i in range(2)]
        st = {"pb": 0, "pbf": 0, "ws": 0, "sq": 0, "sg": 0, "raw": 0, "cacc": 0, "pool": "pool", "first": True}

        def nb():
            i = st["pb"] % NPB
            st["pb"] += 1
            return pbank[i], "pb%d" % i

        def nbf():
            i = st["pbf"] % 2
            st["pbf"] += 1
            return pbf[i], "pbf%d" % i

        def mm(out, lhsT, rhs, start, stop, reads, writes):
            P.add("pe", lambda e: e.matmul(out, lhsT=lhsT, rhs=rhs, start=start, stop=stop), reads, writes)

        def tr(out, in_, idn, reads, writes):
            P.add("pe", lambda e: e.transpose(out, in_, idn), reads, writes)

        def act(out, in_, func, reads, writes, bias=None, scale=None):
            kw = {}
            if bias is not None:
                kw["bias"] = bias
            if scale is not None:
                kw["scale"] = scale
            P.add("act", lambda e: e.activation(out=out, in_=in_, func=func, **kw), reads, writes)

        def tt(eng, out, in0, in1, op, reads, writes):
            P.add(eng, lambda e: e.tensor_tensor(out=out, in0=in0, in1=in1, op=op), reads, writes)

        def stt(out, in0, scalar, in1, op0, op1, reads, writes):
            P.add("dve", lambda e: e.scalar_tensor_tensor(out=out, in0=in0, scalar=scalar, in1=in1, op0=op0, op1=op1), reads, writes)

        def ts(out, in0, s1, s2, op0, op1, reads, writes):
            if s2 is None:
                P.add("dve", lambda e: e.tensor_scalar(out=out, in0=in0, scalar1=s1, scalar2=None, op0=op0), reads, writes)
            else:
                P.add("dve", lambda e: e.tensor_scalar(out=out, in0=in0, scalar1=s1, scalar2=s2, op0=op0, op1=op1), reads, writes)

        def cp(eng, out, in_, reads, writes):
            if eng == "act":
                act(out, in_, AF.Copy, reads, writes)
            else:
                P.add(eng, lambda e: e.tensor_copy(out=out, in_=in_), reads, writes)

        def xc(buf, c, n=T):
            return buf[:, c * n:(c + 1) * n]

        def pvc(l, name, c):
            o = pv_off(l, name) + c
            return pv[:, o:o + 1]

        P.add("sp", lambda e: e.dma_start(out=cst[:], in_=cst_d), writes=["cst"], dma_key="c_cst")
        P.add("sp", lambda e: e.dma_start(out=pv[:], in_=pv_d), writes=["pv"], dma_key="c_pv")
        P.add("sp", lambda e: e.dma_start(out=hb[:], in_=hb_d), writes=["hb"], dma_key="c_hb")
        P.add("pool", lambda e: e.dma_start(out=wbd[:], in_=wbd_d), writes=["wbd"], dma_key="c_wbd")
        P.add("pool", lambda e: e.dma_start(out=bs[:], in_=bs_d), writes=["bs"], dma_key="c_bs")
        P.add("pool", lambda e: e.dma_start(out=identb[:], in_=cst_d[:, 0:128]), writes=["identb"], dma_key="c_idb")
        P.add("dve", lambda e: e.memset(onesb[:], 1.0), writes=["onesb"])
        P.add("dve", lambda e: e.memset(onesf[:], 1.0), writes=["onesf"])
        P.add("dve", lambda e: e.memset(dummy[:], 1.0), writes=["dummy0", "dummy1"])
        P.add("dve", lambda e: e.memset(halo[:], 0.0), writes=["halo%d" % ci for ci in range(10)])
        P.add("dve", lambda e: e.memset(hstate[:], 0.0), writes=["hst0", "hst1", "hst2"])
        P.add("dve", lambda e: e.memset(H32[:], 0.0), writes=["H32"])
        for j in range(3):
            ts(neg3[:, j * 128:(j + 1) * 128], Umask, -1.0, 30000.0, ALU.add, ALU.mult, ["cst"], ["neg3"])
        for l in range(L):
            for nm in ("ffn1_post_g", "ffn2_post_g"):
                o = pv_off(l, nm)
                ts(pv[:, o:o + 8], pv[:, o:o + 8], 0.5, None, ALU.mult, None, ["pv"], ["pv"])
            o = pv_off(l, "lru_lambda")
            act(pv[:, o:o + 3], pv[:, o:o + 3], AF.Exp, ["pv"], ["pv"], scale=-1.0)
            act(pv[:, o:o + 3], pv[:, o:o + 3], AF.Ln, ["pv"], ["pv"], bias=1.0)
            ts(pv[:, o:o + 3], pv[:, o:o + 3], -8.0, None, ALU.mult, None, ["pv"], ["pv"])
            o = l * 48 + 24
            act(hb[:, o:o + 24], hb[:, o:o + 24], AF.Exp, ["hb"], ["hb"])
            ts(hb[:, o:o + 24], hb[:, o:o + 24], -1.0, None, ALU.mult, None, ["hb"], ["hb"])
            for g in range(4):
                o = (l * 4 + g) * 128
                P.add("sp", lambda e, o=o, g=g: e.dma_start(out=wst32[:, g * 128:(g + 1) * 128], in_=wst_d[:, o:o + 128]),
                      writes=["wst32_%d" % g], dma_key="c_wst%d" % g)
                tt("dve", wst[:, o:o + 128], wst32[:, g * 128:(g + 1) * 128], Umask, ALU.mult,
                   ["wst32_%d" % g, "cst"], ["wst"])

        def piece_parts(p_):
            l, r_ = divmod(p_, PPL)
            if r_ < 11 or 26 <= r_ < 37:
                f, jp = (0, r_) if r_ < 11 else (1, r_ - 26)
                wg = wgu_d[f][l].rearrange("(k p) c -> p k c", p=128)
                return [(s * 2048, 8, 256, wg[:, :, s * DFF + jp * 256:s * DFF + (jp + 1) * 256]) for s in range(2)]
            if 11 <= r_ < 19 or r_ >= 37:
                f, dp = (0, r_ - 11) if r_ < 19 else (1, r_ - 37)
                wd = wdn_d[f][l].rearrange("(k p) c -> p k c", p=128)
                return [(0, NFC, 128, wd[:, :, dp * 128:(dp + 1) * 128])]
            if 19 <= r_ < 24:
                c0, c1 = IN_PIECES[r_ - 19]
                wi = win_d[l].rearrange("(k p) c -> p k c", p=128)
                return [(0, 8, c1 - c0, wi[:, :, c0:c1])]
            op_ = r_ - 24
            wo = wout_d[l].rearrange("(k p) c -> p k c", p=128)
            return [(0, 8, 512, wo[:, :, op_ * 512:(op_ + 1) * 512])]

        def wload(p_, n):
            i = st["ws"] % NSLOT
            st["ws"] += 1
            if st["first"]:
                for (off, k, c, src) in piece_parts(p_):
                    dstv = wsl[i][:, off:off + k * c].rearrange("p (k c) -> p k c", k=k)
                    P.add("pool", lambda e, dstv=dstv, src=src: e.dma_start(out=dstv, in_=src),
                          writes=["wsl%d" % i], dma_key="wp%d" % i)
                P.add("sp", lambda e: e.dma_start(out=wsc[p_][:, 0:n], in_=wsl[i][:, 0:n]),
                      reads=["wsl%d" % i], writes=["wsc%d" % p_], dma_key="wb%d" % i)
            else:
                P.add("sp", lambda e: e.dma_start(out=wsl[i][:, 0:n], in_=wsc[p_][:, 0:n]),
                      reads=["wsc%d" % p_], writes=["wsl%d" % i], dma_key="w%d" % i)
            return wsl[i], "wsl%d" % i

        def stats_of(src_ap, src_tok, c, n_c):
            i = st["sq"] % 2
            st["sq"] += 1
            act(sq[i][:], src_ap, AF.Square, [src_tok], ["sq%d" % i])
            return lambda: mm(ps_st[:], onesb[:], sq[i][:], c == 0, c == n_c - 1, ["onesb", "sq%d" % i], ["ps_st"])

        def preload(func):
            kw = {"bias": 1.0} if func == AF.Ln else {}
            P.add("act", lambda e: e.activation(out=dummy[:, 1:2], in_=dummy[:, 0:1], func=func, **kw),
                  ["dummy0"], ["dummy1"])

        def finish_rstd(n):
            act(rstd[:], ps_st[:], AF.Ln, ["ps_st"], ["rstd"], bias=EPS, scale=1.0 / n)
            act(rstd[:], rstd[:], AF.Exp, ["rstd"], ["rstd"], scale=-0.5)

        def prenorm(l, gname):
            for c in range(8):
                stats_of(xc(xT, c), "xT%d" % c, c, 8)()
            finish_rstd(D)
            for c in range(8):
                stt(xc(xn, c), xc(xT, c), pvc(l, gname, c), rstd[:], ALU.mult, ALU.mult,
                    ["xT%d" % c, "pv", "rstd"], ["xn%d" % c])

        def prenorm_lazy(l, gname):
            for c in range(8):
                stats_of(xc(xT, c), "xT%d" % c, c, 8)()
                act(xc(xn, c), xc(xT, c), AF.Copy, ["xT%d" % c, "pv"], ["xn%d" % c], scale=pvc(l, gname, c))
            finish_rstd(D)

        def postnorm(l, gname):
            finish_rstd(D)
            for c in range(8):
                tt("dve", xc(fT, c), xc(fT, c), rstd[:], ALU.mult, ["fT%d" % c, "rstd"], ["fT%d" % c])
                stt(xc(xT, c), xc(fT, c), pvc(l, gname, c), xc(xT, c), ALU.mult, ALU.add,
                    ["fT%d" % c, "pv", "xT%d" % c], ["xT%d" % c])

        def out_group(ps, pst, dc, pend):
            act(xc(fT, dc), ps[:], AF.Copy, [pst], ["fT%d" % dc])
            pend.append(stats_of(ps[:], pst, dc, 8))

        def ffn(l, f):
            pre, post = ("ffn1_pre_g", "ffn1_post_g") if f == 0 else ("ffn2_pre_g", "ffn2_post_g")
            P.tag = 'ffn_pre'
            prenorm_lazy(l, pre)
            P.tag = 'ffn_up'
            preload(AF.Silu)
            for jp in range(11):
                w, wt = wload(pid(l, "gu%d" % f, jp), 4096)
                banks = [[nb(), nb()] for fi in range(2)]
                if jp == 0:
                    for kc in range(8):
                        for fi in range(2):
                            for s in range(2):
                                o = s * 2048 + kc * 256 + fi * 128
                                ps, pst = banks[fi][s]
                                mm(ps[:], w[:, o:o + 128], xc(xn, kc), kc == 0, kc == 7, [wt, "xn%d" % kc], [pst])
                else:
                    for fi in range(2):
                        for s in range(2):
                            ps, pst = banks[fi][s]
                            for kc in range(8):
                                o = s * 2048 + kc * 256 + fi * 128
                                mm(ps[:], w[:, o:o + 128], xc(xn, kc), kc == 0, kc == 7, [wt, "xn%d" % kc], [pst])
                for fi in range(2):
                    j = jp * 2 + fi
                    (pg, pgt), (pu, put) = banks[fi]
                    i = st["sg"] % 2
                    st["sg"] += 1
                    tt("dve", sg[i][:], pg[:], rstd[:], ALU.mult, [pgt, "rstd"], ["sg%d" % i])
                    act(sg[i][:], sg[i][:], AF.Silu, ["sg%d" % i], ["sg%d" % i])
                    tt("dve", cacc[i][:], pu[:], rstd[:], ALU.mult, [put, "rstd"], ["cacc%d" % i])
                    tt("dve", hch(j), sg[i][:], cacc[i][:], ALU.mult, ["sg%d" % i, "cacc%d" % i], ["h%d" % j])
            preload(AF.Ln)
            P.tag = 'ffn_down'
            pend = []
            for dc in range(8):
                w, wt = wload(pid(l, "dn%d" % f, dc), NFC * 128)
                pf, pft = nb()
                for j in range(NFC):
                    o = j * 128
                    mm(pf[:], w[:, o:o + 128], hch(j), j == 0, j == NFC - 1, [wt, "h%d" % j], [pft])
                for fn in pend:
                    fn()
                pend = []
                out_group(pf, pft, dc, pend)
            for fn in pend:
                fn()
            P.tag = 'ffn_post'
            postnorm(l, post)

        NRAW = 3

        def evac_scaled(ps, pst):
            i = st["raw"] % NRAW
            st["raw"] += 1
            pl = st["pend2"]
            while any(bi == i for bi, _ in pl):
                pl.pop(0)[1]()
            tt("dve", raw[i][:, 3:T + 3], ps[:], rstd[:], ALU.mult, [pst, "rstd"], ["raw%d" % i])
            return i

        def conv_s1(l, i, ci, wname, bname, wc, out_ap, out_tok):
            r, rt, rh = raw[i], "raw%d" % i, "rawh%d" % i
            ho = (l * 10 + ci) * 3
            cp("act", r[:, 0:3], halo[:, ho:ho + 3], ["halo%d" % ci], [rh, rt])
            act(out_ap, r[:, 3:T + 3], AF.Identity, [rt, "pv"], [out_tok], bias=pvc(l, bname, wc), scale=pvc(l, wname % 3, wc))

            def s2():
                for k in (2, 1, 0):
                    stt(out_ap, r[:, k:T + k], pvc(l, wname % k, wc), out_ap, ALU.mult, ALU.add,
                        [rt, rh, "pv", out_tok], [out_tok])
                cp("dve", halo[:, ho:ho + 3], r[:, T:T + 3], [rt], ["halo%d" % ci])
            return s2

        def mixer(l):
            P.tag = 'mix_pre'
            prenorm_lazy(l, "mix_pre_g")
            pr2, pr2t = nb()
            for q in range(4):
                tr(pr2[:, q * 128:(q + 1) * 128], rstd[:, q * 128:(q + 1) * 128], ident, ["rstd", "cst"], [pr2t])
            P.add("dve", lambda e: e.tensor_copy(out=r2tok[:].unsqueeze(2),
                                                 in_=pr2[:].rearrange("p (q c) -> p q c", q=4)[:, :, 0:1]),
                  [pr2t], ["r2tok"])
            P.tag = 'mix_in'
            pend2 = []
            st["pend2"] = pend2

            def flush(keep):
                while len(pend2) > keep:
                    pend2.pop(0)[1]()
            def sgu_part1():
                p1, p1t = nb()
                p2, p2t = nb()
                for c in range(2):
                    mm(p1[:], onesf[:], xc(v_g, c), c == 0, c == 1, ["onesf", "v%d" % c], [p1t])
                for c in range(2):
                    tt("dve", tmp[c][:], xc(v_g, c), xc(v_g, c), ALU.mult, ["v%d" % c], ["tmp%d" % c])
                    mm(p2[:], onesf[:], tmp[c][:], c == 0, c == 1, ["onesf", "tmp%d" % c], [p2t])
                mean, var = tmp[2], tmp[3]
                act(mean[:], p1[:], AF.Copy, [p1t], ["tmp2"], scale=1.0 / 256)
                tt("dve", var[:], mean[:], mean[:], ALU.mult, ["tmp2"], ["tmp3"])
                stt(var[:], p2[:], 1.0 / 256, var[:], ALU.mult, ALU.subtract, [p2t, "tmp3"], ["tmp3"])
                act(var[:], var[:], AF.Ln, ["tmp3"], ["tmp3"], bias=EPS)
                act(var[:], var[:], AF.Exp, ["tmp3"], ["tmp3"], scale=-0.5)
                act(dt_a[:], dt_a[:], AF.Exp, ["dt_a"], ["dt_a"])
                act(dt_a[:], dt_a[:], AF.Ln, ["dt_a"], ["dt_a"], bias=1.0)
                tt("dve", ad_t[:], dt_a[:], hb[:, l * 48 + 24:l * 48 + 48], ALU.mult, ["dt_a", "hb"], ["ad_t"])
                if dbg_d is not None and l == 0 and st["first"]:
                    P.add("sp", lambda e: e.dma_start(out=dbg_d[:, 0:24], in_=dt_a[:]), reads=["dt_a"], dma_key="dbg")
                    P.add("sp", lambda e: e.dma_start(out=dbg_d[:, 24:28], in_=r2tok[:]), reads=["r2tok"], dma_key="dbg")
                    P.add("sp", lambda e: e.dma_start(out=dbg_d[:, 28:60], in_=rstd[:, 0:32]), reads=["rstd"], dma_key="dbg")
                for c in range(2):
                    tt("dve", tmp[c][:], xc(v_g, c), mean[:], ALU.subtract, ["v%d" % c, "tmp2"], ["tmp%d" % c])
                    tt("dve", tmp[c][:], tmp[c][:], var[:], ALU.mult, ["tmp%d" % c, "tmp3"], ["tmp%d" % c])
                    ts(hch(10 + c), tmp[c][:], pvc(l, "sgu_ln_g", c), pvc(l, "sgu_ln_b", c), ALU.mult, ALU.add,
                       ["tmp%d" % c, "pv"], ["h%d" % (10 + c)])

            def sgu_part2():
                pm = [nb(), nb()]
                pend = []
                for q in range(4):
                    pt, ptt = nbf()
                    for c in range(2):
                        tr(pt[:, c * 128:(c + 1) * 128], hT[:, (10 + c) * T + q * 128:(10 + c) * T + (q + 1) * 128],
                           identb[:], ["h%d" % (10 + c), "identb"], [ptt])
                    vt = vtok[q % 2]
                    cp("act", vt, pt[:, 0:256], [ptt], [tvtok[q % 2]])
                    for fn in pend:
                        fn()
                    pend = []

                    def mixq(q=q, vt=vt):
                        for g in range(4):
                            c, po = g // 2, (g % 2) * 64
                            o = (l * 4 + g) * 128
                            dst = pm[c][0][po:po + 64, q * 128:(q + 1) * 128]
                            mm(dst, vt[:, g * 64:(g + 1) * 64], wst[:, o:o + 128], True, False,
                               [tvtok[q % 2], "wst"], [pm[c][1]])
                            mm(dst, onesb[0:1, 0:64], bs[0:1, o:o + 128], False, True, ["onesb", "bs"], [pm[c][1]])
                    pend.append(mixq)
                for fn in pend:
                    fn()
                for c in range(2):
                    tt("dve", tmp[c][:], xc(u_g, c), pm[c][0][:], ALU.mult, ["u%d" % c, pm[c][1]], ["tmp%d" % c])

            for ip in (4, 3, 2, 1, 0):
                if ip == 2:
                    P.tag = 'sgu'
                    sgu_part2()
                    P.tag = 'mix_in'
                if ip == 3 and TOG_SGU:
                    flush(0)
                    sgu_part1()
                c0, c1 = IN_PIECES[ip]
                n = c1 - c0
                w, wt = wload(pid(l, "in", ip), 8 * n)
                for (kind, c, col0, width) in IN_CHUNKS:
                    if not (c0 <= col0 < c1):
                        continue
                    lo = col0 - c0
                    if kind == "dt":
                        pdt, pdtt = nb()
                        for q in range(4):
                            for kc in range(8):
                                mm(pdt[:, q * 6:(q + 1) * 6], xn[:, kc * T + q * 128:kc * T + (q + 1) * 128],
                                   w[:, kc * n + lo:kc * n + lo + 6], kc == 0, kc == 7, [wt, "xn%d" % kc], [pdtt])
                        P.add("dve", lambda e, pdt=pdt: e.tensor_tensor(
                            out=dt_a[:].rearrange("p (q h) -> p q h", q=4),
                            in0=pdt[:, 0:24].rearrange("p (q h) -> p q h", q=4),
                            in1=r2tok[:].unsqueeze(2).to_broadcast([128, 4, 6]), op=ALU.mult),
                            [pdtt, "r2tok"], ["dt_a"])
                        tt("dve", dt_a[:], dt_a[:], hb[:, l * 48:l * 48 + 24], ALU.add, ["dt_a", "hb"], ["dt_a"])
                        continue
                    ps, pst = nb()
                    for kc in range(8):
                        mm(ps[:], w[:, kc * n + lo:kc * n + lo + 128], xc(xn, kc), kc == 0, kc == 7,
                           [wt, "xn%d" % kc], [pst])
                    ri = evac_scaled(ps, pst)
                    rsrc, rtk = raw[ri][:, 3:T + 3], "raw%d" % ri
                    if kind == "gate":
                        act(xc(gate_g, c), rsrc, AF.Gelu, [rtk], ["gate%d" % c])
                    elif kind == "z":
                        act(xc(zs, c), rsrc, AF.Silu, [rtk], ["zs%d" % c])
                    elif kind == "u":
                        act(xc(u_g, c), rsrc, AF.Gelu, [rtk], ["u%d" % c])
                    elif kind == "v":
                        act(xc(v_g, c), rsrc, AF.Gelu, [rtk], ["v%d" % c])
                    elif kind == "rec":
                        s2 = conv_s1(l, ri, c, "lru_cw%d", "lru_conv_b", c, xc(rec32, c), "rec%d" % c)

                        def s2rec(s2=s2, c=c):
                            s2()
                            cp("act", hch(c), xc(rec32, c), ["rec%d" % c], ["h%d" % c])
                        flush(1)
                        pend2.append((ri, s2rec))
                        continue
                    elif kind == "xbc":
                        i = st["cacc"] % NRAW
                        st["cacc"] += 1
                        s2 = conv_s1(l, ri, 3 + c, "ssd_cw%d", "ssd_conv_b", c, cacc[i][:], "cacc%d" % i)

                        def s2x(s2=s2, c=c, i=i):
                            s2()
                            act(hch(3 + c), cacc[i][:], AF.Silu, ["cacc%d" % i], ["h%d" % (3 + c)])
                        flush(1)
                        pend2.append((ri, s2x))
                        continue
                    flush(1)
            flush(0)
            if not TOG_SGU:
                sgu_part1()

            P.tag = 'sgu'
            for c in range(2):
                cp("act", xc(xn, 6 + c), tmp[c][:], ["tmp%d" % c], ["xn%d" % (6 + c)])

            lru = []
            A_ = [tmp[0], tmp[1], tmp[2]]
            tA = ["tmp0", "tmp1", "tmp2"]
            B_ = [cacc[0][:], cacc[1][:], cacc[2][:]]
            tB = ["cacc0", "cacc1", "cacc2"]
            C_ = [tmp[3][:], tmp[4][:], rstd[:]]
            tC = ["tmp3", "tmp4", "rstd"]

            def L(fn):
                lru.append(fn)
            for c in range(3):
                def gates(c=c):
                    pr, prt = nb()
                    pi_, pit = nb()
                    o = ((l * 2 + 0) * 3 + c) * 128
                    mm(pr[:], wbd[:, o:o + 128], hch(c), True, True, ["wbd", "h%d" % c], [prt])
                    o = ((l * 2 + 1) * 3 + c) * 128
                    mm(pi_[:], wbd[:, o:o + 128], hch(c), True, True, ["wbd", "h%d" % c], [pit])
                    act(A_[c][:], pr[:], AF.Sigmoid, [prt, "pv"], [tA[c]], bias=pvc(l, "lru_b_r", c))
                    act(B_[c], pi_[:], AF.Sigmoid, [pit, "pv"], [tB[c]], bias=pvc(l, "lru_b_i", c))
                L(gates)
            for c in range(3):
                L(lambda c=c: act(A_[c][:], A_[c][:], AF.Exp, [tA[c], "pv"], [tA[c]], scale=pvc(l, "lru_lambda", c)))
                L(lambda c=c: tt("dve", C_[c], A_[c][:], A_[c][:], ALU.mult, [tA[c]], [tC[c]]))
                L(lambda c=c: tt("dve", B_[c], B_[c], xc(rec32, c), ALU.mult, [tB[c], "rec%d" % c], [tB[c]]))
            for c in range(3):
                L(lambda c=c: act(C_[c], C_[c], AF.Ln, [tC[c]], [tC[c]], bias=1.0, scale=-1.0))
            for c in range(3):
                L(lambda c=c: act(C_[c], C_[c], AF.Exp, [tC[c]], [tC[c]], scale=0.5))
                L(lambda c=c: tt("dve", B_[c], B_[c], C_[c], ALU.mult, [tB[c], tC[c]], [tB[c]]))
            for c in range(3):
                def scan(c=c):
                    hs = hstate[:, l * 3 + c:l * 3 + c + 1]
                    P.add("dve", lambda e: e.tensor_tensor_scan(
                        out=C_[c], data0=A_[c][:], data1=B_[c], initial=hs, op0=ALU.mult, op1=ALU.add),
                        [tA[c], tB[c], "hst%d" % c], [tC[c]])
                    cp(st["pool"], hs, C_[c][:, T - 1:T], [tC[c]], ["hst%d" % c])
                    tt("dve", xc(xn, c), C_[c], xc(gate_g, c), ALU.mult, [tC[c], "gate%d" % c], ["xn%d" % c])
                L(scan)

            def lru_some(k):
                for _ in range(k):
                    if lru:
                        lru.pop(0)()

            P.tag = 'ssd'
            lru_some(3)
            cp("act", Hbf[0], H32[:, l * 384:(l + 1) * 384], ["H32"], ["h19"])
            pe2 = st["pool"]

            def mk_rhsU(k):
                kb, a0 = k % 2, (k // 2) * 6 + (k % 2) * 3
                P.add("dve", lambda e: e.tensor_tensor(
                    out=rhsU[kb][:].rearrange("p (j l) -> p j l", j=3),
                    in0=Umask.unsqueeze(1).to_broadcast([128, 3, 128]),
                    in1=ad_t[:, a0:a0 + 3].unsqueeze(2).to_broadcast([128, 3, 128]), op=ALU.mult),
                    ["cst", "ad_t"], ["rhsU%d" % kb])
            mk_rhsU(0)
            for q in range(4):
                for g in range(2):
                    k = q * 2 + g
                    kb = k % 2
                    a0 = q * 6 + g * 3
                    v3 = lambda ap: ap.rearrange("p (j l) -> p j l", j=3)
                    pD, pDt = nb()
                    mm(pD[:, 0:384], Lstr, rhsU[kb][:], True, False, ["cst", "rhsU%d" % kb], [pDt])
                    mm(pD[:, 0:384], ident, neg3[:], False, True, ["cst", "neg3"], [pDt])
                    pC, pCt = nb()
                    mm(pC[:, 0:384], onesf[:], rhsU[kb][:], True, True, ["onesf", "rhsU%d" % kb], [pCt])
                    pcb, pcbt = nb()
                    mm(pcb[:, 0:128], hT[:, (6 + g) * T + q * 128:(6 + g) * T + (q + 1) * 128],
                       hT[:, (8 + g) * T + q * 128:(8 + g) * T + (q + 1) * 128], True, True,
                       ["h%d" % (6 + g), "h%d" % (8 + g)], [pcbt])
                    if k < 7:
                        mk_rhsU(k + 1)
                    act(Eb[kb][:], pD[:, 0:384], AF.Exp, [pDt], ["E%d" % kb])
                    act(E2b[kb][:], pC[:, 0:384], AF.Exp, [pCt], ["E2%d" % kb])
                    P.add("dve", lambda e, kb=kb, k=k: e.tensor_copy(
                        out=wcol[:, k * 3:(k + 1) * 3].unsqueeze(2),
                        in_=Eb[kb][:].rearrange("p (j l) -> p j l", j=3)[:, :, 127:128]),
                        ["E%d" % kb], ["wcol"])
                    P.add("dve", lambda e, kb=kb, k=k: e.tensor_copy(
                        out=dcol[:, k * 3:(k + 1) * 3].unsqueeze(2),
                        in_=E2b[kb][:].rearrange("p (j l) -> p j l", j=3)[:, :, 127:128]),
                        ["E2%d" % kb], ["dcol"])
                    P.add("dve", lambda e, kb=kb, k=k, pcb=pcb: e.tensor_tensor(
                        out=MTall[:, k * 384:(k + 1) * 384].rearrange("p (j l) -> p j l", j=3),
                        in0=Eb[kb][:].rearrange("p (j l) -> p j l", j=3),
                        in1=pcb[:, 0:128].unsqueeze(1).to_broadcast([128, 3, 128]), op=ALU.mult),
                        ["E%d" % kb, pcbt], ["MT%d" % k])
                    P.add(pe2, lambda e, kb=kb, k=k, g=g, q=q: e.tensor_tensor(
                        out=CsTall[:, k * 384:(k + 1) * 384].rearrange("p (j l) -> p j l", j=3),
                        in0=E2b[kb][:].rearrange("p (j l) -> p j l", j=3),
                        in1=hT[:, (8 + g) * T + q * 128:(8 + g) * T + (q + 1) * 128].unsqueeze(1).to_broadcast([128, 3, 128]),
                        op=ALU.mult),
                        ["E2%d" % kb, "h%d" % (8 + g)], ["CsT%d" % k])
                if q % 2 == 1:
                    lru_some(3)
            xdtB = [xdt, hT[:, 12 * T:12 * T + 384]]
            xdtwB = [xdtw, hT[:, 13 * T:13 * T + 384]]
            BtokB = [Btok, hT[:, 14 * T:14 * T + 256]]
            dtwB = [dtw, dtw2]
            txdt, txdtw, tbtok, tdtw = ["h16", "h12"], ["h17", "h13"], ["h18", "h14"], ["dtw", "dtw2"]

            def prep(q):
                pb_ = q % 2
                pt, ptt = nbf()
                for c in range(3):
                    tr(pt[:, c * 128:(c + 1) * 128], hT[:, (3 + c) * T + q * 128:(3 + c) * T + (q + 1) * 128],
                       identb[:], ["h%d" % (3 + c), "identb"], [ptt])
                for g in range(2):
                    tr(pt[:, 384 + g * 128:384 + (g + 1) * 128],
                       hT[:, (6 + g) * T + q * 128:(6 + g) * T + (q + 1) * 128], identb[:],
                       ["h%d" % (6 + g), "identb"], [ptt])
                P.add("dve", lambda e: e.tensor_tensor(
                    out=xdtB[pb_].rearrange("p (h d) -> p h d", h=6),
                    in0=pt[:, 0:384].rearrange("p (h d) -> p h d", h=6),
                    in1=dt_a[:, q * 6:(q + 1) * 6].unsqueeze(2).to_broadcast([128, 6, 64]), op=ALU.mult),
                    [ptt, "dt_a"], [txdt[pb_]])
                tt("dve", dtwB[pb_][:], dt_a[:, q * 6:(q + 1) * 6], wcol[:, q * 6:(q + 1) * 6], ALU.mult,
                   ["dt_a", "wcol"], [tdtw[pb_]])
                P.add("dve", lambda e: e.tensor_tensor(
                    out=xdtwB[pb_].rearrange("p (h d) -> p h d", h=6),
                    in0=pt[:, 0:384].rearrange("p (h d) -> p h d", h=6),
                    in1=dtwB[pb_][:].unsqueeze(2).to_broadcast([128, 6, 64]), op=ALU.mult),
                    [ptt, tdtw[pb_]], [txdtw[pb_]])
                cp("dve", BtokB[pb_], pt[:, 384:640], [ptt], [tbtok[pb_]])
            prep(0)
            for q in range(4):
                par = q % 2
                if q < 3:
                    prep(q + 1)
                xdt_, xdtw_, Btok_ = xdtB[par], xdtwB[par], BtokB[par]
                py, pyt = nb()
                for h in range(6):
                    g, j, cc, po = h // 3, h % 3, h // 2, (h % 2) * 64
                    k = q * 2 + g
                    dst = py[po:po + 64, cc * 128:(cc + 1) * 128]
                    mm(dst, xdt_[:, h * 64:(h + 1) * 64], MTall[:, k * 384 + j * 128:k * 384 + (j + 1) * 128], True, False,
                       [txdt[par], "MT%d" % k], [pyt])
                    mm(dst, Hbf[par][:, h * 64:(h + 1) * 64], CsTall[:, k * 384 + j * 128:k * 384 + (j + 1) * 128],
                       False, True, ["h%d" % (19 + par), "CsT%d" % k], [pyt])
                for cc in range(3):
                    stt(yT2[:, cc * T + q * 128:cc * T + (q + 1) * 128],
                        hT[:, (3 + cc) * T + q * 128:(3 + cc) * T + (q + 1) * 128], pvc(l, "ssd_dvec", cc),
                        py[:, cc * 128:(cc + 1) * 128], ALU.mult, ALU.add,
                        ["h%d" % (3 + cc), "pv", pyt], ["yS%d" % cc])
                pS, pSt = nb()
                for g in range(2):
                    mm(pS[:, g * 192:(g + 1) * 192], Btok_[:, g * 128:(g + 1) * 128], xdtw_[:, g * 192:(g + 1) * 192],
                       True, True, [tbtok[par], txdtw[par]], [pSt])
                P.add(pe2, lambda e, q=q: e.tensor_tensor(
                    out=H32[:, l * 384:(l + 1) * 384].rearrange("p (h d) -> p h d", h=6),
                    in0=H32[:, l * 384:(l + 1) * 384].rearrange("p (h d) -> p h d", h=6),
                    in1=dcol[:, q * 6:(q + 1) * 6].unsqueeze(2).to_broadcast([128, 6, 64]),
                    op=ALU.mult), ["H32", "dcol"], ["H32"])
                tt("dve", H32[:, l * 384:(l + 1) * 384], H32[:, l * 384:(l + 1) * 384], pS[:, 0:384], ALU.add,
                   ["H32", pSt], ["H32"])
                if q < 3:
                    cp("act", Hbf[1 - par], H32[:, l * 384:(l + 1) * 384], ["H32"], ["h%d" % (19 + 1 - par)])
                lru_some(3)
            lru_some(100)
            P.tag = 'mix_out'
            pend = []
            korder = (6, 7, 0, 1, 2, 3, 4, 5)
            w0, wt0 = wload(pid(l, "out", 0), 4096)
            bks = [nb() for di in range(4)]
            for ki, kc in enumerate(korder[:5]):
                for di in range(4):
                    o = kc * 512 + di * 128
                    mm(bks[di][0][:], w0[:, o:o + 128], xc(xn, kc), ki == 0, False, [wt0, "xn%d" % kc], [bks[di][1]])
            P.tag = 'ssd_norm'
            for cc in range(3):
                tt("dve", xc(yT2, cc), xc(yT2, cc), xc(zs, cc), ALU.mult, ["yS%d" % cc, "zs%d" % cc], ["yS%d" % cc])
                stats_of(xc(yT2, cc), "yS%d" % cc, cc, 3)()
            finish_rstd(384)
            for cc in range(3):
                stt(xc(xn, 3 + cc), xc(yT2, cc), pvc(l, "ssd_norm_g", cc), rstd[:], ALU.mult, ALU.mult,
                    ["yS%d" % cc, "pv", "rstd"], ["xn%d" % (3 + cc)])
            P.tag = 'mix_out'
            for ki, kc in enumerate(korder[5:]):
                for di in range(4):
                    o = kc * 512 + di * 128
                    mm(bks[di][0][:], w0[:, o:o + 128], xc(xn, kc), False, ki == 2, [wt0, "xn%d" % kc], [bks[di][1]])
            for di in range(4):
                for fn in pend:
                    fn()
                del pend[:]
                out_group(bks[di][0], bks[di][1], di, pend)
            w, wt = wload(pid(l, "out", 1), 4096)
            for di in range(4):
                dc = 4 + di
                pf, pft = nb()
                for ki, kc in enumerate(korder):
                    o = kc * 512 + di * 128
                    mm(pf[:], w[:, o:o + 128], xc(xn, kc), ki == 0, ki == 7, [wt, "xn%d" % kc], [pft])
                for fn in pend:
                    fn()
                pend = []
                out_group(pf, pft, dc, pend)
            for fn in pend:
                fn()
            P.tag = 'mix_post'
            postnorm(l, "mix_post_g")

        fall = ["fT%d" % c for c in range(8)]
        for ti in range(n_tiles):
            P.tag = 'io_in'
            src = x_d[ti * T:(ti + 1) * T, :].rearrange("(q p) d -> p q d", p=128)
            P.add("sp", lambda e, src=src: e.dma_start(out=fT[:].rearrange("p (q d) -> p q d", q=4), in_=src),
                  writes=fall, dma_key="xin")
            for c in range(8):
                ps, pst = nb()
                for q in range(4):
                    tr(ps[:, q * 128:(q + 1) * 128], fT[:, q * D + c * 128:q * D + (c + 1) * 128], ident,
                       fall + ["cst"], [pst])
                cp("dve" if c % 2 else "act", xc(xT, c), ps[:], [pst], ["xT%d" % c])
            st["pool"] = "dve" if ti == 0 else "pool"
            st["first"] = (ti == 0)
            for l in range(L):
                ffn(l, 0)
                mixer(l)
                ffn(l, 1)
            P.tag = 'io_out'
            for q in range(4):
                for hf in range(2):
                    ps, pst = nb()
                    for c4 in range(4):
                        c = hf * 4 + c4
                        tr(ps[:, c4 * 128:(c4 + 1) * 128], xT[:, c * T + q * 128:c * T + (q + 1) * 128], ident,
                           ["xT%d" % c, "cst"], [pst])
                    cp("dve" if hf else "act", fT[:, q * D + hf * 512:q * D + (hf + 1) * 512], ps[:], [pst], fall)
            dst = out_d[ti * T:(ti + 1) * T, :].rearrange("(q p) d -> p q d", p=128)
            P.add("sp", lambda e, dst=dst: e.dma_start(out=dst, in_=fT[:].rearrange("p (q d) -> p q d", q=4)),
                  reads=fall, dma_key="xout")

        stats = P.emit(final_dma_keys=["xout"])
    build.prog = P
    return nc, stats


N_CORES = 8
SEQ = 4096
DEPTH = 4


def kernel(**inputs):
    L = DEPTH
    x = np.asarray(inputs["x"], np.float32)
    B = x.shape[0]
    n_tiles = x.shape[1] // T
    nc, _ = build(n_tiles, L)
    shared = prep_params(inputs, L)
    for nm in ("ffn1_w_gu", "ffn2_w_gu", "ffn1_w_down", "ffn2_w_down", "mix_w_in", "mix_w_out"):
        shared[nm] = np.ascontiguousarray(np.asarray(inputs[nm], np.float32))
    in_maps = []
    for b in range(B):
        m = dict(shared)
        m["x"] = np.ascontiguousarray(x[b])
        in_maps.append(m)
    res = run_bass_kernel_spmd(nc, in_maps, core_ids=list(range(B)))
    return np.stack([np.asarray(r["out"], np.float32) for r in res.results], axis=0)
```

```python
import numpy as np
from contextlib import ExitStack
import concourse.bass as bass
import concourse.mybir as mybir
from concourse.bass_utils import run_bass_kernel_spmd

F32 = mybir.dt.float32
BF16 = mybir.dt.bfloat16
AF = mybir.ActivationFunctionType
ALU = mybir.AluOpType

D = 1024
DFF = 2816
NFC = 22
INC = 2566
T = 512
EPS = 1e-6
SLOT = 4160
NSLOT = 4
import os
TOG_OUT = os.environ.get('TOG_OUT', '1') == '1'
TOG_SGU = os.environ.get('TOG_SGU', '1') == '1'


class Op:
    __slots__ = ("eng", "fn", "is_dma", "key", "waits_dma", "deps", "sig", "idx", "tag")


class Prog:
    ENGS = ("pe", "act", "dve", "pool", "sp")

    def __init__(self, nc):
        self.nc = nc
        self.ops = []
        self.last_w = {}
        self.readers = {}
        self.dma_count = {}
        self.dma_keys = []
        self.tag = ""

    def add(self, eng, fn, reads=(), writes=(), dma_key=None):
        op = Op()
        op.eng = eng
        op.fn = fn
        op.is_dma = dma_key is not None
        op.key = dma_key
        op.idx = len(self.ops)
        op.sig = 0
        op.tag = self.tag
        deps = {}
        wd = {}

        def dep(d, same_ok):
            if d is None or d is op:
                return
            if d.is_dma:
                wd[d.key] = self.dma_count[d.key]
            else:
                if d.eng == eng and same_ok and not op.is_dma and eng == "pe":
                    return
                deps[d.idx] = d

        for t in reads:
            dep(self.last_w.get(t), False)
        for t in writes:
            dep(self.last_w.get(t), True)
            for r in self.readers.get(t, ()):
                dep(r, True)
        op.deps = list(deps.values())
        op.waits_dma = wd
        for d in op.deps:
            d.sig = 1
        for t in reads:
            self.readers.setdefault(t, []).append(op)
        for t in writes:
            self.last_w[t] = op
            self.readers[t] = []
        if op.is_dma:
            if dma_key not in self.dma_count:
                self.dma_count[dma_key] = 0
                self.dma_keys.append(dma_key)
            self.dma_count[dma_key] += 16
        self.ops.append(op)
        return op

    def emit(self, final_dma_keys=()):
        nc = self.nc
        with ExitStack() as es:
            esem = {e: es.enter_context(nc.semaphore("s_" + e)) for e in self.ENGS}
            dsem = {k: es.enter_context(nc.semaphore("d_%d" % i)) for i, k in enumerate(self.dma_keys)}
            cnt = {e: 0 for e in self.ENGS}
            for op in self.ops:
                if (not op.is_dma) and op.sig:
                    cnt[op.eng] += 1
                    op.sig = cnt[op.eng]
            per = {e: [o for o in self.ops if o.eng == e] for e in self.ENGS}
            stats = {e: [len(per[e]), 0] for e in self.ENGS}
            block = es.enter_context(nc.Block())

            def run(engname, engobj):
                waited = {}
                for op in per[engname]:
                    for d in op.deps:
                        k = ("e", d.eng)
                        if waited.get(k, 0) < d.sig:
                            engobj.wait_ge(esem[d.eng], d.sig)
                            waited[k] = d.sig
                            stats[engname][1] += 1
                    for key, val in op.waits_dma.items():
                        k = ("d", key)
                        if waited.get(k, 0) < val:
                            engobj.wait_ge(dsem[key], val)
                            waited[k] = val
                            stats[engname][1] += 1
                    ins = op.fn(engobj)
                    if op.is_dma:
                        ins.then_inc(dsem[op.key], 16)
                    elif op.sig:
                        ins.then_inc(esem[engname], 1)
                if engname == "sp":
                    for key in final_dma_keys:
                        engobj.wait_ge(dsem[key], self.dma_count[key])

            @block.tensor
            def _(e):
                run("pe", e)

            @block.scalar
            def _(e):
                run("act", e)

            @block.vector
            def _(e):
                run("dve", e)

            @block.gpsimd
            def _(e):
                run("pool", e)

            @block.sync
            def _(e):
                run("sp", e)
        return stats


PV_SPEC = [
    ("ffn1_pre_g", 8), ("ffn1_post_g", 8), ("mix_pre_g", 8), ("mix_post_g", 8),
    ("ffn2_pre_g", 8), ("ffn2_post_g", 8),
    ("lru_cw0", 3), ("lru_cw1", 3), ("lru_cw2", 3), ("lru_cw3", 3), ("lru_conv_b", 3),
    ("lru_b_r", 3), ("lru_b_i", 3), ("lru_lambda", 3),
    ("ssd_cw0", 7), ("ssd_cw1", 7), ("ssd_cw2", 7), ("ssd_cw3", 7), ("ssd_conv_b", 7),
    ("ssd_norm_g", 3), ("ssd_dvec", 3), ("sgu_ln_g", 2), ("sgu_ln_b", 2),
]
PV_PER_LAYER = sum(n for _, n in PV_SPEC)


def pv_off(l, name):
    o = l * PV_PER_LAYER
    for nm, n in PV_SPEC:
        if nm == name:
            return o
        o += n
    raise KeyError(name)


def _colmajor(v):
    v = np.asarray(v, np.float32)
    return np.ascontiguousarray(v.reshape(-1, 128).T)


def prep_params(inp, L):
    pv = np.zeros((128, L * PV_PER_LAYER), np.float32)
    for l in range(L):
        vecs = {
            "ffn1_pre_g": inp["ffn1_pre_g"][l], "ffn1_post_g": inp["ffn1_post_g"][l],
            "mix_pre_g": inp["mix_pre_g"][l], "mix_post_g": inp["mix_post_g"][l],
            "ffn2_pre_g": inp["ffn2_pre_g"][l], "ffn2_post_g": inp["ffn2_post_g"][l],
            "lru_conv_b": inp["lru_conv_b"][l], "lru_b_r": inp["lru_b_r"][l],
            "lru_b_i": inp["lru_b_i"][l], "lru_lambda": inp["lru_lambda"][l],
            "ssd_conv_b": inp["ssd_conv_b"][l], "ssd_norm_g": inp["ssd_norm_g"][l],
            "ssd_dvec": np.repeat(np.asarray(inp["ssd_d"][l]), 64),
            "sgu_ln_g": inp["sgu_ln_g"][l], "sgu_ln_b": inp["sgu_ln_b"][l],
        }
        for k in range(4):
            vecs["lru_cw%d" % k] = inp["lru_conv_w"][l][k]
            vecs["ssd_cw%d" % k] = inp["ssd_conv_w"][l][k]
        for nm, n in PV_SPEC:
            o = pv_off(l, nm)
            pv[:, o:o + n] = _colmajor(vecs[nm])
    hb = np.zeros((128, L * 48), np.float32)
    for l in range(L):
        hb[:, l * 48:l * 48 + 24] = np.tile(np.asarray(inp["ssd_dt_bias"][l], np.float32), 4)[None, :]
        hb[:, l * 48 + 24:l * 48 + 48] = np.tile(np.asarray(inp["ssd_a_log"][l], np.float32), 4)[None, :]
    wbd = np.zeros((128, L * 6 * 128), np.float32)
    for l in range(L):
        for gi, nm in enumerate(("lru_w_r", "lru_w_i")):
            w = np.asarray(inp[nm][l], np.float32)
            for c in range(3):
                o = ((l * 2 + gi) * 3 + c) * 128
                for hh in range(2):
                    wbd[hh * 64:(hh + 1) * 64, o + hh * 64:o + (hh + 1) * 64] = w[2 * c + hh]
    ws = np.asarray(inp["sgu_w_s"], np.float32)[:L]
    wst = np.ascontiguousarray(ws.transpose(3, 0, 1, 2).reshape(128, L * 4 * 128))
    bs = np.ascontiguousarray(np.asarray(inp["sgu_b_s"], np.float32)[:L].reshape(1, L * 4 * 128))
    k = np.arange(128)
    ident = np.eye(128, dtype=np.float32)
    U = (k[:, None] <= k[None, :]).astype(np.float32)
    Ls = (k[:, None] > k[None, :]).astype(np.float32)
    cst = np.ascontiguousarray(np.concatenate([ident, U, Ls], axis=1))
    return {"pv": pv, "hb": hb, "wbd": wbd, "wst": wst, "bs": bs, "cst": cst}


IN_CHUNKS = ([("rec", c, 384 + c * 128, 128) for c in range(3)] +
             [("gate", c, c * 128, 128) for c in range(3)] +
             [("z", c, 768 + c * 128, 128) for c in range(3)] +
             [("xbc", c, 1152 + c * 128, 128) for c in range(7)] +
             [("dt", 0, 2048, 6)] +
             [("u", c, 2054 + c * 128, 128) for c in range(2)] +
             [("v", c, 2310 + c * 128, 128) for c in range(2)])
IN_PIECES = [(0, 512), (512, 1024), (1024, 1536), (1536, 2048), (2048, 2566)]


def build(n_tiles, L):
    NTOK = n_tiles * T
    nc = bass.Bass("TRN2", target_bir_lowering=False)

    def din(name, shape, dt=F32):
        return nc.dram_tensor(name, shape, dt, kind="ExternalInput").ap()

    x_d = din("x", [NTOK, D])
    wgu_d = [din("ffn1_w_gu", [L, D, 2 * DFF]), din("ffn2_w_gu", [L, D, 2 * DFF])]
    wdn_d = [din("ffn1_w_down", [L, DFF, D]), din("ffn2_w_down", [L, DFF, D])]
    win_d = din("mix_w_in", [L, D, INC])
    wout_d = din("mix_w_out", [L, D, D])
    pv_d = din("pv", [128, L * PV_PER_LAYER])
    hb_d = din("hb", [128, L * 48])
    wbd_d = din("wbd", [128, L * 6 * 128])
    wst_d = din("wst", [128, L * 4 * 128])
    bs_d = din("bs", [1, L * 4 * 128])
    cst_d = din("cst", [128, 3 * 128])
    out_d = nc.dram_tensor("out", [NTOK, D], F32, kind="ExternalOutput").ap()
    dbg_d = nc.dram_tensor("dbg", [128, 64], F32, kind="ExternalOutput").ap() if os.environ.get("KDBG") else None

    PPL = 45
    wsc = nc.dram_tensor("wsc", [L * PPL, 128, SLOT], BF16).ap()

    def pid(l, sub, i):
        base = {"gu0": 0, "dn0": 11, "in": 19, "out": 24, "gu1": 26, "dn1": 37}[sub]
        return l * PPL + base + i

    P = Prog(nc)
    es = ExitStack()
    with es:
        def sb(name, n, dt=F32, parts=128):
            return es.enter_context(nc.sbuf_tensor(name, [parts, n], dt))

        def psum(name, n, dt=F32):
            return es.enter_context(nc.psum_tensor(name, [128, n], dt))

        xT = sb("xT", 8 * T)
        xn = sb("xn", 8 * T, BF16)
        fT = sb("fT", 8 * T)
        hT = sb("hT", NFC * T, BF16)
        wsl = [sb("wsl%d" % i, SLOT, BF16) for i in range(NSLOT)]
        sq = [sb("sq%d" % i, T, BF16) for i in range(2)]
        rstd = sb("rstd", T)
        sg = [sb("sg%d" % i, T) for i in range(2)]
        pv = sb("pv_s", L * PV_PER_LAYER)
        hb = sb("hb_s", L * 48)
        wbd = sb("wbd_s", L * 6 * 128, BF16)
        wst32 = sb("wst32", 4 * 128)
        wst = sb("wst_s", L * 4 * 128, BF16)
        bs = sb("bs_s", L * 4 * 128, BF16, parts=1)
        cst = sb("cst_s", 3 * 128)
        identb = sb("identb", 128, BF16)
        onesb = sb("onesb", 128, BF16)
        onesf = sb("onesf", 128)
        halo = sb("halo", L * 10 * 3)
        hstate = sb("hstate", L * 3)
        H32 = sb("H32", L * 384)
        gate_g = sb("gate_g", 3 * T)
        rec32 = sb("rec32", 3 * T)
        zs = sb("zs", 3 * T)
        raw = [sb("raw%d" % i, T + 3) for i in range(3)]
        cacc = [sb("cacc%d" % i, T) for i in range(3)]
        u_g = sb("u_g", 2 * T)
        v_g = sb("v_g", 2 * T)
        tmp = [sb("tmp%d" % i, T) for i in range(5)]
        rhsU = [sb("rhsU%d" % i, 384) for i in range(2)]
        Eb = [sb("E%d" % i, 384) for i in range(2)]
        E2b = [sb("E2%d" % i, 384) for i in range(2)]
        dt_a = sb("dt_a", 24)
        ad_t = sb("ad_t", 24)
        dtw = sb("dtw", 6)
        r2tok = sb("r2tok", 4)
        MTall = sb("MTall", 8 * 384, BF16)
        CsTall = sb("CsTall", 8 * 384, BF16)
        wcol = sb("wcol", 24)
        neg3 = sb("neg3", 384)
        dtw2 = sb("dtw2", 6)
        dcol = sb("dcol", 24)
        dummy = sb("dummy_act", 2)
        yT = sb("yT", 3 * T)
        yT2 = yT
        ident = cst[:, 0:128]
        Umask = cst[:, 128:256]
        Lstr = cst[:, 256:384]

        def hch(j, n=T, o=0):
            return hT[:, j * T + o:j * T + o + n]
        MT = [hT[:, 12 * T + g * 384:12 * T + (g + 1) * 384] for g in range(2)]
        CsT = [hT[:, 14 * T + g * 384:14 * T + (g + 1) * 384] for g in range(2)]
        xdt = hT[:, 16 * T:16 * T + 384]
        xdtw = hT[:, 17 * T:17 * T + 384]
        Btok = hT[:, 18 * T:18 * T + 256]
        Hbf = [hT[:, 19 * T:19 * T + 384], hT[:, 20 * T:20 * T + 384]]
        vtok = [hT[:, 21 * T:21 * T + 256], hT[:, 18 * T + 256:18 * T + 512]]
        tvtok = ["h21", "h18"]
        tMT = [["h12"], ["h12", "h13"]]
        tCsT = [["h14"], ["h14", "h15"]]

        NPB = 5
        pbank = [psum("pb%d" % i, 512) for i in range(NPB)]
        ps_st = psum("ps_st", 512)
        pbf = [psum("pbf%d" % i, 1024, BF16) for i in range(2)]
        st = {"pb": 0, "pbf": 0, "ws": 0, "sq": 0, "sg": 0, "raw": 0, "cacc": 0, "pool": "pool", "first": True}

        def nb():
            i = st["pb"] % NPB
            st["pb"] += 1
            return pbank[i], "pb%d" % i

        def nbf():
            i = st["pbf"] % 2
            st["pbf"] += 1
            return pbf[i], "pbf%d" % i

        def mm(out, lhsT, rhs, start, stop, reads, writes):
            P.add("pe", lambda e: e.matmul(out, lhsT=lhsT, rhs=rhs, start=start, stop=stop), reads, writes)

        def tr(out, in_, idn, reads, writes):
            P.add("pe", lambda e: e.transpose(out, in_, idn), reads, writes)

        def act(out, in_, func, reads, writes, bias=None, scale=None):
            kw = {}
            if bias is not None:
                kw["bias"] = bias
            if scale is not None:
                kw["scale"] = scale
            P.add("act", lambda e: e.activation(out=out, in_=in_, func=func, **kw), reads, writes)

        def tt(eng, out, in0, in1, op, reads, writes):
            P.add(eng, lambda e: e.tensor_tensor(out=out, in0=in0, in1=in1, op=op), reads, writes)

        def stt(out, in0, scalar, in1, op0, op1, reads, writes):
            P.add("dve", lambda e: e.scalar_tensor_tensor(out=out, in0=in0, scalar=scalar, in1=in1, op0=op0, op1=op1), reads, writes)

        def ts(out, in0, s1, s2, op0, op1, reads, writes):
            if s2 is None:
                P.add("dve", lambda e: e.tensor_scalar(out=out, in0=in0, scalar1=s1, scalar2=None, op0=op0), reads, writes)
            else:
                P.add("dve", lambda e: e.tensor_scalar(out=out, in0=in0, scalar1=s1, scalar2=s2, op0=op0, op1=op1), reads, writes)

        def cp(eng, out, in_, reads, writes):
            if eng == "act":
                act(out, in_, AF.Copy, reads, writes)
            else:
                P.add(eng, lambda e: e.tensor_copy(out=out, in_=in_), reads, writes)

        def xc(buf, c, n=T):
            return buf[:, c * n:(c + 1) * n]

        def pvc(l, name, c):
            o = pv_off(l, name) + c
            return pv[:, o:o + 1]

        P.add("sp", lambda e: e.dma_start(out=cst[:], in_=cst_d), writes=["cst"], dma_key="c_cst")
        P.add("sp", lambda e: e.dma_start(out=pv[:], in_=pv_d), writes=["pv"], dma_key="c_pv")
        P.add("sp", lambda e: e.dma_start(out=hb[:], in_=hb_d), writes=["hb"], dma_key="c_hb")
        P.add("pool", lambda e: e.dma_start(out=wbd[:], in_=wbd_d), writes=["wbd"], dma_key="c_wbd")
        P.add("pool", lambda e: e.dma_start(out=bs[:], in_=bs_d), writes=["bs"], dma_key="c_bs")
        P.add("pool", lambda e: e.dma_start(out=identb[:], in_=cst_d[:, 0:128]), writes=["identb"], dma_key="c_idb")
        P.add("dve", lambda e: e.memset(onesb[:], 1.0), writes=["onesb"])
        P.add("dve", lambda e: e.memset(onesf[:], 1.0), writes=["onesf"])
        P.add("dve", lambda e: e.memset(dummy[:], 1.0), writes=["dummy0", "dummy1"])
        P.add("dve", lambda e: e.memset(halo[:], 0.0), writes=["halo%d" % ci for ci in range(10)])
        P.add("dve", lambda e: e.memset(hstate[:], 0.0), writes=["hst0", "hst1", "hst2"])
        P.add("dve", lambda e: e.memset(H32[:], 0.0), writes=["H32"])
        for j in range(3):
            ts(neg3[:, j * 128:(j + 1) * 128], Umask, -1.0, 30000.0, ALU.add, ALU.mult, ["cst"], ["neg3"])
        for l in range(L):
            for nm in ("ffn1_post_g", "ffn2_post_g"):
                o = pv_off(l, nm)
                ts(pv[:, o:o + 8], pv[:, o:o + 8], 0.5, None, ALU.mult, None, ["pv"], ["pv"])
            o = pv_off(l, "lru_lambda")
            act(pv[:, o:o + 3], pv[:, o:o + 3], AF.Exp, ["pv"], ["pv"], scale=-1.0)
            act(pv[:, o:o + 3], pv[:, o:o + 3], AF.Ln, ["pv"], ["pv"], bias=1.0)
            ts(pv[:, o:o + 3], pv[:, o:o + 3], -8.0, None, ALU.mult, None, ["pv"], ["pv"])
            o = l * 48 + 24
            act(hb[:, o:o + 24], hb[:, o:o + 24], AF.Exp, ["hb"], ["hb"])
            ts(hb[:, o:o + 24], hb[:, o:o + 24], -1.0, None, ALU.mult, None, ["hb"], ["hb"])
            for g in range(4):
                o = (l * 4 + g) * 128
                P.add("sp", lambda e, o=o, g=g: e.dma_start(out=wst32[:, g * 128:(g + 1) * 128], in_=wst_d[:, o:o + 128]),
                      writes=["wst32_%d" % g], dma_key="c_wst%d" % g)
                tt("dve", wst[:, o:o + 128], wst32[:, g * 128:(g + 1) * 128], Umask, ALU.mult,
                   ["wst32_%d" % g, "cst"], ["wst"])

        def piece_parts(p_):
            l, r_ = divmod(p_, PPL)
            if r_ < 11 or 26 <= r_ < 37:
                f, jp = (0, r_) if r_ < 11 else (1, r_ - 26)
                wg = wgu_d[f][l].rearrange("(k p) c -> p k c", p=128)
                return [(s * 2048, 8, 256, wg[:, :, s * DFF + jp * 256:s * DFF + (jp + 1) * 256]) for s in range(2)]
            if 11 <= r_ < 19 or r_ >= 37:
                f, dp = (0, r_ - 11) if r_ < 19 else (1, r_ - 37)
                wd = wdn_d[f][l].rearrange("(k p) c -> p k c", p=128)
                return [(0, NFC, 128, wd[:, :, dp * 128:(dp + 1) * 128])]
            if 19 <= r_ < 24:
                c0, c1 = IN_PIECES[r_ - 19]
                wi = win_d[l].rearrange("(k p) c -> p k c", p=128)
                return [(0, 8, c1 - c0, wi[:, :, c0:c1])]
            op_ = r_ - 24
            wo = wout_d[l].rearrange("(k p) c -> p k c", p=128)
            return [(0, 8, 512, wo[:, :, op_ * 512:(op_ + 1) * 512])]

        def wload(p_, n):
            i = st["ws"] % NSLOT
            st["ws"] += 1
            if st["first"]:
                for (off, k, c, src) in piece_parts(p_):
                    dstv = wsl[i][:, off:off + k * c].rearrange("p (k c) -> p k c", k=k)
                    P.add("pool", lambda e, dstv=dstv, src=src: e.dma_start(out=dstv, in_=src),
                          writes=["wsl%d" % i], dma_key="wp%d" % i)
                P.add("sp", lambda e: e.dma_start(out=wsc[p_][:, 0:n], in_=wsl[i][:, 0:n]),
                      reads=["wsl%d" % i], writes=["wsc%d" % p_], dma_key="wb%d" % i)
            else:
                P.add("sp", lambda e: e.dma_start(out=wsl[i][:, 0:n], in_=wsc[p_][:, 0:n]),
                      reads=["wsc%d" % p_], writes=["wsl%d" % i], dma_key="w%d" % i)
            return wsl[i], "wsl%d" % i

        def stats_of(src_ap, src_tok, c, n_c):
            i = st["sq"] % 2
            st["sq"] += 1
            act(sq[i][:], src_ap, AF.Square, [src_tok], ["sq%d" % i])
            return lambda: mm(ps_st[:], onesb[:], sq[i][:], c == 0, c == n_c - 1, ["onesb", "sq%d" % i], ["ps_st"])

        def preload(func):
            kw = {"bias": 1.0} if func == AF.Ln else {}
            P.add("act", lambda e: e.activation(out=dummy[:, 1:2], in_=dummy[:, 0:1], func=func, **kw),
                  ["dummy0"], ["dummy1"])

        def finish_rstd(n):
            act(rstd[:], ps_st[:], AF.Ln, ["ps_st"], ["rstd"], bias=EPS, scale=1.0 / n)
            act(rstd[:], rstd[:], AF.Exp, ["rstd"], ["rstd"], scale=-0.5)

        def prenorm(l, gname):
            for c in range(8):
                stats_of(xc(xT, c), "xT%d" % c, c, 8)()
            finish_rstd(D)
            for c in range(8):
                stt(xc(xn, c), xc(xT, c), pvc(l, gname, c), rstd[:], ALU.mult, ALU.mult,
                    ["xT%d" % c, "pv", "rstd"], ["xn%d" % c])

        def prenorm_lazy(l, gname):
            for c in range(8):
                stats_of(xc(xT, c), "xT%d" % c, c, 8)()
                act(xc(xn, c), xc(xT, c), AF.Copy, ["xT%d" % c, "pv"], ["xn%d" % c], scale=pvc(l, gname, c))
            finish_rstd(D)

        def postnorm(l, gname):
            finish_rstd(D)
            for c in range(8):
                tt("dve", xc(fT, c), xc(fT, c), rstd[:], ALU.mult, ["fT%d" % c, "rstd"], ["fT%d" % c])
                stt(xc(xT, c), xc(fT, c), pvc(l, gname, c), xc(xT, c), ALU.mult, ALU.add,
                    ["fT%d" % c, "pv", "xT%d" % c], ["xT%d" % c])

        def out_group(ps, pst, dc, pend):
            act(xc(fT, dc), ps[:], AF.Copy, [pst], ["fT%d" % dc])
            pend.append(stats_of(ps[:], pst, dc, 8))

        def ffn(l, f):
            pre, post = ("ffn1_pre_g", "ffn1_post_g") if f == 0 else ("ffn2_pre_g", "ffn2_post_g")
            P.tag = 'ffn_pre'
            prenorm_lazy(l, pre)
            P.tag = 'ffn_up'
            preload(AF.Silu)
            for jp in range(11):
                w, wt = wload(pid(l, "gu%d" % f, jp), 4096)
                banks = [[nb(), nb()] for fi in range(2)]
                if jp == 0:
                    for kc in range(8):
                        for fi in range(2):
                            for s in range(2):
                                o = s * 2048 + kc * 256 + fi * 128
                                ps, pst = banks[fi][s]
                                mm(ps[:], w[:, o:o + 128], xc(xn, kc), kc == 0, kc == 7, [wt, "xn%d" % kc], [pst])
                else:
                    for fi in range(2):
                        for s in range(2):
                            ps, pst = banks[fi][s]
                            for kc in range(8):
                                o = s * 2048 + kc * 256 + fi * 128
                                mm(ps[:], w[:, o:o + 128], xc(xn, kc), kc == 0, kc == 7, [wt, "xn%d" % kc], [pst])
                for fi in range(2):
                    j = jp * 2 + fi
                    (pg, pgt), (pu, put) = banks[fi]
                    i = st["sg"] % 2
                    st["sg"] += 1
                    tt("dve", sg[i][:], pg[:], rstd[:], ALU.mult, [pgt, "rstd"], ["sg%d" % i])
                    act(sg[i][:], sg[i][:], AF.Silu, ["sg%d" % i], ["sg%d" % i])
                    tt("dve", cacc[i][:], pu[:], rstd[:], ALU.mult, [put, "rstd"], ["cacc%d" % i])
                    tt("dve", hch(j), sg[i][:], cacc[i][:], ALU.mult, ["sg%d" % i, "cacc%d" % i], ["h%d" % j])
            preload(AF.Ln)
            P.tag = 'ffn_down'
            pend = []
            for dc in range(8):
                w, wt = wload(pid(l, "dn%d" % f, dc), NFC * 128)
                pf, pft = nb()
                for j in range(NFC):
                    o = j * 128
                    mm(pf[:], w[:, o:o + 128], hch(j), j == 0, j == NFC - 1, [wt, "h%d" % j], [pft])
                for fn in pend:
                    fn()
                pend = []
                out_group(pf, pft, dc, pend)
            for fn in pend:
                fn()
            P.tag = 'ffn_post'
            postnorm(l, post)

        NRAW = 3

        def evac_scaled(ps, pst):
            i = st["raw"] % NRAW
            st["raw"] += 1
            pl = st["pend2"]
            while any(bi == i for bi, _ in pl):
                pl.pop(0)[1]()
            tt("dve", raw[i][:, 3:T + 3], ps[:], rstd[:], ALU.mult, [pst, "rstd"], ["raw%d" % i])
            return i

        def conv_s1(l, i, ci, wname, bname, wc, out_ap, out_tok):
            r, rt, rh = raw[i], "raw%d" % i, "rawh%d" % i
            ho = (l * 10 + ci) * 3
            cp("act", r[:, 0:3], halo[:, ho:ho + 3], ["halo%d" % ci], [rh, rt])
            act(out_ap, r[:, 3:T + 3], AF.Identity, [rt, "pv"], [out_tok], bias=pvc(l, bname, wc), scale=pvc(l, wname % 3, wc))

            def s2():
                for k in (2, 1, 0):
                    stt(out_ap, r[:, k:T + k], pvc(l, wname % k, wc), out_ap, ALU.mult, ALU.add,
                        [rt, rh, "pv", out_tok], [out_tok])
                cp("dve", halo[:, ho:ho + 3], r[:, T:T + 3], [rt], ["halo%d" % ci])
            return s2

        def mixer(l):
            P.tag = 'mix_pre'
            prenorm_lazy(l, "mix_pre_g")
            pr2, pr2t = nb()
            for q in range(4):
                tr(pr2[:, q * 128:(q + 1) * 128], rstd[:, q * 128:(q + 1) * 128], ident, ["rstd", "cst"], [pr2t])
            P.add("dve", lambda e: e.tensor_copy(out=r2tok[:].unsqueeze(2),
                                                 in_=pr2[:].rearrange("p (q c) -> p q c", q=4)[:, :, 0:1]),
                  [pr2t], ["r2tok"])
            P.tag = 'mix_in'
            pend2 = []
            st["pend2"] = pend2

            def flush(keep):
                while len(pend2) > keep:
                    pend2.pop(0)[1]()
            def sgu_part1():
                p1, p1t = nb()
                p2, p2t = nb()
                for c in range(2):
                    mm(p1[:], onesf[:], xc(v_g, c), c == 0, c == 1, ["onesf", "v%d" % c], [p1t])
                for c in range(2):
                    tt("dve", tmp[c][:], xc(v_g, c), xc(v_g, c), ALU.mult, ["v%d" % c], ["tmp%d" % c])
                    mm(p2[:], onesf[:], tmp[c][:], c == 0, c == 1, ["onesf", "tmp%d" % c], [p2t])
                mean, var = tmp[2], tmp[3]
                act(mean[:], p1[:], AF.Copy, [p1t], ["tmp2"], scale=1.0 / 256)
                tt("dve", var[:], mean[:], mean[:], ALU.mult, ["tmp2"], ["tmp3"])
                stt(var[:], p2[:], 1.0 / 256, var[:], ALU.mult, ALU.subtract, [p2t, "tmp3"], ["tmp3"])
                act(var[:], var[:], AF.Ln, ["tmp3"], ["tmp3"], bias=EPS)
                act(var[:], var[:], AF.Exp, ["tmp3"], ["tmp3"], scale=-0.5)
                act(dt_a[:], dt_a[:], AF.Exp, ["dt_a"], ["dt_a"])
                act(dt_a[:], dt_a[:], AF.Ln, ["dt_a"], ["dt_a"], bias=1.0)
                tt("dve", ad_t[:], dt_a[:], hb[:, l * 48 + 24:l * 48 + 48], ALU.mult, ["dt_a", "hb"], ["ad_t"])
                if dbg_d is not None and l == 0 and st["first"]:
                    P.add("sp", lambda e: e.dma_start(out=dbg_d[:, 0:24], in_=dt_a[:]), reads=["dt_a"], dma_key="dbg")
                    P.add("sp", lambda e: e.dma_start(out=dbg_d[:, 24:28], in_=r2tok[:]), reads=["r2tok"], dma_key="dbg")
                    P.add("sp", lambda e: e.dma_start(out=dbg_d[:, 28:60], in_=rstd[:, 0:32]), reads=["rstd"], dma_key="dbg")
                for c in range(2):
                    tt("dve", tmp[c][:], xc(v_g, c), mean[:], ALU.subtract, ["v%d" % c, "tmp2"], ["tmp%d" % c])
                    tt("dve", tmp[c][:], tmp[c][:], var[:], ALU.mult, ["tmp%d" % c, "tmp3"], ["tmp%d" % c])
                    ts(hch(10 + c), tmp[c][:], pvc(l, "sgu_ln_g", c), pvc(l, "sgu_ln_b", c), ALU.mult, ALU.add,
                       ["tmp%d" % c, "pv"], ["h%d" % (10 + c)])

            def sgu_part2():
                pm = [nb(), nb()]
                pend = []
                for q in range(4):
                    pt, ptt = nbf()
                    for c in range(2):
                        tr(pt[:, c * 128:(c + 1) * 128], hT[:, (10 + c) * T + q * 128:(10 + c) * T + (q + 1) * 128],
                           identb[:], ["h%d" % (10 + c), "identb"], [ptt])
                    vt = vtok[q % 2]
                    cp("act", vt, pt[:, 0:256], [ptt], [tvtok[q % 2]])
                    for fn in pend:
                        fn()
                    pend = []

                    def mixq(q=q, vt=vt):
                        for g in range(4):
                            c, po = g // 2, (g % 2) * 64
                            o = (l * 4 + g) * 128
                            dst = pm[c][0][po:po + 64, q * 128:(q + 1) * 128]
                            mm(dst, vt[:, g * 64:(g + 1) * 64], wst[:, o:o + 128], True, False,
                               [tvtok[q % 2], "wst"], [pm[c][1]])
                            mm(dst, onesb[0:1, 0:64], bs[0:1, o:o + 128], False, True, ["onesb", "bs"], [pm[c][1]])
                    pend.append(mixq)
                for fn in pend:
                    fn()
                for c in range(2):
                    tt("dve", tmp[c][:], xc(u_g, c), pm[c][0][:], ALU.mult, ["u%d" % c, pm[c][1]], ["tmp%d" % c])

            for ip in (4, 3, 2, 1, 0):
                if ip == 2:
                    P.tag = 'sgu'
                    sgu_part2()
                    P.tag = 'mix_in'
                if ip == 3 and TOG_SGU:
                    flush(0)
                    sgu_part1()
                c0, c1 = IN_PIECES[ip]
                n = c1 - c0
                w, wt = wload(pid(l, "in", ip), 8 * n)
                for (kind, c, col0, width) in IN_CHUNKS:
                    if not (c0 <= col0 < c1):
                        continue
                    lo = col0 - c0
                    if kind == "dt":
                        pdt, pdtt = nb()
                        for q in range(4):
                            for kc in range(8):
                                mm(pdt[:, q * 6:(q + 1) * 6], xn[:, kc * T + q * 128:kc * T + (q + 1) * 128],
                                   w[:, kc * n + lo:kc * n + lo + 6], kc == 0, kc == 7, [wt, "xn%d" % kc], [pdtt])
                        P.add("dve", lambda e, pdt=pdt: e.tensor_tensor(
                            out=dt_a[:].rearrange("p (q h) -> p q h", q=4),
                            in0=pdt[:, 0:24].rearrange("p (q h) -> p q h", q=4),
                            in1=r2tok[:].unsqueeze(2).to_broadcast([128, 4, 6]), op=ALU.mult),
                            [pdtt, "r2tok"], ["dt_a"])
                        tt("dve", dt_a[:], dt_a[:], hb[:, l * 48:l * 48 + 24], ALU.add, ["dt_a", "hb"], ["dt_a"])
                        continue
                    ps, pst = nb()
                    for kc in range(8):
                        mm(ps[:], w[:, kc * n + lo:kc * n + lo + 128], xc(xn, kc), kc == 0, kc == 7,
                           [wt, "xn%d" % kc], [pst])
                    ri = evac_scaled(ps, pst)
                    rsrc, rtk = raw[ri][:, 3:T + 3], "raw%d" % ri
                    if kind == "gate":
                        act(xc(gate_g, c), rsrc, AF.Gelu, [rtk], ["gate%d" % c])
                    elif kind == "z":
                        act(xc(zs, c), rsrc, AF.Silu, [rtk], ["zs%d" % c])
                    elif kind == "u":
                        act(xc(u_g, c), rsrc, AF.Gelu, [rtk], ["u%d" % c])
                    elif kind == "v":
                        act(xc(v_g, c), rsrc, AF.Gelu, [rtk], ["v%d" % c])
                    elif kind == "rec":
                        s2 = conv_s1(l, ri, c, "lru_cw%d", "lru_conv_b", c, xc(rec32, c), "rec%d" % c)

                        def s2rec(s2=s2, c=c):
                            s2()
                            cp("act", hch(c), xc(rec32, c), ["rec%d" % c], ["h%d" % c])
                        flush(1)
                        pend2.append((ri, s2rec))
                        continue
                    elif kind == "xbc":
                        i = st["cacc"] % NRAW
                        st["cacc"] += 1
                        s2 = conv_s1(l, ri, 3 + c, "ssd_cw%d", "ssd_conv_b", c, cacc[i][:], "cacc%d" % i)

                        def s2x(s2=s2, c=c, i=i):
                            s2()
                            act(hch(3 + c), cacc[i][:], AF.Silu, ["cacc%d" % i], ["h%d" % (3 + c)])
                        flush(1)
                        pend2.append((ri, s2x))
                        continue
                    flush(1)
            flush(0)
            if not TOG_SGU:
                sgu_part1()

            P.tag = 'sgu'
            for c in range(2):
                cp("act", xc(xn, 6 + c), tmp[c][:], ["tmp%d" % c], ["xn%d" % (6 + c)])

            lru = []
            A_ = [tmp[0], tmp[1], tmp[2]]
            tA = ["tmp0", "tmp1", "tmp2"]
            B_ = [cacc[0][:], cacc[1][:], cacc[2][:]]
            tB = ["cacc0", "cacc1", "cacc2"]
            C_ = [tmp[3][:], tmp[4][:], rstd[:]]
            tC = ["tmp3", "tmp4", "rstd"]

            def L(fn):
                lru.append(fn)
            for c in range(3):
                def gates(c=c):
                    pr, prt = nb()
                    pi_, pit = nb()
                    o = ((l * 2 + 0) * 3 + c) * 128
                    mm(pr[:], wbd[:, o:o + 128], hch(c), True, True, ["wbd", "h%d" % c], [prt])
                    o = ((l * 2 + 1) * 3 + c) * 128
                    mm(pi_[:], wbd[:, o:o + 128], hch(c), True, True, ["wbd", "h%d" % c], [pit])
                    act(A_[c][:], pr[:], AF.Sigmoid, [prt, "pv"], [tA[c]], bias=pvc(l, "lru_b_r", c))
                    act(B_[c], pi_[:], AF.Sigmoid, [pit, "pv"], [tB[c]], bias=pvc(l, "lru_b_i", c))
                L(gates)
            for c in range(3):
                L(lambda c=c: act(A_[c][:], A_[c][:], AF.Exp, [tA[c], "pv"], [tA[c]], scale=pvc(l, "lru_lambda", c)))
                L(lambda c=c: tt("dve", C_[c], A_[c][:], A_[c][:], ALU.mult, [tA[c]], [tC[c]]))
                L(lambda c=c: tt("dve", B_[c], B_[c], xc(rec32, c), ALU.mult, [tB[c], "rec%d" % c], [tB[c]]))
            for c in range(3):
                L(lambda c=c: act(C_[c], C_[c], AF.Ln, [tC[c]], [tC[c]], bias=1.0, scale=-1.0))
            for c in range(3):
                L(lambda c=c: act(C_[c], C_[c], AF.Exp, [tC[c]], [tC[c]], scale=0.5))
                L(lambda c=c: tt("dve", B_[c], B_[c], C_[c], ALU.mult, [tB[c], tC[c]], [tB[c]]))
            for c in range(3):
                def scan(c=c):
                    hs = hstate[:, l * 3 + c:l * 3 + c + 1]
                    P.add("dve", lambda e: e.tensor_tensor_scan(
                        out=C_[c], data0=A_[c][:], data1=B_[c], initial=hs, op0=ALU.mult, op1=ALU.add),
                        [tA[c], tB[c], "hst%d" % c], [tC[c]])
                    cp(st["pool"], hs, C_[c][:, T - 1:T], [tC[c]], ["hst%d" % c])
                    tt("dve", xc(xn, c), C_[c], xc(gate_g, c), ALU.mult, [tC[c], "gate%d" % c], ["xn%d" % c])
                L(scan)

            def lru_some(k):
                for _ in range(k):
                    if lru:
                        lru.pop(0)()

            P.tag = 'ssd'
            lru_some(3)
            cp("act", Hbf[0], H32[:, l * 384:(l + 1) * 384], ["H32"], ["h19"])
            pe2 = st["pool"]

            def mk_rhsU(k):
                kb, a0 = k % 2, (k // 2) * 6 + (k % 2) * 3
                P.add("dve", lambda e: e.tensor_tensor(
                    out=rhsU[kb][:].rearrange("p (j l) -> p j l", j=3),
                    in0=Umask.unsqueeze(1).to_broadcast([128, 3, 128]),
                    in1=ad_t[:, a0:a0 + 3].unsqueeze(2).to_broadcast([128, 3, 128]), op=ALU.mult),
                    ["cst", "ad_t"], ["rhsU%d" % kb])
            mk_rhsU(0)
            for q in range(4):
                for g in range(2):
                    k = q * 2 + g
                    kb = k % 2
                    a0 = q * 6 + g * 3
                    v3 = lambda ap: ap.rearrange("p (j l) -> p j l", j=3)
                    pD, pDt = nb()
                    mm(pD[:, 0:384], Lstr, rhsU[kb][:], True, False, ["cst", "rhsU%d" % kb], [pDt])
                    mm(pD[:, 0:384], ident, neg3[:], False, True, ["cst", "neg3"], [pDt])
                    pC, pCt = nb()
                    mm(pC[:, 0:384], onesf[:], rhsU[kb][:], True, True, ["onesf", "rhsU%d" % kb], [pCt])
                    pcb, pcbt = nb()
                    mm(pcb[:, 0:128], hT[:, (6 + g) * T + q * 128:(6 + g) * T + (q + 1) * 128],
                       hT[:, (8 + g) * T + q * 128:(8 + g) * T + (q + 1) * 128], True, True,
                       ["h%d" % (6 + g), "h%d" % (8 + g)], [pcbt])
                    if k < 7:
                        mk_rhsU(k + 1)
                    act(Eb[kb][:], pD[:, 0:384], AF.Exp, [pDt], ["E%d" % kb])
                    act(E2b[kb][:], pC[:, 0:384], AF.Exp, [pCt], ["E2%d" % kb])
                    P.add("dve", lambda e, kb=kb, k=k: e.tensor_copy(
                        out=wcol[:, k * 3:(k + 1) * 3].unsqueeze(2),
                        in_=Eb[kb][:].rearrange("p (j l) -> p j l", j=3)[:, :, 127:128]),
                        ["E%d" % kb], ["wcol"])
                    P.add("dve", lambda e, kb=kb, k=k: e.tensor_copy(
                        out=dcol[:, k * 3:(k + 1) * 3].unsqueeze(2),
                        in_=E2b[kb][:].rearrange("p (j l) -> p j l", j=3)[:, :, 127:128]),
                        ["E2%d" % kb], ["dcol"])
                    P.add("dve", lambda e, kb=kb, k=k, pcb=pcb: e.tensor_tensor(
                        out=MTall[:, k * 384:(k + 1) * 384].rearrange("p (j l) -> p j l", j=3),
                        in0=Eb[kb][:].rearrange("p (j l) -> p j l", j=3),
                        in1=pcb[:, 0:128].unsqueeze(1).to_broadcast([128, 3, 128]), op=ALU.mult),
                        ["E%d" % kb, pcbt], ["MT%d" % k])
                    P.add(pe2, lambda e, kb=kb, k=k, g=g, q=q: e.tensor_tensor(
                        out=CsTall[:, k * 384:(k + 1) * 384].rearrange("p (j l) -> p j l", j=3),
                        in0=E2b[kb][:].rearrange("p (j l) -> p j l", j=3),
                        in1=hT[:, (8 + g) * T + q * 128:(8 + g) * T + (q + 1) * 128].unsqueeze(1).to_broadcast([128, 3, 128]),
                        op=ALU.mult),
                        ["E2%d" % kb, "h%d" % (8 + g)], ["CsT%d" % k])
                if q % 2 == 1:
                    lru_some(3)
            xdtB = [xdt, hT[:, 12 * T:12 * T + 384]]
            xdtwB = [xdtw, hT[:, 13 * T:13 * T + 384]]
            BtokB = [Btok, hT[:, 14 * T:14 * T + 256]]
            dtwB = [dtw, dtw2]
            txdt, txdtw, tbtok, tdtw = ["h16", "h12"], ["h17", "h13"], ["h18", "h14"], ["dtw", "dtw2"]

            def prep(q):
                pb_ = q % 2
                pt, ptt = nbf()
                for c in range(3):
                    tr(pt[:, c * 128:(c + 1) * 128], hT[:, (3 + c) * T + q * 128:(3 + c) * T + (q + 1) * 128],
                       identb[:], ["h%d" % (3 + c), "identb"], [ptt])
                for g in range(2):
                    tr(pt[:, 384 + g * 128:384 + (g + 1) * 128],
                       hT[:, (6 + g) * T + q * 128:(6 + g) * T + (q + 1) * 128], identb[:],
                       ["h%d" % (6 + g), "identb"], [ptt])
                P.add("dve", lambda e: e.tensor_tensor(
                    out=xdtB[pb_].rearrange("p (h d) -> p h d", h=6),
                    in0=pt[:, 0:384].rearrange("p (h d) -> p h d", h=6),
                    in1=dt_a[:, q * 6:(q + 1) * 6].unsqueeze(2).to_broadcast([128, 6, 64]), op=ALU.mult),
                    [ptt, "dt_a"], [txdt[pb_]])
                tt("dve", dtwB[pb_][:], dt_a[:, q * 6:(q + 1) * 6], wcol[:, q * 6:(q + 1) * 6], ALU.mult,
                   ["dt_a", "wcol"], [tdtw[pb_]])
                P.add("dve", lambda e: e.tensor_tensor(
                    out=xdtwB[pb_].rearrange("p (h d) -> p h d", h=6),
                    in0=pt[:, 0:384].rearrange("p (h d) -> p h d", h=6),
                    in1=dtwB[pb_][:].unsqueeze(2).to_broadcast([128, 6, 64]), op=ALU.mult),
                    [ptt, tdtw[pb_]], [txdtw[pb_]])
                cp("dve", BtokB[pb_], pt[:, 384:640], [ptt], [tbtok[pb_]])
            prep(0)
            for q in range(4):
                par = q % 2
                if q < 3:
                    prep(q + 1)
                xdt_, xdtw_, Btok_ = xdtB[par], xdtwB[par], BtokB[par]
                py, pyt = nb()
                for h in range(6):
                    g, j, cc, po = h // 3, h % 3, h // 2, (h % 2) * 64
                    k = q * 2 + g
                    dst = py[po:po + 64, cc * 128:(cc + 1) * 128]
                    mm(dst, xdt_[:, h * 64:(h + 1) * 64], MTall[:, k * 384 + j * 128:k * 384 + (j + 1) * 128], True, False,
                       [txdt[par], "MT%d" % k], [pyt])
                    mm(dst, Hbf[par][:, h * 64:(h + 1) * 64], CsTall[:, k * 384 + j * 128:k * 384 + (j + 1) * 128],
                       False, True, ["h%d" % (19 + par), "CsT%d" % k], [pyt])
                for cc in range(3):
                    stt(yT2[:, cc * T + q * 128:cc * T + (q + 1) * 128],
                        hT[:, (3 + cc) * T + q * 128:(3 + cc) * T + (q + 1) * 128], pvc(l, "ssd_dvec", cc),
                        py[:, cc * 128:(cc + 1) * 128], ALU.mult, ALU.add,
                        ["h%d" % (3 + cc), "pv", pyt], ["yS%d" % cc])
                pS, pSt = nb()
                for g in range(2):
                    mm(pS[:, g * 192:(g + 1) * 192], Btok_[:, g * 128:(g + 1) * 128], xdtw_[:, g * 192:(g + 1) * 192],
                       True, True, [tbtok[par], txdtw[par]], [pSt])
                P.add("dve", lambda e, q=q: e.tensor_tensor(
                    out=H32[:, l * 384:(l + 1) * 384].rearrange("p (h d) -> p h d", h=6),
                    in0=H32[:, l * 384:(l + 1) * 384].rearrange("p (h d) -> p h d", h=6),
                    in1=dcol[:, q * 6:(q + 1) * 6].unsqueeze(2).to_broadcast([128, 6, 64]),
                    op=ALU.mult), ["H32", "dcol"], ["H32"])
                tt("dve", H32[:, l * 384:(l + 1) * 384], H32[:, l * 384:(l + 1) * 384], pS[:, 0:384], ALU.add,
                   ["H32", pSt], ["H32"])
                if q < 3:
                    cp("dve", Hbf[1 - par], H32[:, l * 384:(l + 1) * 384], ["H32"], ["h%d" % (19 + 1 - par)])
                lru_some(3)
            lru_some(100)
            P.tag = 'mix_out'
            pend = []
            korder = (6, 7, 0, 1, 2, 3, 4, 5)
            w0, wt0 = wload(pid(l, "out", 0), 4096)
            bks = [nb() for di in range(4)]
            for ki, kc in enumerate(korder[:5]):
                for di in range(4):
                    o = kc * 512 + di * 128
                    mm(bks[di][0][:], w0[:, o:o + 128], xc(xn, kc), ki == 0, False, [wt0, "xn%d" % kc], [bks[di][1]])
            P.tag = 'ssd_norm'
            for cc in range(3):
                tt("dve", xc(yT2, cc), xc(yT2, cc), xc(zs, cc), ALU.mult, ["yS%d" % cc, "zs%d" % cc], ["yS%d" % cc])
                stats_of(xc(yT2, cc), "yS%d" % cc, cc, 3)()
            finish_rstd(384)
            for cc in range(3):
                stt(xc(xn, 3 + cc), xc(yT2, cc), pvc(l, "ssd_norm_g", cc), rstd[:], ALU.mult, ALU.mult,
                    ["yS%d" % cc, "pv", "rstd"], ["xn%d" % (3 + cc)])
            P.tag = 'mix_out'
            for ki, kc in enumerate(korder[5:]):
                for di in range(4):
                    o = kc * 512 + di * 128
                    mm(bks[di][0][:], w0[:, o:o + 128], xc(xn, kc), False, ki == 2, [wt0, "xn%d" % kc], [bks[di][1]])
            for di in range(4):
                for fn in pend:
                    fn()
                del pend[:]
                out_group(bks[di][0], bks[di][1], di, pend)
            w, wt = wload(pid(l, "out", 1), 4096)
            for di in range(4):
                dc = 4 + di
                pf, pft = nb()
                for ki, kc in enumerate(korder):
                    o = kc * 512 + di * 128
                    mm(pf[:], w[:, o:o + 128], xc(xn, kc), ki == 0, ki == 7, [wt, "xn%d" % kc], [pft])
                for fn in pend:
                    fn()
                pend = []
                out_group(pf, pft, dc, pend)
            for fn in pend:
                fn()
            P.tag = 'mix_post'
            postnorm(l, "mix_post_g")

        fall = ["fT%d" % c for c in range(8)]
        for ti in range(n_tiles):
            P.tag = 'io_in'
            src = x_d[ti * T:(ti + 1) * T, :].rearrange("(q p) d -> p q d", p=128)
            P.add("sp", lambda e, src=src: e.dma_start(out=fT[:].rearrange("p (q d) -> p q d", q=4), in_=src),
                  writes=fall, dma_key="xin")
            for c in range(8):
                ps, pst = nb()
                for q in range(4):
                    tr(ps[:, q * 128:(q + 1) * 128], fT[:, q * D + c * 128:q * D + (c + 1) * 128], ident,
                       fall + ["cst"], [pst])
                cp("dve" if c % 2 else "act", xc(xT, c), ps[:], [pst], ["xT%d" % c])
            st["pool"] = "dve" if ti == 0 else "pool"
            st["first"] = (ti == 0)
            for l in range(L):
                ffn(l, 0)
                mixer(l)
                ffn(l, 1)
            P.tag = 'io_out'
            for q in range(4):
                for hf in range(2):
                    ps, pst = nb()
                    for c4 in range(4):
                        c = hf * 4 + c4
                        tr(ps[:, c4 * 128:(c4 + 1) * 128], xT[:, c * T + q * 128:c * T + (q + 1) * 128], ident,
                           ["xT%d" % c, "cst"], [pst])
                    cp("dve" if hf else "act", fT[:, q * D + hf * 512:q * D + (hf + 1) * 512], ps[:], [pst], fall)
            dst = out_d[ti * T:(ti + 1) * T, :].rearrange("(q p) d -> p q d", p=128)
            P.add("sp", lambda e, dst=dst: e.dma_start(out=dst, in_=fT[:].rearrange("p (q d) -> p q d", q=4)),
                  reads=fall, dma_key="xout")

        stats = P.emit(final_dma_keys=["xout"])
    build.prog = P
    return nc, stats


N_CORES = 8
SEQ = 4096
DEPTH = 4


def kernel(**inputs):
    L = DEPTH
    x = np.asarray(inputs["x"], np.float32)
    B = x.shape[0]
    n_tiles = x.shape[1] // T
    nc, _ = build(n_tiles, L)
    shared = prep_params(inputs, L)
    for nm in ("ffn1_w_gu", "ffn2_w_gu", "ffn1_w_down", "ffn2_w_down", "mix_w_in", "mix_w_out"):
        shared[nm] = np.ascontiguousarray(np.asarray(inputs[nm], np.float32))
    in_maps = []
    for b in range(B):
        m = dict(shared)
        m["x"] = np.ascontiguousarray(x[b])
        in_maps.append(m)
    res = run_bass_kernel_spmd(nc, in_maps, core_ids=list(range(B)))
    return np.stack([np.asarray(r["out"], np.float32) for r in res.results], axis=0)
```

```python
import numpy as np
from contextlib import ExitStack
import concourse.bass as bass
import concourse.mybir as mybir
from concourse.bass_utils import run_bass_kernel_spmd

F32 = mybir.dt.float32
BF16 = mybir.dt.bfloat16
AF = mybir.ActivationFunctionType
ALU = mybir.AluOpType

D = 1024
DFF = 2816
NFC = 22
INC = 2566
T = 512
EPS = 1e-6
SLOT = 4160
NSLOT = 4
import os
TOG_OUT = os.environ.get('TOG_OUT', '1') == '1'
TOG_SGU = os.environ.get('TOG_SGU', '1') == '1'


class Op:
    __slots__ = ("eng", "fn", "is_dma", "key", "waits_dma", "deps", "sig", "idx", "tag")


class Prog:
    ENGS = ("pe", "act", "dve", "pool", "sp")

    def __init__(self, nc):
        self.nc = nc
        self.ops = []
        self.last_w = {}
        self.readers = {}
        self.dma_count = {}
        self.dma_keys = []
        self.tag = ""

    def add(self, eng, fn, reads=(), writes=(), dma_key=None):
        op = Op()
        op.eng = eng
        op.fn = fn
        op.is_dma = dma_key is not None
        op.key = dma_key
        op.idx = len(self.ops)
        op.sig = 0
        op.tag = self.tag
        deps = {}
        wd = {}

        def dep(d, same_ok):
            if d is None or d is op:
                return
            if d.is_dma:
                wd[d.key] = self.dma_count[d.key]
            else:
                if d.eng == eng and same_ok and not op.is_dma and eng == "pe":
                    return
                deps[d.idx] = d

        for t in reads:
            dep(self.last_w.get(t), False)
        for t in writes:
            dep(self.last_w.get(t), True)
            for r in self.readers.get(t, ()):
                dep(r, True)
        op.deps = list(deps.values())
        op.waits_dma = wd
        for d in op.deps:
            d.sig = 1
        for t in reads:
            self.readers.setdefault(t, []).append(op)
        for t in writes:
            self.last_w[t] = op
            self.readers[t] = []
        if op.is_dma:
            if dma_key not in self.dma_count:
                self.dma_count[dma_key] = 0
                self.dma_keys.append(dma_key)
            self.dma_count[dma_key] += 16
        self.ops.append(op)
        return op

    def emit(self, final_dma_keys=()):
        nc = self.nc
        with ExitStack() as es:
            esem = {e: es.enter_context(nc.semaphore("s_" + e)) for e in self.ENGS}
            dsem = {k: es.enter_context(nc.semaphore("d_%d" % i)) for i, k in enumerate(self.dma_keys)}
            cnt = {e: 0 for e in self.ENGS}
            for op in self.ops:
                if (not op.is_dma) and op.sig:
                    cnt[op.eng] += 1
                    op.sig = cnt[op.eng]
            per = {e: [o for o in self.ops if o.eng == e] for e in self.ENGS}
            stats = {e: [len(per[e]), 0] for e in self.ENGS}
            block = es.enter_context(nc.Block())

            def run(engname, engobj):
                waited = {}
                for op in per[engname]:
                    for d in op.deps:
                        k = ("e", d.eng)
                        if waited.get(k, 0) < d.sig:
                            engobj.wait_ge(esem[d.eng], d.sig)
                            waited[k] = d.sig
                            stats[engname][1] += 1
                    for key, val in op.waits_dma.items():
                        k = ("d", key)
                        if waited.get(k, 0) < val:
                            engobj.wait_ge(dsem[key], val)
                            waited[k] = val
                            stats[engname][1] += 1
                    ins = op.fn(engobj)
                    if op.is_dma:
                        ins.then_inc(dsem[op.key], 16)
                    elif op.sig:
                        ins.then_inc(esem[engname], 1)
                if engname == "sp":
                    for key in final_dma_keys:
                        engobj.wait_ge(dsem[key], self.dma_count[key])

            @block.tensor
            def _(e):
                run("pe", e)

            @block.scalar
            def _(e):
                run("act", e)

            @block.vector
            def _(e):
                run("dve", e)

            @block.gpsimd
            def _(e):
                run("pool", e)

            @block.sync
            def _(e):
                run("sp", e)
        return stats


PV_SPEC = [
    ("ffn1_pre_g", 8), ("ffn1_post_g", 8), ("mix_pre_g", 8), ("mix_post_g", 8),
    ("ffn2_pre_g", 8), ("ffn2_post_g", 8),
    ("lru_cw0", 3), ("lru_cw1", 3), ("lru_cw2", 3), ("lru_cw3", 3), ("lru_conv_b", 3),
    ("lru_b_r", 3), ("lru_b_i", 3), ("lru_lambda", 3),
    ("ssd_cw0", 7), ("ssd_cw1", 7), ("ssd_cw2", 7), ("ssd_cw3", 7), ("ssd_conv_b", 7),
    ("ssd_norm_g", 3), ("ssd_dvec", 3), ("sgu_ln_g", 2), ("sgu_ln_b", 2),
]
PV_PER_LAYER = sum(n for _, n in PV_SPEC)


def pv_off(l, name):
    o = l * PV_PER_LAYER
    for nm, n in PV_SPEC:
        if nm == name:
            return o
        o += n
    raise KeyError(name)


def _colmajor(v):
    v = np.asarray(v, np.float32)
    return np.ascontiguousarray(v.reshape(-1, 128).T)


def prep_params(inp, L):
    pv = np.zeros((128, L * PV_PER_LAYER), np.float32)
    for l in range(L):
        vecs = {
            "ffn1_pre_g": inp["ffn1_pre_g"][l], "ffn1_post_g": inp["ffn1_post_g"][l],
            "mix_pre_g": inp["mix_pre_g"][l], "mix_post_g": inp["mix_post_g"][l],
            "ffn2_pre_g": inp["ffn2_pre_g"][l], "ffn2_post_g": inp["ffn2_post_g"][l],
            "lru_conv_b": inp["lru_conv_b"][l], "lru_b_r": inp["lru_b_r"][l],
            "lru_b_i": inp["lru_b_i"][l], "lru_lambda": inp["lru_lambda"][l],
            "ssd_conv_b": inp["ssd_conv_b"][l], "ssd_norm_g": inp["ssd_norm_g"][l],
            "ssd_dvec": np.repeat(np.asarray(inp["ssd_d"][l]), 64),
            "sgu_ln_g": inp["sgu_ln_g"][l], "sgu_ln_b": inp["sgu_ln_b"][l],
        }
        for k in range(4):
            vecs["lru_cw%d" % k] = inp["lru_conv_w"][l][k]
            vecs["ssd_cw%d" % k] = inp["ssd_conv_w"][l][k]
        for nm, n in PV_SPEC:
            o = pv_off(l, nm)
            pv[:, o:o + n] = _colmajor(vecs[nm])
    hb = np.zeros((128, L * 48), np.float32)
    for l in range(L):
        hb[:, l * 48:l * 48 + 24] = np.tile(np.asarray(inp["ssd_dt_bias"][l], np.float32), 4)[None, :]
        hb[:, l * 48 + 24:l * 48 + 48] = np.tile(np.asarray(inp["ssd_a_log"][l], np.float32), 4)[None, :]
    wbd = np.zeros((128, L * 6 * 128), np.float32)
    for l in range(L):
        for gi, nm in enumerate(("lru_w_r", "lru_w_i")):
            w = np.asarray(inp[nm][l], np.float32)
            for c in range(3):
                o = ((l * 2 + gi) * 3 + c) * 128
                for hh in range(2):
                    wbd[hh * 64:(hh + 1) * 64, o + hh * 64:o + (hh + 1) * 64] = w[2 * c + hh]
    ws = np.asarray(inp["sgu_w_s"], np.float32)[:L]
    wst = np.ascontiguousarray(ws.transpose(3, 0, 1, 2).reshape(128, L * 4 * 128))
    bs = np.ascontiguousarray(np.asarray(inp["sgu_b_s"], np.float32)[:L].reshape(1, L * 4 * 128))
    k = np.arange(128)
    ident = np.eye(128, dtype=np.float32)
    U = (k[:, None] <= k[None, :]).astype(np.float32)
    Ls = (k[:, None] > k[None, :]).astype(np.float32)
    cst = np.ascontiguousarray(np.concatenate([ident, U, Ls], axis=1))
    return {"pv": pv, "hb": hb, "wbd": wbd, "wst": wst, "bs": bs, "cst": cst}


IN_CHUNKS = ([("rec", c, 384 + c * 128, 128) for c in range(3)] +
             [("gate", c, c * 128, 128) for c in range(3)] +
             [("z", c, 768 + c * 128, 128) for c in range(3)] +
             [("xbc", c, 1152 + c * 128, 128) for c in range(7)] +
             [("dt", 0, 2048, 6)] +
             [("u", c, 2054 + c * 128, 128) for c in range(2)] +
             [("v", c, 2310 + c * 128, 128) for c in range(2)])
IN_PIECES = [(0, 512), (512, 1024), (1024, 1536), (1536, 2048), (2048, 2566)]


def build(n_tiles, L):
    NTOK = n_tiles * T
    nc = bass.Bass("TRN2", target_bir_lowering=False)

    def din(name, shape, dt=F32):
        return nc.dram_tensor(name, shape, dt, kind="ExternalInput").ap()

    x_d = din("x", [NTOK, D])
    wgu_d = [din("ffn1_w_gu", [L, D, 2 * DFF]), din("ffn2_w_gu", [L, D, 2 * DFF])]
    wdn_d = [din("ffn1_w_down", [L, DFF, D]), din("ffn2_w_down", [L, DFF, D])]
    win_d = din("mix_w_in", [L, D, INC])
    wout_d = din("mix_w_out", [L, D, D])
    pv_d = din("pv", [128, L * PV_PER_LAYER])
    hb_d = din("hb", [128, L * 48])
    wbd_d = din("wbd", [128, L * 6 * 128])
    wst_d = din("wst", [128, L * 4 * 128])
    bs_d = din("bs", [1, L * 4 * 128])
    cst_d = din("cst", [128, 3 * 128])
    out_d = nc.dram_tensor("out", [NTOK, D], F32, kind="ExternalOutput").ap()
    dbg_d = nc.dram_tensor("dbg", [128, 64], F32, kind="ExternalOutput").ap() if os.environ.get("KDBG") else None

    PPL = 45
    wsc = nc.dram_tensor("wsc", [L * PPL, 128, SLOT], BF16).ap()

    def pid(l, sub, i):
        base = {"gu0": 0, "dn0": 11, "in": 19, "out": 24, "gu1": 26, "dn1": 37}[sub]
        return l * PPL + base + i

    P = Prog(nc)
    es = ExitStack()
    with es:
        def sb(name, n, dt=F32, parts=128):
            return es.enter_context(nc.sbuf_tensor(name, [parts, n], dt))

        def psum(name, n, dt=F32):
            return es.enter_context(nc.psum_tensor(name, [128, n], dt))

        xT = sb("xT", 8 * T)
        xn = sb("xn", 8 * T, BF16)
        fT = sb("fT", 8 * T)
        hT = sb("hT", NFC * T, BF16)
        wsl = [sb("wsl%d" % i, SLOT, BF16) for i in range(NSLOT)]
        sq = [sb("sq%d" % i, T, BF16) for i in range(2)]
        rstd = sb("rstd", T)
        sg = [sb("sg%d" % i, T) for i in range(2)]
        pv = sb("pv_s", L * PV_PER_LAYER)
        hb = sb("hb_s", L * 48)
        wbd = sb("wbd_s", L * 6 * 128, BF16)
        wst32 = sb("wst32", 4 * 128)
        wst = sb("wst_s", L * 4 * 128, BF16)
        bs = sb("bs_s", L * 4 * 128, BF16, parts=1)
        cst = sb("cst_s", 3 * 128)
        identb = sb("identb", 128, BF16)
        onesb = sb("onesb", 128, BF16)
        onesf = sb("onesf", 128)
        halo = sb("halo", L * 10 * 3)
        hstate = sb("hstate", L * 3)
        H32 = sb("H32", L * 384)
        gate_g = sb("gate_g", 3 * T)
        rec32 = sb("rec32", 3 * T)
        zs = sb("zs", 3 * T)
        raw = [sb("raw%d" % i, T + 3) for i in range(3)]
        cacc = [sb("cacc%d" % i, T) for i in range(3)]
        u_g = sb("u_g", 2 * T)
        v_g = sb("v_g", 2 * T)
        tmp = [sb("tmp%d" % i, T) for i in range(5)]
        rhsU = [sb("rhsU%d" % i, 384) for i in range(2)]
        Eb = [sb("E%d" % i, 384) for i in range(2)]
        E2b = [sb("E2%d" % i, 384) for i in range(2)]
        dt_a = sb("dt_a", 24)
        ad_t = sb("ad_t", 24)
        dtw = sb("dtw", 6)
        r2tok = sb("r2tok", 4)
        MTall = sb("MTall", 8 * 384, BF16)
        CsTall = sb("CsTall", 8 * 384, BF16)
        wcol = sb("wcol", 24)
        neg3 = sb("neg3", 384)
        dtw2 = sb("dtw2", 6)
        dcol = sb("dcol", 24)
        dummy = sb("dummy_act", 2)
        yT = sb("yT", 3 * T)
        yT2 = yT
        ident = cst[:, 0:128]
        Umask = cst[:, 128:256]
        Lstr = cst[:, 256:384]

        def hch(j, n=T, o=0):
            return hT[:, j * T + o:j * T + o + n]
        MT = [hT[:, 12 * T + g * 384:12 * T + (g + 1) * 384] for g in range(2)]
        CsT = [hT[:, 14 * T + g * 384:14 * T + (g + 1) * 384] for g in range(2)]
        xdt = hT[:, 16 * T:16 * T + 384]
        xdtw = hT[:, 17 * T:17 * T + 384]
        Btok = hT[:, 18 * T:18 * T + 256]
        Hbf = [hT[:, 19 * T:19 * T + 384], hT[:, 20 * T:20 * T + 384]]
        vtok = [hT[:, 21 * T:21 * T + 256], hT[:, 18 * T + 256:18 * T + 512]]
        tvtok = ["h21", "h18"]
        tMT = [["h12"], ["h12", "h13"]]
        tCsT = [["h14"], ["h14", "h15"]]

        NPB = 5
        pbank = [psum("pb%d" % i, 512) for i in range(NPB)]
        ps_st = psum("ps_st", 512)
        pbf = [psum("pbf%d" % i, 1024, BF16) for i in range(2)]
        st = {"pb": 0, "pbf": 0, "ws": 0, "sq": 0, "sg": 0, "raw": 0, "cacc": 0, "pool": "pool", "first": True}

        def nb():
            i = st["pb"] % NPB
            st["pb"] += 1
            return pbank[i], "pb%d" % i

        def nbf():
            i = st["pbf"] % 2
            st["pbf"] += 1
            return pbf[i], "pbf%d" % i

        def mm(out, lhsT, rhs, start, stop, reads, writes):
            P.add("pe", lambda e: e.matmul(out, lhsT=lhsT, rhs=rhs, start=start, stop=stop), reads, writes)

        def tr(out, in_, idn, reads, writes):
            P.add("pe", lambda e: e.transpose(out, in_, idn), reads, writes)

        def act(out, in_, func, reads, writes, bias=None, scale=None):
            kw = {}
            if bias is not None:
                kw["bias"] = bias
            if scale is not None:
                kw["scale"] = scale
            P.add("act", lambda e: e.activation(out=out, in_=in_, func=func, **kw), reads, writes)

        def tt(eng, out, in0, in1, op, reads, writes):
            P.add(eng, lambda e: e.tensor_tensor(out=out, in0=in0, in1=in1, op=op), reads, writes)

        def stt(out, in0, scalar, in1, op0, op1, reads, writes):
            P.add("dve", lambda e: e.scalar_tensor_tensor(out=out, in0=in0, scalar=scalar, in1=in1, op0=op0, op1=op1), reads, writes)

        def ts(out, in0, s1, s2, op0, op1, reads, writes):
            if s2 is None:
                P.add("dve", lambda e: e.tensor_scalar(out=out, in0=in0, scalar1=s1, scalar2=None, op0=op0), reads, writes)
            else:
                P.add("dve", lambda e: e.tensor_scalar(out=out, in0=in0, scalar1=s1, scalar2=s2, op0=op0, op1=op1), reads, writes)

        def cp(eng, out, in_, reads, writes):
            if eng == "act":
                act(out, in_, AF.Copy, reads, writes)
            else:
                P.add(eng, lambda e: e.tensor_copy(out=out, in_=in_), reads, writes)

        def xc(buf, c, n=T):
            return buf[:, c * n:(c + 1) * n]

        def pvc(l, name, c):
            o = pv_off(l, name) + c
            return pv[:, o:o + 1]

        P.add("sp", lambda e: e.dma_start(out=cst[:], in_=cst_d), writes=["cst"], dma_key="c_cst")
        P.add("sp", lambda e: e.dma_start(out=pv[:], in_=pv_d), writes=["pv"], dma_key="c_pv")
        P.add("sp", lambda e: e.dma_start(out=hb[:], in_=hb_d), writes=["hb"], dma_key="c_hb")
        P.add("pool", lambda e: e.dma_start(out=wbd[:], in_=wbd_d), writes=["wbd"], dma_key="c_wbd")
        P.add("pool", lambda e: e.dma_start(out=bs[:], in_=bs_d), writes=["bs"], dma_key="c_bs")
        P.add("pool", lambda e: e.dma_start(out=identb[:], in_=cst_d[:, 0:128]), writes=["identb"], dma_key="c_idb")
        P.add("dve", lambda e: e.memset(onesb[:], 1.0), writes=["onesb"])
        P.add("dve", lambda e: e.memset(onesf[:], 1.0), writes=["onesf"])
        P.add("dve", lambda e: e.memset(dummy[:], 1.0), writes=["dummy0", "dummy1"])
        P.add("dve", lambda e: e.memset(halo[:], 0.0), writes=["halo%d" % ci for ci in range(10)])
        P.add("dve", lambda e: e.memset(hstate[:], 0.0), writes=["hst0", "hst1", "hst2"])
        P.add("dve", lambda e: e.memset(H32[:], 0.0), writes=["H32"])
        for j in range(3):
            ts(neg3[:, j * 128:(j + 1) * 128], Umask, -1.0, 30000.0, ALU.add, ALU.mult, ["cst"], ["neg3"])
        for l in range(L):
            for nm in ("ffn1_post_g", "ffn2_post_g"):
                o = pv_off(l, nm)
                ts(pv[:, o:o + 8], pv[:, o:o + 8], 0.5, None, ALU.mult, None, ["pv"], ["pv"])
            o = pv_off(l, "lru_lambda")
            act(pv[:, o:o + 3], pv[:, o:o + 3], AF.Exp, ["pv"], ["pv"], scale=-1.0)
            act(pv[:, o:o + 3], pv[:, o:o + 3], AF.Ln, ["pv"], ["pv"], bias=1.0)
            ts(pv[:, o:o + 3], pv[:, o:o + 3], -8.0, None, ALU.mult, None, ["pv"], ["pv"])
            o = l * 48 + 24
            act(hb[:, o:o + 24], hb[:, o:o + 24], AF.Exp, ["hb"], ["hb"])
            ts(hb[:, o:o + 24], hb[:, o:o + 24], -1.0, None, ALU.mult, None, ["hb"], ["hb"])
            for g in range(4):
                o = (l * 4 + g) * 128
                P.add("sp", lambda e, o=o, g=g: e.dma_start(out=wst32[:, g * 128:(g + 1) * 128], in_=wst_d[:, o:o + 128]),
                      writes=["wst32_%d" % g], dma_key="c_wst%d" % g)
                tt("dve", wst[:, o:o + 128], wst32[:, g * 128:(g + 1) * 128], Umask, ALU.mult,
                   ["wst32_%d" % g, "cst"], ["wst"])

        def piece_parts(p_):
            l, r_ = divmod(p_, PPL)
            if r_ < 11 or 26 <= r_ < 37:
                f, jp = (0, r_) if r_ < 11 else (1, r_ - 26)
                wg = wgu_d[f][l].rearrange("(k p) c -> p k c", p=128)
                return [(s * 2048, 8, 256, wg[:, :, s * DFF + jp * 256:s * DFF + (jp + 1) * 256]) for s in range(2)]
            if 11 <= r_ < 19 or r_ >= 37:
                f, dp = (0, r_ - 11) if r_ < 19 else (1, r_ - 37)
                wd = wdn_d[f][l].rearrange("(k p) c -> p k c", p=128)
                return [(0, NFC, 128, wd[:, :, dp * 128:(dp + 1) * 128])]
            if 19 <= r_ < 24:
                c0, c1 = IN_PIECES[r_ - 19]
                wi = win_d[l].rearrange("(k p) c -> p k c", p=128)
                return [(0, 8, c1 - c0, wi[:, :, c0:c1])]
            op_ = r_ - 24
            wo = wout_d[l].rearrange("(k p) c -> p k c", p=128)
            return [(0, 8, 512, wo[:, :, op_ * 512:(op_ + 1) * 512])]

        def wload(p_, n):
            i = st["ws"] % NSLOT
            st["ws"] += 1
            if st["first"]:
                for (off, k, c, src) in piece_parts(p_):
                    dstv = wsl[i][:, off:off + k * c].rearrange("p (k c) -> p k c", k=k)
                    P.add("pool", lambda e, dstv=dstv, src=src: e.dma_start(out=dstv, in_=src),
                          writes=["wsl%d" % i], dma_key="wp%d" % i)
                P.add("sp", lambda e: e.dma_start(out=wsc[p_][:, 0:n], in_=wsl[i][:, 0:n]),
                      reads=["wsl%d" % i], writes=["wsc%d" % p_], dma_key="wb%d" % i)
            else:
                P.add("sp", lambda e: e.dma_start(out=wsl[i][:, 0:n], in_=wsc[p_][:, 0:n]),
                      reads=["wsc%d" % p_], writes=["wsl%d" % i], dma_key="w%d" % i)
            return wsl[i], "wsl%d" % i

        def stats_of(src_ap, src_tok, c, n_c):
            i = st["sq"] % 2
            st["sq"] += 1
            act(sq[i][:], src_ap, AF.Square, [src_tok], ["sq%d" % i])
            return lambda: mm(ps_st[:], onesb[:], sq[i][:], c == 0, c == n_c - 1, ["onesb", "sq%d" % i], ["ps_st"])

        def preload(func):
            kw = {"bias": 1.0} if func == AF.Ln else {}
            P.add("act", lambda e: e.activation(out=dummy[:, 1:2], in_=dummy[:, 0:1], func=func, **kw),
                  ["dummy0"], ["dummy1"])

        def finish_rstd(n):
            act(rstd[:], ps_st[:], AF.Ln, ["ps_st"], ["rstd"], bias=EPS, scale=1.0 / n)
            act(rstd[:], rstd[:], AF.Exp, ["rstd"], ["rstd"], scale=-0.5)

        def prenorm(l, gname):
            for c in range(8):
                stats_of(xc(xT, c), "xT%d" % c, c, 8)()
            finish_rstd(D)
            for c in range(8):
                stt(xc(xn, c), xc(xT, c), pvc(l, gname, c), rstd[:], ALU.mult, ALU.mult,
                    ["xT%d" % c, "pv", "rstd"], ["xn%d" % c])

        def prenorm_lazy(l, gname):
            for c in range(8):
                stats_of(xc(xT, c), "xT%d" % c, c, 8)()
                act(xc(xn, c), xc(xT, c), AF.Copy, ["xT%d" % c, "pv"], ["xn%d" % c], scale=pvc(l, gname, c))
            finish_rstd(D)

        def postnorm(l, gname):
            finish_rstd(D)
            for c in range(8):
                tt("dve", xc(fT, c), xc(fT, c), rstd[:], ALU.mult, ["fT%d" % c, "rstd"], ["fT%d" % c])
                stt(xc(xT, c), xc(fT, c), pvc(l, gname, c), xc(xT, c), ALU.mult, ALU.add,
                    ["fT%d" % c, "pv", "xT%d" % c], ["xT%d" % c])

        def out_group(ps, pst, dc, pend):
            act(xc(fT, dc), ps[:], AF.Copy, [pst], ["fT%d" % dc])
            pend.append(stats_of(ps[:], pst, dc, 8))

        def ffn(l, f):
            pre, post = ("ffn1_pre_g", "ffn1_post_g") if f == 0 else ("ffn2_pre_g", "ffn2_post_g")
            P.tag = 'ffn_pre'
            prenorm_lazy(l, pre)
            P.tag = 'ffn_up'
            preload(AF.Silu)
            for jp in range(11):
                w, wt = wload(pid(l, "gu%d" % f, jp), 4096)
                banks = [[nb(), nb()] for fi in range(2)]
                if jp == 0:
                    for kc in range(8):
                        for fi in range(2):
                            for s in range(2):
                                o = s * 2048 + kc * 256 + fi * 128
                                ps, pst = banks[fi][s]
                                mm(ps[:], w[:, o:o + 128], xc(xn, kc), kc == 0, kc == 7, [wt, "xn%d" % kc], [pst])
                else:
                    for fi in range(2):
                        for s in range(2):
                            ps, pst = banks[fi][s]
                            for kc in range(8):
                                o = s * 2048 + kc * 256 + fi * 128
                                mm(ps[:], w[:, o:o + 128], xc(xn, kc), kc == 0, kc == 7, [wt, "xn%d" % kc], [pst])
                for fi in range(2):
                    j = jp * 2 + fi
                    (pg, pgt), (pu, put) = banks[fi]
                    i = st["sg"] % 2
                    st["sg"] += 1
                    tt("dve", sg[i][:], pg[:], rstd[:], ALU.mult, [pgt, "rstd"], ["sg%d" % i])
                    act(sg[i][:], sg[i][:], AF.Silu, ["sg%d" % i], ["sg%d" % i])
                    tt("dve", cacc[i][:], pu[:], rstd[:], ALU.mult, [put, "rstd"], ["cacc%d" % i])
                    tt("dve", hch(j), sg[i][:], cacc[i][:], ALU.mult, ["sg%d" % i, "cacc%d" % i], ["h%d" % j])
            preload(AF.Ln)
            P.tag = 'ffn_down'
            pend = []
            for dc in range(8):
                w, wt = wload(pid(l, "dn%d" % f, dc), NFC * 128)
                pf, pft = nb()
                for j in range(NFC):
                    o = j * 128
                    mm(pf[:], w[:, o:o + 128], hch(j), j == 0, j == NFC - 1, [wt, "h%d" % j], [pft])
                for fn in pend:
                    fn()
                pend = []
                out_group(pf, pft, dc, pend)
            for fn in pend:
                fn()
            P.tag = 'ffn_post'
            postnorm(l, post)

        NRAW = 3

        def evac_scaled(ps, pst):
            i = st["raw"] % NRAW
            st["raw"] += 1
            pl = st["pend2"]
            while any(bi == i for bi, _ in pl):
                pl.pop(0)[1]()
            tt("dve", raw[i][:, 3:T + 3], ps[:], rstd[:], ALU.mult, [pst, "rstd"], ["raw%d" % i])
            return i

        def conv_s1(l, i, ci, wname, bname, wc, out_ap, out_tok):
            r, rt, rh = raw[i], "raw%d" % i, "rawh%d" % i
            ho = (l * 10 + ci) * 3
            cp("act", r[:, 0:3], halo[:, ho:ho + 3], ["halo%d" % ci], [rh, rt])
            act(out_ap, r[:, 3:T + 3], AF.Identity, [rt, "pv"], [out_tok], bias=pvc(l, bname, wc), scale=pvc(l, wname % 3, wc))

            def s2():
                for k in (2, 1, 0):
                    stt(out_ap, r[:, k:T + k], pvc(l, wname % k, wc), out_ap, ALU.mult, ALU.add,
                        [rt, rh, "pv", out_tok], [out_tok])
                cp("dve", halo[:, ho:ho + 3], r[:, T:T + 3], [rt], ["halo%d" % ci])
            return s2

        def mixer(l):
            P.tag = 'mix_pre'
            prenorm_lazy(l, "mix_pre_g")
            pr2, pr2t = nb()
            for q in range(4):
                tr(pr2[:, q * 128:(q + 1) * 128], rstd[:, q * 128:(q + 1) * 128], ident, ["rstd", "cst"], [pr2t])
            P.add("dve", lambda e: e.tensor_copy(out=r2tok[:].unsqueeze(2),
                                                 in_=pr2[:].rearrange("p (q c) -> p q c", q=4)[:, :, 0:1]),
                  [pr2t], ["r2tok"])
            P.tag = 'mix_in'
            pend2 = []
            st["pend2"] = pend2

            def flush(keep):
                while len(pend2) > keep:
                    pend2.pop(0)[1]()
            def sgu_part1():
                p1, p1t = nb()
                p2, p2t = nb()
                for c in range(2):
                    mm(p1[:], onesf[:], xc(v_g, c), c == 0, c == 1, ["onesf", "v%d" % c], [p1t])
                for c in range(2):
                    tt("dve", tmp[c][:], xc(v_g, c), xc(v_g, c), ALU.mult, ["v%d" % c], ["tmp%d" % c])
                    mm(p2[:], onesf[:], tmp[c][:], c == 0, c == 1, ["onesf", "tmp%d" % c], [p2t])
                mean, var = tmp[2], tmp[3]
                act(mean[:], p1[:], AF.Copy, [p1t], ["tmp2"], scale=1.0 / 256)
                tt("dve", var[:], mean[:], mean[:], ALU.mult, ["tmp2"], ["tmp3"])
                stt(var[:], p2[:], 1.0 / 256, var[:], ALU.mult, ALU.subtract, [p2t, "tmp3"], ["tmp3"])
                act(var[:], var[:], AF.Ln, ["tmp3"], ["tmp3"], bias=EPS)
                act(var[:], var[:], AF.Exp, ["tmp3"], ["tmp3"], scale=-0.5)
                act(dt_a[:], dt_a[:], AF.Exp, ["dt_a"], ["dt_a"])
                act(dt_a[:], dt_a[:], AF.Ln, ["dt_a"], ["dt_a"], bias=1.0)
                tt("dve", ad_t[:], dt_a[:], hb[:, l * 48 + 24:l * 48 + 48], ALU.mult, ["dt_a", "hb"], ["ad_t"])
                if dbg_d is not None and l == 0 and st["first"]:
                    P.add("sp", lambda e: e.dma_start(out=dbg_d[:, 0:24], in_=dt_a[:]), reads=["dt_a"], dma_key="dbg")
                    P.add("sp", lambda e: e.dma_start(out=dbg_d[:, 24:28], in_=r2tok[:]), reads=["r2tok"], dma_key="dbg")
                    P.add("sp", lambda e: e.dma_start(out=dbg_d[:, 28:60], in_=rstd[:, 0:32]), reads=["rstd"], dma_key="dbg")
                for c in range(2):
                    tt("dve", tmp[c][:], xc(v_g, c), mean[:], ALU.subtract, ["v%d" % c, "tmp2"], ["tmp%d" % c])
                    tt("dve", tmp[c][:], tmp[c][:], var[:], ALU.mult, ["tmp%d" % c, "tmp3"], ["tmp%d" % c])
                    ts(hch(10 + c), tmp[c][:], pvc(l, "sgu_ln_g", c), pvc(l, "sgu_ln_b", c), ALU.mult, ALU.add,
                       ["tmp%d" % c, "pv"], ["h%d" % (10 + c)])

            def sgu_part2():
                pm = [nb(), nb()]
                pend = []
                for q in range(4):
                    pt, ptt = nbf()
                    for c in range(2):
                        tr(pt[:, c * 128:(c + 1) * 128], hT[:, (10 + c) * T + q * 128:(10 + c) * T + (q + 1) * 128],
                           identb[:], ["h%d" % (10 + c), "identb"], [ptt])
                    vt = vtok[q % 2]
                    cp("act", vt, pt[:, 0:256], [ptt], [tvtok[q % 2]])
                    for fn in pend:
                        fn()
                    pend = []

                    def mixq(q=q, vt=vt):
                        for g in range(4):
                            c, po = g // 2, (g % 2) * 64
                            o = (l * 4 + g) * 128
                            dst = pm[c][0][po:po + 64, q * 128:(q + 1) * 128]
                            mm(dst, vt[:, g * 64:(g + 1) * 64], wst[:, o:o + 128], True, False,
                               [tvtok[q % 2], "wst"], [pm[c][1]])
                            mm(dst, onesb[0:1, 0:64], bs[0:1, o:o + 128], False, True, ["onesb", "bs"], [pm[c][1]])
                    pend.append(mixq)
                for fn in pend:
                    fn()
                for c in range(2):
                    tt("dve", tmp[c][:], xc(u_g, c), pm[c][0][:], ALU.mult, ["u%d" % c, pm[c][1]], ["tmp%d" % c])

            for ip in (4, 3, 2, 1, 0):
                if ip == 2:
                    P.tag = 'sgu'
                    sgu_part2()
                    P.tag = 'mix_in'
                if ip == 3 and TOG_SGU:
                    flush(0)
                    sgu_part1()
                c0, c1 = IN_PIECES[ip]
                n = c1 - c0
                w, wt = wload(pid(l, "in", ip), 8 * n)
                for (kind, c, col0, width) in IN_CHUNKS:
                    if not (c0 <= col0 < c1):
                        continue
                    lo = col0 - c0
                    if kind == "dt":
                        pdt, pdtt = nb()
                        for q in range(4):
                            for kc in range(8):
                                mm(pdt[:, q * 6:(q + 1) * 6], xn[:, kc * T + q * 128:kc * T + (q + 1) * 128],
                                   w[:, kc * n + lo:kc * n + lo + 6], kc == 0, kc == 7, [wt, "xn%d" % kc], [pdtt])
                        P.add("dve", lambda e, pdt=pdt: e.tensor_tensor(
                            out=dt_a[:].rearrange("p (q h) -> p q h", q=4),
                            in0=pdt[:, 0:24].rearrange("p (q h) -> p q h", q=4),
                            in1=r2tok[:].unsqueeze(2).to_broadcast([128, 4, 6]), op=ALU.mult),
                            [pdtt, "r2tok"], ["dt_a"])
                        tt("dve", dt_a[:], dt_a[:], hb[:, l * 48:l * 48 + 24], ALU.add, ["dt_a", "hb"], ["dt_a"])
                        continue
                    ps, pst = nb()
                    for kc in range(8):
                        mm(ps[:], w[:, kc * n + lo:kc * n + lo + 128], xc(xn, kc), kc == 0, kc == 7,
                           [wt, "xn%d" % kc], [pst])
                    ri = evac_scaled(ps, pst)
                    rsrc, rtk = raw[ri][:, 3:T + 3], "raw%d" % ri
                    if kind == "gate":
                        act(xc(gate_g, c), rsrc, AF.Gelu, [rtk], ["gate%d" % c])
                    elif kind == "z":
                        act(xc(zs, c), rsrc, AF.Silu, [rtk], ["zs%d" % c])
                    elif kind == "u":
                        act(xc(u_g, c), rsrc, AF.Gelu, [rtk], ["u%d" % c])
                    elif kind == "v":
                        act(xc(v_g, c), rsrc, AF.Gelu, [rtk], ["v%d" % c])
                    elif kind == "rec":
                        s2 = conv_s1(l, ri, c, "lru_cw%d", "lru_conv_b", c, xc(rec32, c), "rec%d" % c)

                        def s2rec(s2=s2, c=c):
                            s2()
                            cp("act", hch(c), xc(rec32, c), ["rec%d" % c], ["h%d" % c])
                        flush(1)
                        pend2.append((ri, s2rec))
                        continue
                    elif kind == "xbc":
                        i = st["cacc"] % NRAW
                        st["cacc"] += 1
                        s2 = conv_s1(l, ri, 3 + c, "ssd_cw%d", "ssd_conv_b", c, cacc[i][:], "cacc%d" % i)

                        def s2x(s2=s2, c=c, i=i):
                            s2()
                            act(hch(3 + c), cacc[i][:], AF.Silu, ["cacc%d" % i], ["h%d" % (3 + c)])
                        flush(1)
                        pend2.append((ri, s2x))
                        continue
                    flush(1)
            flush(0)
            if not TOG_SGU:
                sgu_part1()

            P.tag = 'sgu'
            for c in range(2):
                cp("act", xc(xn, 6 + c), tmp[c][:], ["tmp%d" % c], ["xn%d" % (6 + c)])

            lru = []
            A_ = [tmp[0], tmp[1], tmp[2]]
            tA = ["tmp0", "tmp1", "tmp2"]
            B_ = [cacc[0][:], cacc[1][:], cacc[2][:]]
            tB = ["cacc0", "cacc1", "cacc2"]
            C_ = [tmp[3][:], tmp[4][:], rstd[:]]
            tC = ["tmp3", "tmp4", "rstd"]

            def L(fn):
                lru.append(fn)
            for c in range(3):
                def gates(c=c):
                    pr, prt = nb()
                    pi_, pit = nb()
                    o = ((l * 2 + 0) * 3 + c) * 128
                    mm(pr[:], wbd[:, o:o + 128], hch(c), True, True, ["wbd", "h%d" % c], [prt])
                    o = ((l * 2 + 1) * 3 + c) * 128
                    mm(pi_[:], wbd[:, o:o + 128], hch(c), True, True, ["wbd", "h%d" % c], [pit])
                    act(A_[c][:], pr[:], AF.Sigmoid, [prt, "pv"], [tA[c]], bias=pvc(l, "lru_b_r", c))
                    act(B_[c], pi_[:], AF.Sigmoid, [pit, "pv"], [tB[c]], bias=pvc(l, "lru_b_i", c))
                L(gates)
            for c in range(3):
                L(lambda c=c: act(A_[c][:], A_[c][:], AF.Exp, [tA[c], "pv"], [tA[c]], scale=pvc(l, "lru_lambda", c)))
                L(lambda c=c: tt("dve", C_[c], A_[c][:], A_[c][:], ALU.mult, [tA[c]], [tC[c]]))
                L(lambda c=c: tt("dve", B_[c], B_[c], xc(rec32, c), ALU.mult, [tB[c], "rec%d" % c], [tB[c]]))
            for c in range(3):
                L(lambda c=c: act(C_[c], C_[c], AF.Ln, [tC[c]], [tC[c]], bias=1.0, scale=-1.0))
            for c in range(3):
                L(lambda c=c: act(C_[c], C_[c], AF.Exp, [tC[c]], [tC[c]], scale=0.5))
                L(lambda c=c: tt("dve", B_[c], B_[c], C_[c], ALU.mult, [tB[c], tC[c]], [tB[c]]))
            for c in range(3):
                def scan(c=c):
                    hs = hstate[:, l * 3 + c:l * 3 + c + 1]
                    P.add("dve", lambda e: e.tensor_tensor_scan(
                        out=C_[c], data0=A_[c][:], data1=B_[c], initial=hs, op0=ALU.mult, op1=ALU.add),
                        [tA[c], tB[c], "hst%d" % c], [tC[c]])
                    cp(st["pool"], hs, C_[c][:, T - 1:T], [tC[c]], ["hst%d" % c])
                    tt("dve", xc(xn, c), C_[c], xc(gate_g, c), ALU.mult, [tC[c], "gate%d" % c], ["xn%d" % c])
                L(scan)

            def lru_some(k):
                for _ in range(k):
                    if lru:
                        lru.pop(0)()

            P.tag = 'ssd'
            cp("act", Hbf[0], H32[:, l * 384:(l + 1) * 384], ["H32"], ["h19"])
            pe2 = st["pool"]

            def mk_rhsU(k):
                kb, a0 = k % 2, (k // 2) * 6 + (k % 2) * 3
                P.add("dve", lambda e: e.tensor_tensor(
                    out=rhsU[kb][:].rearrange("p (j l) -> p j l", j=3),
                    in0=Umask.unsqueeze(1).to_broadcast([128, 3, 128]),
                    in1=ad_t[:, a0:a0 + 3].unsqueeze(2).to_broadcast([128, 3, 128]), op=ALU.mult),
                    ["cst", "ad_t"], ["rhsU%d" % kb])
            mk_rhsU(0)
            for q in range(4):
                for g in range(2):
                    k = q * 2 + g
                    kb = k % 2
                    a0 = q * 6 + g * 3
                    v3 = lambda ap: ap.rearrange("p (j l) -> p j l", j=3)
                    pD, pDt = nb()
                    mm(pD[:, 0:384], Lstr, rhsU[kb][:], True, False, ["cst", "rhsU%d" % kb], [pDt])
                    mm(pD[:, 0:384], ident, neg3[:], False, True, ["cst", "neg3"], [pDt])
                    pC, pCt = nb()
                    mm(pC[:, 0:384], onesf[:], rhsU[kb][:], True, True, ["onesf", "rhsU%d" % kb], [pCt])
                    pcb, pcbt = nb()
                    mm(pcb[:, 0:128], hT[:, (6 + g) * T + q * 128:(6 + g) * T + (q + 1) * 128],
                       hT[:, (8 + g) * T + q * 128:(8 + g) * T + (q + 1) * 128], True, True,
                       ["h%d" % (6 + g), "h%d" % (8 + g)], [pcbt])
                    if k < 7:
                        mk_rhsU(k + 1)
                    act(Eb[kb][:], pD[:, 0:384], AF.Exp, [pDt], ["E%d" % kb])
                    act(E2b[kb][:], pC[:, 0:384], AF.Exp, [pCt], ["E2%d" % kb])
                    P.add("dve", lambda e, kb=kb, k=k: e.tensor_copy(
                        out=wcol[:, k * 3:(k + 1) * 3].unsqueeze(2),
                        in_=Eb[kb][:].rearrange("p (j l) -> p j l", j=3)[:, :, 127:128]),
                        ["E%d" % kb], ["wcol"])
                    P.add("dve", lambda e, kb=kb, k=k: e.tensor_copy(
                        out=dcol[:, k * 3:(k + 1) * 3].unsqueeze(2),
                        in_=E2b[kb][:].rearrange("p (j l) -> p j l", j=3)[:, :, 127:128]),
                        ["E2%d" % kb], ["dcol"])
                    P.add("dve", lambda e, kb=kb, k=k, pcb=pcb: e.tensor_tensor(
                        out=MTall[:, k * 384:(k + 1) * 384].rearrange("p (j l) -> p j l", j=3),
                        in0=Eb[kb][:].rearrange("p (j l) -> p j l", j=3),
                        in1=pcb[:, 0:128].unsqueeze(1).to_broadcast([128, 3, 128]), op=ALU.mult),
                        ["E%d" % kb, pcbt], ["MT%d" % k])
                    P.add(pe2, lambda e, kb=kb, k=k, g=g, q=q: e.tensor_tensor(
                        out=CsTall[:, k * 384:(k + 1) * 384].rearrange("p (j l) -> p j l", j=3),
                        in0=E2b[kb][:].rearrange("p (j l) -> p j l", j=3),
                        in1=hT[:, (8 + g) * T + q * 128:(8 + g) * T + (q + 1) * 128].unsqueeze(1).to_broadcast([128, 3, 128]),
                        op=ALU.mult),
                        ["E2%d" % kb, "h%d" % (8 + g)], ["CsT%d" % k])
                if q % 2 == 1:
                    lru_some(3)
            xdtB = [xdt, hT[:, 12 * T:12 * T + 384]]
            xdtwB = [xdtw, hT[:, 13 * T:13 * T + 384]]
            BtokB = [Btok, hT[:, 14 * T:14 * T + 256]]
            dtwB = [dtw, dtw2]
            txdt, txdtw, tbtok, tdtw = ["h16", "h12"], ["h17", "h13"], ["h18", "h14"], ["dtw", "dtw2"]

            def prep(q):
                pb_ = q % 2
                pt, ptt = nbf()
                for c in range(3):
                    tr(pt[:, c * 128:(c + 1) * 128], hT[:, (3 + c) * T + q * 128:(3 + c) * T + (q + 1) * 128],
                       identb[:], ["h%d" % (3 + c), "identb"], [ptt])
                for g in range(2):
                    tr(pt[:, 384 + g * 128:384 + (g + 1) * 128],
                       hT[:, (6 + g) * T + q * 128:(6 + g) * T + (q + 1) * 128], identb[:],
                       ["h%d" % (6 + g), "identb"], [ptt])
                P.add("dve", lambda e: e.tensor_tensor(
                    out=xdtB[pb_].rearrange("p (h d) -> p h d", h=6),
                    in0=pt[:, 0:384].rearrange("p (h d) -> p h d", h=6),
                    in1=dt_a[:, q * 6:(q + 1) * 6].unsqueeze(2).to_broadcast([128, 6, 64]), op=ALU.mult),
                    [ptt, "dt_a"], [txdt[pb_]])
                tt("dve", dtwB[pb_][:], dt_a[:, q * 6:(q + 1) * 6], wcol[:, q * 6:(q + 1) * 6], ALU.mult,
                   ["dt_a", "wcol"], [tdtw[pb_]])
                P.add("dve", lambda e: e.tensor_tensor(
                    out=xdtwB[pb_].rearrange("p (h d) -> p h d", h=6),
                    in0=pt[:, 0:384].rearrange("p (h d) -> p h d", h=6),
                    in1=dtwB[pb_][:].unsqueeze(2).to_broadcast([128, 6, 64]), op=ALU.mult),
                    [ptt, tdtw[pb_]], [txdtw[pb_]])
                cp("dve", BtokB[pb_], pt[:, 384:640], [ptt], [tbtok[pb_]])
            prep(0)
            for q in range(4):
                par = q % 2
                if q < 3:
                    prep(q + 1)
                xdt_, xdtw_, Btok_ = xdtB[par], xdtwB[par], BtokB[par]
                py, pyt = nb()
                for h in range(6):
                    g, j, cc, po = h // 3, h % 3, h // 2, (h % 2) * 64
                    k = q * 2 + g
                    dst = py[po:po + 64, cc * 128:(cc + 1) * 128]
                    mm(dst, xdt_[:, h * 64:(h + 1) * 64], MTall[:, k * 384 + j * 128:k * 384 + (j + 1) * 128], True, False,
                       [txdt[par], "MT%d" % k], [pyt])
                    mm(dst, Hbf[par][:, h * 64:(h + 1) * 64], CsTall[:, k * 384 + j * 128:k * 384 + (j + 1) * 128],
                       False, True, ["h%d" % (19 + par), "CsT%d" % k], [pyt])
                for cc in range(3):
                    stt(yT2[:, cc * T + q * 128:cc * T + (q + 1) * 128],
                        hT[:, (3 + cc) * T + q * 128:(3 + cc) * T + (q + 1) * 128], pvc(l, "ssd_dvec", cc),
                        py[:, cc * 128:(cc + 1) * 128], ALU.mult, ALU.add,
                        ["h%d" % (3 + cc), "pv", pyt], ["yS%d" % cc])
                pS, pSt = nb()
                for g in range(2):
                    mm(pS[:, g * 192:(g + 1) * 192], Btok_[:, g * 128:(g + 1) * 128], xdtw_[:, g * 192:(g + 1) * 192],
                       True, True, [tbtok[par], txdtw[par]], [pSt])
                P.add("dve", lambda e, q=q: e.tensor_tensor(
                    out=H32[:, l * 384:(l + 1) * 384].rearrange("p (h d) -> p h d", h=6),
                    in0=H32[:, l * 384:(l + 1) * 384].rearrange("p (h d) -> p h d", h=6),
                    in1=dcol[:, q * 6:(q + 1) * 6].unsqueeze(2).to_broadcast([128, 6, 64]),
                    op=ALU.mult), ["H32", "dcol"], ["H32"])
                tt("dve", H32[:, l * 384:(l + 1) * 384], H32[:, l * 384:(l + 1) * 384], pS[:, 0:384], ALU.add,
                   ["H32", pSt], ["H32"])
                if q < 3:
                    cp("dve", Hbf[1 - par], H32[:, l * 384:(l + 1) * 384], ["H32"], ["h%d" % (19 + 1 - par)])
                lru_some(5)
            lru_some(100)
            P.tag = 'mix_out'
            pend = []
            korder = (6, 7, 0, 1, 2, 3, 4, 5)
            w0, wt0 = wload(pid(l, "out", 0), 4096)
            bks = [nb() for di in range(4)]
            for ki, kc in enumerate(korder[:5]):
                for di in range(4):
                    o = kc * 512 + di * 128
                    mm(bks[di][0][:], w0[:, o:o + 128], xc(xn, kc), ki == 0, False, [wt0, "xn%d" % kc], [bks[di][1]])
            P.tag = 'ssd_norm'
            for cc in range(3):
                tt("dve", xc(yT2, cc), xc(yT2, cc), xc(zs, cc), ALU.mult, ["yS%d" % cc, "zs%d" % cc], ["yS%d" % cc])
                stats_of(xc(yT2, cc), "yS%d" % cc, cc, 3)()
            finish_rstd(384)
            for cc in range(3):
                stt(xc(xn, 3 + cc), xc(yT2, cc), pvc(l, "ssd_norm_g", cc), rstd[:], ALU.mult, ALU.mult,
                    ["yS%d" % cc, "pv", "rstd"], ["xn%d" % (3 + cc)])
            P.tag = 'mix_out'
            for ki, kc in enumerate(korder[5:]):
                for di in range(4):
                    o = kc * 512 + di * 128
                    mm(bks[di][0][:], w0[:, o:o + 128], xc(xn, kc), False, ki == 2, [wt0, "xn%d" % kc], [bks[di][1]])
            for di in range(4):
                for fn in pend:
                    fn()
                del pend[:]
                out_group(bks[di][0], bks[di][1], di, pend)
            w, wt = wload(pid(l, "out", 1), 4096)
            for di in range(4):
                dc = 4 + di
                pf, pft = nb()
                for ki, kc in enumerate(korder):
                    o = kc * 512 + di * 128
                    mm(pf[:], w[:, o:o + 128], xc(xn, kc), ki == 0, ki == 7, [wt, "xn%d" % kc], [pft])
                for fn in pend:
                    fn()
                pend = []
                out_group(pf, pft, dc, pend)
            for fn in pend:
                fn()
            P.tag = 'mix_post'
            postnorm(l, "mix_post_g")

        fall = ["fT%d" % c for c in range(8)]
        for ti in range(n_tiles):
            P.tag = 'io_in'
            src = x_d[ti * T:(ti + 1) * T, :].rearrange("(q p) d -> p q d", p=128)
            P.add("sp", lambda e, src=src: e.dma_start(out=fT[:].rearrange("p (q d) -> p q d", q=4), in_=src),
                  writes=fall, dma_key="xin")
            for c in range(8):
                ps, pst = nb()
                for q in range(4):
                    tr(ps[:, q * 128:(q + 1) * 128], fT[:, q * D + c * 128:q * D + (c + 1) * 128], ident,
                       fall + ["cst"], [pst])
                cp("dve" if c % 2 else "act", xc(xT, c), ps[:], [pst], ["xT%d" % c])
            st["pool"] = "dve" if ti == 0 else "pool"
            st["first"] = (ti == 0)
            for l in range(L):
                ffn(l, 0)
                mixer(l)
                ffn(l, 1)
            P.tag = 'io_out'
            for q in range(4):
                for hf in range(2):
                    ps, pst = nb()
                    for c4 in range(4):
                        c = hf * 4 + c4
                        tr(ps[:, c4 * 128:(c4 + 1) * 128], xT[:, c * T + q * 128:c * T + (q + 1) * 128], ident,
                           ["xT%d" % c, "cst"], [pst])
                    cp("dve" if hf else "act", fT[:, q * D + hf * 512:q * D + (hf + 1) * 512], ps[:], [pst], fall)
            dst = out_d[ti * T:(ti + 1) * T, :].rearrange("(q p) d -> p q d", p=128)
            P.add("sp", lambda e, dst=dst: e.dma_start(out=dst, in_=fT[:].rearrange("p (q d) -> p q d", q=4)),
                  reads=fall, dma_key="xout")

        stats = P.emit(final_dma_keys=["xout"])
    build.prog = P
    return nc, stats


N_CORES = 8
SEQ = 4096
DEPTH = 4


def kernel(**inputs):
    L = DEPTH
    x = np.asarray(inputs["x"], np.float32)
    B = x.shape[0]
    n_tiles = x.shape[1] // T
    nc, _ = build(n_tiles, L)
    shared = prep_params(inputs, L)
    for nm in ("ffn1_w_gu", "ffn2_w_gu", "ffn1_w_down", "ffn2_w_down", "mix_w_in", "mix_w_out"):
        shared[nm] = np.ascontiguousarray(np.asarray(inputs[nm], np.float32))
    in_maps = []
    for b in range(B):
        m = dict(shared)
        m["x"] = np.ascontiguousarray(x[b])
        in_maps.append(m)
    res = run_bass_kernel_spmd(nc, in_maps, core_ids=list(range(B)))
    return np.stack([np.asarray(r["out"], np.float32) for r in res.results], axis=0)
```
